# Optimizing a Trainium2 kernel written in Bass

```python
import jax, jax.numpy as jnp
from jax import lax
import numpy as np

D_MODEL = 1024
BATCH = 8
SEQ = 4096
DEPTH = 1

N_META = 16
ATTN_HEADS = 8
HEAD_DIM = D_MODEL // 16
D_ATTN = ATTN_HEADS * HEAD_DIM
CONV_GROUPS = 8
D_CONV = D_MODEL // 2
CONV_WIDTH = 3
D_MIX = D_ATTN + D_CONV
Q_BLOCK = 128
EPS = 1e-6
SPLIT_SIZES = (D_ATTN, D_ATTN, D_ATTN, ATTN_HEADS, D_ATTN, D_CONV, D_CONV, D_CONV, D_CONV)
D_IN = sum(SPLIT_SIZES)
SPLIT_POINTS = tuple(int(s) for s in np.cumsum(SPLIT_SIZES)[:-1])

kernel_name = "hymba_fox_shortconv_hybrid"


def _rmsnorm(x, g):
    xf = x.astype(jnp.float32)
    y = xf * lax.rsqrt(jnp.mean(xf * xf, axis=-1, keepdims=True) + EPS)
    return (y * g.astype(jnp.float32)).astype(x.dtype)


def _group_rmsnorm(y, g, n_groups):
    lead = y.shape[:-1]
    c = y.shape[-1]
    yf = y.astype(jnp.float32).reshape(lead + (n_groups, c // n_groups))
    yf = yf * lax.rsqrt(jnp.mean(yf * yf, axis=-1, keepdims=True) + EPS)
    return (yf.reshape(lead + (c,)) * g.astype(jnp.float32)).astype(y.dtype)


def _fox_attention(q, k, v, cum_logf):
    b, l, h, dh = q.shape
    scale = dh ** -0.5
    key_pos = jnp.arange(l)
    c_keys = jnp.transpose(cum_logf, (0, 2, 1))

    def block(args):
        q_blk, c_blk, t_blk = args
        s = jnp.einsum('bqhd,bkhd->bhqk', q_blk, k,
                       preferred_element_type=jnp.float32) * scale
        s = s + jnp.transpose(c_blk, (0, 2, 1))[..., :, None] - c_keys[:, :, None, :]
        s = jnp.where(key_pos[None, :] <= t_blk[:, None], s, -jnp.inf)
        p = jax.nn.softmax(s, axis=-1)
        return jnp.einsum('bhqk,bkhd->bqhd', p.astype(v.dtype), v)

    out_meta = block((q[:, :N_META], cum_logf[:, :N_META], key_pos[:N_META]))
    n_blk = (l - N_META) // Q_BLOCK
    q_r = jnp.transpose(q[:, N_META:].reshape(b, n_blk, Q_BLOCK, h, dh), (1, 0, 2, 3, 4))
    c_r = jnp.transpose(cum_logf[:, N_META:].reshape(b, n_blk, Q_BLOCK, h), (1, 0, 2, 3))
    t_r = key_pos[N_META:].reshape(n_blk, Q_BLOCK)
    out_r = lax.map(block, (q_r, c_r, t_r))
    out_r = jnp.transpose(out_r, (1, 0, 2, 3, 4)).reshape(b, l - N_META, h, dh)
    return jnp.concatenate([out_meta, out_r], axis=1)


def _causal_depthwise_conv(x, w):
    c = x.shape[-1]
    return lax.conv_general_dilated(
        x, w.reshape(CONV_WIDTH, 1, c).astype(x.dtype),
        window_strides=(1,), padding=[(CONV_WIDTH - 1, 0)],
        dimension_numbers=('NWC', 'WIO', 'NWC'), feature_group_count=c)


def _hybrid_layer(h, norm_g, w_in, b_f, conv_w, attn_norm_g, conv_norm_g, w_out):
    b, l, _ = h.shape
    u = _rmsnorm(h, norm_g)
    proj = jnp.einsum('bld,de->ble', u, w_in)
    q, k, v, f_logit, z_attn, gate_b, gate_c, xc, z_conv = jnp.split(proj, SPLIT_POINTS, axis=-1)

    log_f = jax.nn.log_sigmoid(f_logit.astype(jnp.float32) + b_f.astype(jnp.float32))
    cum_logf = jnp.cumsum(log_f, axis=1)
    shp = (b, l, ATTN_HEADS, HEAD_DIM)
    attn = _fox_attention(q.reshape(shp), k.reshape(shp), v.reshape(shp), cum_logf)
    y_attn = _group_rmsnorm(attn.reshape(b, l, D_ATTN), attn_norm_g, ATTN_HEADS) * jax.nn.silu(z_attn)

    conv = _causal_depthwise_conv(gate_c * xc, conv_w)
    y_conv = _group_rmsnorm(gate_b * conv, conv_norm_g, CONV_GROUPS) * jax.nn.silu(z_conv)

    mix = jnp.concatenate([y_attn, y_conv], axis=-1)
    return h + jnp.einsum('ble,ed->bld', mix, w_out)


def setup_inputs(seed: int = 0) -> dict:
    key = jax.random.key(seed)
    ks = jax.random.split(key, 10)
    f32 = jnp.float32
    x = jax.random.normal(ks[0], (BATCH, SEQ, D_MODEL), f32)
    meta = jax.random.normal(ks[1], (N_META, D_MODEL), f32)
    norm_g = 1.0 + 0.02 * jax.random.normal(ks[2], (DEPTH, D_MODEL), f32)
    w_in = jax.random.normal(ks[3], (DEPTH, D_MODEL, D_IN), f32) * D_MODEL ** -0.5
    b_f = jax.random.uniform(ks[4], (DEPTH, ATTN_HEADS), f32, minval=1.0, maxval=5.0)
    conv_w = jax.random.normal(ks[5], (DEPTH, CONV_WIDTH, D_CONV), f32) * CONV_WIDTH ** -0.5
    attn_norm_g = 1.0 + 0.02 * jax.random.normal(ks[6], (DEPTH, D_ATTN), f32)
    conv_norm_g = 1.0 + 0.02 * jax.random.normal(ks[7], (DEPTH, D_CONV), f32)
    w_out = jax.random.normal(ks[8], (DEPTH, D_MIX, D_MODEL), f32) * D_MIX ** -0.5
    final_norm_g = 1.0 + 0.02 * jax.random.normal(ks[9], (D_MODEL,), f32)
    return {"x": x, "meta": meta, "norm_g": norm_g, "w_in": w_in, "b_f": b_f,
            "conv_w": conv_w, "attn_norm_g": attn_norm_g, "conv_norm_g": conv_norm_g,
            "w_out": w_out, "final_norm_g": final_norm_g}


def reference(x, meta, norm_g, w_in, b_f, conv_w, attn_norm_g, conv_norm_g, w_out, final_norm_g):
    b = x.shape[0]
    meta_b = jnp.broadcast_to(meta.astype(x.dtype)[None], (b, N_META, x.shape[-1]))
    h = jnp.concatenate([meta_b, x], axis=1)
    for layer in range(DEPTH):
        h = _hybrid_layer(h, norm_g[layer], w_in[layer], b_f[layer], conv_w[layer],
                          attn_norm_g[layer], conv_norm_g[layer], w_out[layer])
    return _rmsnorm(h[:, N_META:], final_norm_g)
```

```python
import numpy as np
from contextlib import ExitStack
import concourse.bass as bass
import concourse.mybir as mybir
from concourse.bass_utils import run_bass_kernel_spmd

F32 = mybir.dt.float32
BF16 = mybir.dt.bfloat16
AF = mybir.ActivationFunctionType
ALU = mybir.AluOpType

D = 1024
KC = 8
DIN = 4104
EPS = 1e-6
NEG = -30000.0
C_ID, C_TRI, C_SEL, C_MASK, C_WE, C_WO, C_WG, C_COLS = 0, 128, 256, 384, 512, 640, 768, 896
NCST = 904


class Buf:
    __slots__ = ("name", "last_w", "readers")

    def __init__(self, name=""):
        self.name = name
        self.last_w = None
        self.readers = []


class Op:
    __slots__ = ("eng", "fn", "deps", "signal", "count", "dma_key", "sem", "final", "idx")

    def __init__(self, eng, fn, dma_key=None):
        self.eng = eng
        self.fn = fn
        self.deps = []
        self.signal = False
        self.count = None
        self.dma_key = dma_key
        self.sem = None
        self.final = False


class Sched:
    ENGS = ("pe", "act", "dve", "pool", "sp")

    def __init__(self):
        self.ops = []
        self.last = {}
        self.pending_barrier = {}

    def op(self, eng, fn, reads=(), writes=(), dma_key=None):
        o = Op(eng, fn, dma_key)
        deps = {}
        for b in reads:
            if b.last_w is not None:
                deps[id(b.last_w)] = b.last_w
        for b in writes:
            if b.last_w is not None:
                deps[id(b.last_w)] = b.last_w
            for r in b.readers:
                deps[id(r)] = r
        if eng in self.pending_barrier:
            for d in self.pending_barrier.pop(eng):
                deps[id(d)] = d
        best = {}
        for d in deps.values():
            k = ("dma", d.dma_key) if d.dma_key is not None else d.eng
            if k not in best or best[k].idx < d.idx:
                best[k] = d
        o.deps = list(best.values())
        o.idx = len(self.ops)
        for b in reads:
            b.readers.append(o)
        for b in writes:
            b.last_w = o
            b.readers = []
        self.ops.append(o)
        self.last[("dma", dma_key) if dma_key is not None else eng] = o
        return o

    def barrier(self):
        lasts = list(self.last.values())
        for e in self.ENGS:
            self.pending_barrier[e] = list(self.pending_barrier.get(e, [])) + lasts

    def emit(self, nc, es):
        ops = self.ops
        for o in ops:
            for d in o.deps:
                if d.dma_key is not None:
                    continue
                if d.eng == "pe" and o.eng == "pe" and o.dma_key is None:
                    continue
                d.signal = True
        eng_sem = {e: es.enter_context(nc.semaphore("s_" + e)) for e in ("pe", "act", "dve", "pool")}
        cnt = {e: 0 for e in eng_sem}
        dma_sems, dma_cnt = {}, {}
        for o in ops:
            if o.dma_key is not None:
                if o.dma_key not in dma_sems:
                    dma_sems[o.dma_key] = es.enter_context(nc.semaphore("d_" + str(o.dma_key)))
                    dma_cnt[o.dma_key] = 0
                dma_cnt[o.dma_key] += 16
                o.sem, o.count = dma_sems[o.dma_key], dma_cnt[o.dma_key]
            elif o.signal:
                cnt[o.eng] += 1
                o.sem, o.count = eng_sem[o.eng], cnt[o.eng]
        streams = {e: [o for o in ops if o.eng == e] for e in self.ENGS}
        final = [o for o in ops if o.final]

        def run(ename, eng):
            known = {}
            for o in streams[ename]:
                for d in o.deps:
                    if d.sem is None:
                        continue
                    if d.eng == "pe" and ename == "pe" and d.dma_key is None and o.dma_key is None:
                        continue
                    k = id(d.sem)
                    if known.get(k, 0) >= d.count:
                        continue
                    eng.wait_ge(d.sem, d.count)
                    known[k] = d.count
                ins = o.fn(eng)
                if o.dma_key is not None:
                    ins.then_inc(o.sem, 16)
                elif o.signal:
                    ins.then_inc(o.sem, 1)
            if ename == "sp":
                for o in final:
                    eng.wait_ge(o.sem, o.count)

        with nc.Block() as block:
            @block.tensor
            def _(e):
                run("pe", e)

            @block.scalar
            def _(e):
                run("act", e)

            @block.vector
            def _(e):
                run("dve", e)

            @block.gpsimd
            def _(e):
                run("pool", e)

            @block.sync
            def _(e):
                run("sp", e)


class Arena:
    def __init__(self, ap, width):
        self.ap, self.width, self.off, self.top = ap, width, 0, width

    def reset(self):
        self.off, self.top = 0, self.width

    def alloc_top(self, n, dt):
        w = n if dt == F32 else (n + 1) // 2
        assert self.top - w >= self.off, ("arena overflow (top)", self.off, w, self.top)
        self.top -= w
        a = self.ap[:, self.top:self.top + w]
        if dt != F32:
            a = a.bitcast(dt)[:, 0:n]
        return a

    def alloc(self, n, dt):
        w = n if dt == F32 else (n + 1) // 2
        assert self.off + w <= self.top, ("arena overflow", self.off, w, self.top)
        a = self.ap[:, self.off:self.off + w]
        self.off += w
        if dt != F32:
            a = a.bitcast(dt)[:, 0:n]
        return a


def make_cst():
    c = np.zeros((128, NCST), np.float32)
    p = np.arange(128)
    c[:, C_ID:C_ID + 128] = np.eye(128)
    c[:, C_TRI:C_TRI + 128] = (p[:, None] <= p[None, :])
    c[127, C_SEL:C_SEL + 128] = 1.0
    c[:, C_MASK:C_MASK + 128] = np.where(p[:, None] > p[None, :], NEG, 0.0)
    c[0:64, C_WE:C_WE + 64] = 1.0 / 64
    c[64, C_WE:C_WE + 64] = EPS
    c[64:128, C_WO + 64:C_WO + 128] = 1.0 / 64
    c[63, C_WO + 64:C_WO + 128] = EPS
    c[0:64, C_WG:C_WG + 64] = 1.0 / 64
    c[64:128, C_WG + 64:C_WG + 128] = 1.0 / 64
    c[:, C_COLS + 0] = (p < 16)
    c[:, C_COLS + 1] = np.where(p < 16, 0.0, NEG)
    c[:, C_COLS + 2] = -1.0 * (p < 16)
    c[:, C_COLS + 3] = (p == 63)
    c[:, C_COLS + 4] = 1.0
    c[:, C_COLS + 5] = 0.0
    return c


def build_program(NB):
    NG = NB // 4
    T = 16 + 128 * NB
    SEQ = 128 * NB
    NT = NB + 1
    NF = NT * 8

    nc = bass.Bass("TRN2", target_bir_lowering=False)

    def dram(name, shape, kind="ExternalInput"):
        return nc.dram_tensor(name, shape, F32, kind=kind).ap()

    x_d = dram("x", [SEQ, D])
    meta_d = dram("meta", [16, D])
    ng_d = dram("norm_g", [1, D])
    win_d = dram("w_in", [D, DIN])
    bfr_d = dram("bf_rep", [1, NF])
    cw_d = dram("cwT", [128, 12])
    ag_d = dram("ag", [128, 4])
    cg_d = dram("cg", [128, 4])
    wout_d = dram("w_out", [D, D])
    fg_d = dram("fg", [1, D])
    cst_d = dram("cst", [128, NCST])
    ones_d = dram("ones_bf", [1, T // 2])
    y_d = dram("y", [SEQ, D], kind="ExternalOutput")

    win_v = win_d.rearrange("(kc p) e -> p kc e", p=128)
    wout_v = wout_d.rearrange("(kc p) e -> p kc e", p=128)

    S = Sched()

    def OP(eng, name, reads, writes, *args, **kw):
        return S.op(eng, lambda e: getattr(e, name)(*args, **kw), reads=reads, writes=writes)

    def DMA(key, out, in_, reads, writes):
        return S.op("sp", lambda e: e.dma_start(out=out, in_=in_), reads=reads, writes=writes, dma_key=key)

    def MM(out, lhsT, rhs, start, stop, reads, writes):
        return S.op("pe", lambda e: e.matmul(out, lhsT=lhsT, rhs=rhs, start=start, stop=stop),
                    reads=reads, writes=writes)

    def ACT(out, in_, func, reads, writes, **kw):
        return S.op("act", lambda e: e.activation(out=out, in_=in_, func=func, **kw), reads=reads, writes=writes)

    with ExitStack() as es:
        def sb(name, shape, dt):
            return es.enter_context(nc.sbuf_tensor("sb_" + name, shape, dt))

        cst = sb("cst", [128, NCST], F32); cst_b = Buf()
        identb = sb("identb", [128, 128], BF16); identb_b = Buf()
        maskb = sb("maskb", [128, 128], BF16); maskb_b = Buf()
        web = sb("web", [128, 256], BF16); web_b = Buf()
        uT = sb("uT", [128, KC, T], BF16)
        uT_t = [Buf() for _ in range(NB + 1)]

        def ub(p0, n):
            out = []
            if p0 < 16:
                out.append(uT_t[0])
            lo = max(p0, 16)
            hi = p0 + n
            if hi > lo:
                out += uT_t[1 + (lo - 16) // 128:1 + (hi - 16 + 127) // 128]
            return out
        mixA = sb("mixA", [128, 4, SEQ], BF16); mixA_b = [[Buf() for _ in range(NG)] for _ in range(4)]
        c_all = sb("c_all", [128, NF], F32); c_all_b = Buf()
        negc = sb("negc", [128, NF], F32); negc_b = Buf()
        cw_t = sb("cw_t", [128, 12], F32); ag_t = sb("ag_t", [128, 4], F32); cg_t = sb("cg_t", [128, 4], F32)
        small_b = Buf()
        halo = sb("halo", [128, 4, 2], F32); halo_b = [Buf() for _ in range(4)]
        stat = sb("stat", [128, 8], F32)
        stat_b = [Buf(), Buf()]
        UW = 26828
        U = sb("U", [128, UW], F32)
        ar = Arena(U[:, :], UW)

        def ps(name):
            return es.enter_context(nc.psum_tensor("ps_" + name, [128, 512], F32))
        A = [ps("A0"), ps("A1")]; A_b = [Buf(), Buf()]
        SB = [ps("S0"), ps("S1"), ps("S2")]; SB_b = [Buf(), Buf(), Buf()]
        OB = [ps("O0"), ps("O1")]; OB_b = [Buf(), Buf()]
        X = ps("X"); X_b = Buf()

        ident_f = cst[:, C_ID:C_ID + 128]
        col = lambda i: cst[:, C_COLS + i:C_COLS + i + 1]

        DMA("cst", cst[:], cst_d[:, :], [], [cst_b])
        DMA("small", cw_t[:], cw_d[:, :], [], [small_b])
        DMA("small", ag_t[:], ag_d[:, :], [], [small_b])
        DMA("small", cg_t[:], cg_d[:, :], [], [small_b])
        OP("dve", "tensor_copy", [cst_b], [identb_b], out=identb[:], in_=ident_f)
        OP("dve", "tensor_copy", [cst_b], [maskb_b], out=maskb[:], in_=cst[:, C_MASK:C_MASK + 128])
        OP("dve", "tensor_copy", [cst_b], [web_b], out=web[:], in_=cst[:, C_WE:C_WE + 256])

        ar.reset()
        wbf = [ar.alloc_top(KC * 512, BF16).rearrange("p (k e) -> p k e", e=512) for _ in range(2)]
        wbf_b = [[Buf() for _ in range(4)] for _ in range(2)]
        KEKO = ar.alloc_top(2 * T, BF16)
        KE, KO = KEKO[:, 0:T], KEKO[:, T:2 * T]; K_b = Buf()
        KEO = [KE, KO]
        vaug = ar.alloc_top(NT * 256, BF16).rearrange("p (j c) -> p j c", c=256); vaug_b = Buf()
        CP = [ar.alloc_top(4 * 128, BF16).rearrange("p (t m) -> p t m", m=128) for _ in range(2)]
        CP_b = [Buf(), Buf()]
        wst = [ar.alloc_top(KC * 128, F32).rearrange("p (k e) -> p k e", e=128) for _ in range(2)]
        wst_b = [Buf(), Buf()]
        wob = ar.alloc(KC * D, BF16).rearrange("p (k e) -> p k e", e=D)
        wob_b = [Buf() for _ in range(8)]
        wfb = ar.alloc(KC * 8, BF16).rearrange("p (k e) -> p k e", e=8); wfb_b = Buf()
        bfr = ar.alloc(NF, F32); bfr_b = Buf()
        fblk = ar.alloc(max(4 * NF, D), F32)
        fx, fa, fm, car = [fblk[:, k_ * NF:(k_ + 1) * NF] for k_ in range(4)]
        f_b = Buf()
        shared_off = ar.off

        zc, oc_ = col(5), col(4)
        vflat = vaug.rearrange("p j c -> p (j c)")
        init_ops = [
            lambda: ACT(KEKO.bitcast(F32), zc.to_broadcast([128, T]), AF.Copy, [cst_b], [K_b]),
            lambda: DMA("ones", KE[64:65, :].bitcast(F32), ones_d[:, :], [K_b], [K_b]),
            lambda: DMA("ones", KO[63:64, :].bitcast(F32), ones_d[:, :], [K_b], [K_b]),
            lambda: ACT(vflat.bitcast(F32), zc.to_broadcast([128, NT * 128]), AF.Copy, [cst_b], [vaug_b]),
            lambda: ACT(vaug[:, :, 64:65], oc_.to_broadcast([128, NT]).rearrange("p (j o) -> p j o", o=1), AF.Copy,
                        [cst_b], [vaug_b]),
            lambda: ACT(vaug[:, :, 191:192], oc_.to_broadcast([128, NT]).rearrange("p (j o) -> p j o", o=1), AF.Copy,
                        [cst_b], [vaug_b]),
            lambda: ACT(CP[0].rearrange("p t m -> p (t m)").bitcast(F32), zc.to_broadcast([128, 256]), AF.Copy,
                        [cst_b], [CP_b[0]]),
            lambda: ACT(CP[1].rearrange("p t m -> p (t m)").bitcast(F32), zc.to_broadcast([128, 256]), AF.Copy,
                        [cst_b], [CP_b[1]]),
        ]
        vaug4 = vaug.rearrange("p j (b d) -> p j b d", d=64)

        wst_cnt = [0]

        wtasks = []
        wpending = []

        def load_w(dst, dst_b, src, extra_w=(), defer=False):
            if defer:
                wtasks.append((dst, dst_b, src, extra_w))
                return
            s = wst_cnt[0] % 2
            wst_cnt[0] += 1
            DMA("wst%d" % s, wst[s][:, :, :], src, [], [wst_b[s]])
            OP("dve", "tensor_copy", [wst_b[s]], [dst_b] + list(extra_w), out=dst, in_=wst[s][:, :, :])

        def wtask_step(n=1):
            while wpending:
                s_, dst, dst_b, extra_w = wpending.pop(0)
                OP("dve", "tensor_copy", [wst_b[s_]], [dst_b] + list(extra_w), out=dst, in_=wst[s_][:, :, :])
            for _ in range(n):
                if not wtasks:
                    break
                dst, dst_b, src, extra_w = wtasks.pop(0)
                s_ = wst_cnt[0] % 2
                wst_cnt[0] += 1
                DMA("wst%d" % s_, wst[s_][:, :, :], src, [], [wst_b[s_]])
                wpending.append((s_, dst, dst_b, extra_w))

        def wtask_flush():
            while wtasks or wpending:
                wtask_step(2)

        def load_pair_w(c, defer=False):
            s = c % 2
            for i, c0 in enumerate((128 * c, 512 + 128 * c, 1024 + 128 * c, 1544 + 128 * c)):
                load_w(wbf[s][:, :, 128 * i:128 * i + 128], wbf_b[s][i], win_v[:, :, c0:c0 + 128], defer=defer)

        NXS = 4
        xt = [ar.alloc(D, F32) for _ in range(3)] + [fblk[:, 0:D]]; xt_b = [Buf() for _ in range(NXS)]
        xb = [ar.alloc(D, BF16) for _ in range(2)]; xb_b = [Buf(), Buf()]
        junk = ar.alloc(D, BF16); junk_b = Buf()
        grep = ar.alloc(D, F32); grep_b = Buf()
        DMA("grep", grep, ng_d.partition_broadcast(128), [], [grep_b])
        Xb16 = X[:].bitcast(BF16)

        def tile_stats(src, src_b, slot):
            st, st_b = stat[:, 4 * slot:4 * slot + 4], stat_b[slot]
            ACT(junk, src, AF.Square, [src_b], [junk_b, st_b], accum_out=st[:, 0:1])
            ACT(st[:, 1:2], st[:, 0:1], AF.Ln, [st_b], [st_b], scale=1.0 / D, bias=EPS)
            ACT(st[:, 2:3], st[:, 1:2], AF.Exp, [st_b], [st_b], scale=-0.5)
            return st[:, 2:3], st_b

        def kcols(j):
            return (0, 128) if j == 0 else (16 + 128 * (j - 1), 128)

        a_cnt = [0]

        def next_A():
            i = a_cnt[0] % 2
            a_cnt[0] += 1
            return A[i], A_b[i]

        def pair_w(c):
            ws = c % 2
            return [wbf[ws][:, :, 128 * i:128 * i + 128] for i in range(4)], wbf_b[ws]

        def kv_inproj_part(c, g):
            (wq, wk, wv, wz), (wq_b, wk_b, wv_b, wz_b) = pair_w(c)
            ranges = ([(0, 16)] if g == 0 else []) + [(16 + 512 * g, 512)]
            for (p0, n) in ranges:
                a, a_b = next_A()
                for kc in range(KC):
                    MM(a[:, 0:n], wk[:, kc, :], uT[:, kc, p0:p0 + n], kc == 0, kc == KC - 1, [wk_b] + ub(p0, n), [a_b])
                OP("dve", "tensor_copy", [a_b], [K_b], out=KE[0:64, p0:p0 + n], in_=a[0:64, 0:n])
                OP("dve", "tensor_copy", [a_b], [K_b], out=KO[64:128, p0:p0 + n], in_=a[64:128, 0:n])
            tiles = ([0] if g == 0 else []) + list(range(4 * g + 1, 4 * g + 5))
            for jb in range(0, len(tiles), 4):
                tl = tiles[jb:jb + 4]
                nt = len(tl)
                a, a_b = next_A()
                for t, j in enumerate(tl):
                    p0, n = kcols(j)
                    for kc in range(KC):
                        MM(a[:, 128 * t:128 * t + 128], uT[:, kc, p0:p0 + n], wv[:, kc, :], kc == 0, kc == KC - 1,
                           [wv_b] + ub(p0, n), [a_b])
                OP("dve", "tensor_copy", [a_b], [vaug_b], out=vaug4[:, tl[0]:tl[0] + nt, 0:4:3, :],
                   in_=a[:, 0:128 * nt].rearrange("p (t b d) -> p t b d", b=2, d=64))

        def kv_inproj(c):
            for g in range(NG):
                kv_inproj_part(c, g)

        TB = [(X, X_b), (SB[0], SB_b[0])]
        TB16 = [X[:].bitcast(BF16), SB[0][:].bitcast(BF16)]

        def a_load(ti):
            s3 = ti % NXS
            if ti == 0:
                OP("dve", "memset", [], [xt_b[s3]], xt[s3], 0.0)
                DMA("xt%d" % s3, xt[s3][0:16, :], meta_d[:, :], [], [xt_b[s3]])
            else:
                DMA("xt%d" % s3, xt[s3], x_d[(ti - 1) * 128:ti * 128, :], [], [xt_b[s3]])

        def a_norm(ti):
            s3, s2 = ti % NXS, ti % 2
            rs, rs_b = tile_stats(xt[s3], xt_b[s3], s2)
            OP("dve", "scalar_tensor_tensor", [xt_b[s3], rs_b, grep_b], [xb_b[s2]],
               out=xb[s2], in0=xt[s3], scalar=rs, in1=grep, op0=ALU.mult, op1=ALU.mult)

        for ti in range(min(NXS, NT)):
            a_load(ti)
        DMA("wf", wst[0][:, :, 0:8], win_v[:, :, 1536:1544], [], [wst_b[0]])
        DMA("bfr", bfr, bfr_d.partition_broadcast(128), [], [bfr_b])
        OP("dve", "tensor_copy", [wst_b[0]], [wfb_b], out=wfb, in_=wst[0][:, :, 0:8])
        a_norm(0)
        for ti in range(NT):
            s2 = ti % 2
            tb, tb_b = TB[s2]
            tb16 = TB16[s2]
            for kc in range(KC):
                S.op("pe", lambda e, kc=kc, s2=s2, tb16=tb16: e.transpose(out=tb16[:, kc * 128:(kc + 1) * 128],
                                                                           in_=xb[s2][:, kc * 128:(kc + 1) * 128],
                                                                           identity=identb[:]),
                     reads=[xb_b[s2], identb_b], writes=[tb_b])
            if ti + 1 < NT:
                a_norm(ti + 1)
            if ti >= 1 and init_ops:
                init_ops.pop(0)()
            if ti == 0:
                OP("dve", "tensor_copy", [tb_b], [uT_t[0]], out=uT[:, :, 0:16],
                   in_=tb16.rearrange("p (k t) -> p k t", t=128)[:, :, 0:16])
            else:
                p0 = 16 + 128 * (ti - 1)
                OP("dve", "tensor_copy", [tb_b], [uT_t[ti]], out=uT[:, :, p0:p0 + 128],
                   in_=tb16.rearrange("p (k t) -> p k t", t=128))
            if ti + NXS < NT:
                a_load(ti + NXS)
            if ti == 1:
                load_pair_w(0)
            if ti >= 4 and ti % 4 == 0:
                kv_inproj_part(0, ti // 4 - 1)

        while init_ops:
            init_ops.pop(0)()

        for j in range(NT):
            p0, n = kcols(j)
            for kc in range(KC):
                MM(X[:, 8 * j:8 * j + 8], uT[:, kc, p0:p0 + n], wfb[:, kc, :], kc == 0, kc == KC - 1,
                   ub(p0, n) + [wfb_b], [X_b])
        OP("dve", "tensor_tensor", [X_b, bfr_b], [f_b], out=fx, in0=X[:, 0:NF], in1=bfr, op=ALU.add)
        ACT(fa, fx, AF.Abs, [f_b], [f_b])
        ACT(fa, fa, AF.Exp, [f_b], [f_b], scale=-1.0)
        ACT(fa, fa, AF.Ln, [f_b], [f_b], bias=1.0)
        OP("dve", "tensor_scalar_min", [f_b], [f_b], out=fm, in0=fx, scalar1=0.0)
        OP("dve", "tensor_tensor", [f_b], [f_b], out=fm, in0=fm, in1=fa, op=ALU.subtract)
        OP("dve", "tensor_scalar", [f_b, cst_b], [f_b], out=fm[:, 0:8], in0=fm[:, 0:8], scalar1=col(0),
           scalar2=None, op0=ALU.mult)
        cw_sb, tot_sb = fx, fa
        MM(X[:, 0:NF], cst[:, C_TRI:C_TRI + 128], fm, True, True, [f_b, cst_b], [X_b])
        OP("dve", "tensor_copy", [X_b], [f_b], out=cw_sb, in_=X[:, 0:NF])
        MM(X[:, 0:NF], cst[:, C_SEL:C_SEL + 128], cw_sb, True, True, [f_b, cst_b], [X_b])
        OP("dve", "tensor_copy", [X_b], [f_b], out=tot_sb, in_=X[:, 0:NF])
        OP("dve", "memset", [], [f_b], car[:, 0:8], 0.0)
        for j in range(1, NT):
            OP("dve", "tensor_tensor", [f_b], [f_b], out=car[:, 8 * j:8 * j + 8], in0=car[:, 8 * j - 8:8 * j],
               in1=tot_sb[:, 8 * j - 8:8 * j], op=ALU.add)
        OP("dve", "tensor_tensor", [f_b], [c_all_b], out=c_all[:], in0=cw_sb, in1=car, op=ALU.add)
        OP("dve", "tensor_scalar", [c_all_b], [negc_b], out=negc[:, 8:NF], in0=c_all[:, 8:NF], scalar1=-1.0,
           scalar2=None, op0=ALU.mult)
        OP("dve", "tensor_scalar", [c_all_b, cst_b], [negc_b], out=negc[:, 0:8], in0=c_all[:, 0:8],
           scalar1=col(2), scalar2=col(1), op0=ALU.mult, op1=ALU.add)

        ar.off = shared_off
        QE = [ar.alloc(512, BF16) for _ in range(2)]; QO = [ar.alloc(512, BF16) for _ in range(2)]
        Q_b = [Buf(), Buf()]
        PT = [ar.alloc(512, BF16) for _ in range(4)]; PT_b = [Buf() for _ in range(4)]
        ez = ar.alloc(512, F32); ez_b = Buf()
        sz = [ar.alloc(512, F32) for _ in range(2)]; sz_b = [Buf(), Buf()]
        sqE = ar.alloc(512, F32); sqO = ar.alloc(512, F32); sq_b = Buf()
        rstd = ar.alloc(512, F32); rstd_b = Buf()
        lnr = rstd
        tmp, tmp_b = ez, ez_b
        hl = fblk.bitcast(BF16)
        hiE, loE, hiO, loO = [hl[:, 512 * k_:512 * k_ + 512] for k_ in range(4)]
        hl_b = Buf()

        kr = [(0, 16)] + [(16 + 512 * g, 512) for g in range(NG)]
        s_cnt = [0]
        q_cnt = [0]
        OcE = ar.alloc(512, F32); OcO = ar.alloc(512, F32); Oc_b = [Buf(), Buf()]
        Oc = [OcE, OcO]

        def prologue(c, G):
            (wq, wk, wv, wz), (wq_b, wk_b, wv_b, wz_b) = pair_w(c)
            q0 = 16 + 512 * G
            qs = q_cnt[0] % 2
            q_cnt[0] += 1
            a, a_b = next_A()
            for kc in range(KC):
                MM(a[:, :], wq[:, kc, :], uT[:, kc, q0:q0 + 512], kc == 0, kc == KC - 1, [wq_b] + ub(q0, 512), [a_b])
                yield qs
            OP("dve", "tensor_copy", [a_b], [Q_b[qs]], out=QE[qs][0:64, :], in_=a[0:64, :])
            OP("dve", "tensor_copy", [a_b], [Q_b[qs]], out=QO[qs][64:128, :], in_=a[64:128, :])
            OP("dve", "tensor_scalar", [c_all_b], [CP_b[qs]], out=CP[qs][:, :, 63:65],
               in0=c_all[:, 8 * (4 * G + 1):8 * (4 * G + 5)].rearrange("p (t h) -> p t h", h=8)[:, :, 2 * c:2 * c + 2],
               scalar1=8.0, scalar2=None, op0=ALU.mult)
            for t in range(4):
                MM(X[:, 128 * t:128 * t + 128], CP[qs][:, t, :], identb[:], True, True, [CP_b[qs], identb_b], [X_b])
            yield qs
            OP("dve", "tensor_copy", [X_b], [Q_b[qs]], out=QE[qs][64:128, :], in_=X[64:128, :])
            OP("dve", "tensor_copy", [X_b], [Q_b[qs]], out=QO[qs][0:64, :], in_=X[0:64, :])
            a, a_b = next_A()
            for kc in range(KC):
                MM(a[:, :], wz[:, kc, :], uT[:, kc, q0:q0 + 512], kc == 0, kc == KC - 1, [wz_b] + ub(q0, 512), [a_b])
                yield qs
            ACT(ez, a[:, :], AF.Exp, [a_b], [ez_b], scale=-1.0)
            OP("dve", "tensor_scalar_add", [ez_b], [ez_b], out=ez, in0=ez, scalar1=1.0)
            OP("dve", "reciprocal", [ez_b], [ez_b], out=ez, in_=ez)
            OP("dve", "tensor_tensor", [ez_b, a_b], [sz_b[qs]], out=sz[qs], in0=a[:, :], in1=ez, op=ALU.mult)
            yield qs

        def run_all(gen):
            qs = None
            for qs in gen:
                pass
            return qs

        def make_steps(c, G, qs):
            steps = []
            LAG = 2
            nblk = 4 * G + 5
            blocks = []
            for par in range(2):
                for j in range(nblk):
                    r = j - (4 * G + 1)
                    blocks.append((par, j, 128 * r if r > 0 else 0, r >= 0))
            slots = {}
            tot = len(blocks)
            for idx in range(tot + LAG):
                def step(idx=idx):
                    if idx < tot:
                        par, j, c0, diag = blocks[idx]
                        Kt = KEO[par]
                        Qt = (QE if par == 0 else QO)[qs]
                        hb = 2 * c + (1 - par)
                        si = s_cnt[0] % 3
                        pi = s_cnt[0] % 4
                        s_cnt[0] += 1
                        slots[idx] = pi
                        p0, n = kcols(j)
                        MM(SB[si][:, c0:512], Kt[:, p0:p0 + 128], Qt[:, c0:512], True, not diag,
                           [K_b, Q_b[qs]], [SB_b[si]])
                        if diag:
                            MM(SB[si][:, c0:c0 + 128], identb[:], maskb[:], False, True,
                               [identb_b, maskb_b], [SB_b[si]])
                        ACT(PT[pi][:, c0:512], SB[si][:, c0:512], AF.Exp, [SB_b[si], negc_b], [PT_b[pi]],
                            bias=negc[:, 8 * j + hb:8 * j + hb + 1], scale=0.125)
                    if idx >= LAG:
                        par, j, c0, diag = blocks[idx - LAG]
                        O, O_b = OB[par], OB_b[par]
                        pi = slots[idx - LAG]
                        MM(O[:, c0:512], vaug[:, j, 128 * par:128 * par + 128], PT[pi][:, c0:512],
                           j == 0, j == nblk - 1, [vaug_b, PT_b[pi]], [O_b])
                        if j == nblk - 1:
                            OP("dve", "tensor_copy", [O_b], [Oc_b[par]], out=Oc[par], in_=O[:, :])
                steps.append(step)
            return steps

        def post(c, G, qs):
            OP("dve", "tensor_tensor", [Oc_b[0]], [sq_b], out=sqE, in0=OcE, in1=OcE, op=ALU.mult)
            OP("dve", "tensor_tensor", [Oc_b[1]], [sq_b], out=sqO, in0=OcO, in1=OcO, op=ALU.mult)
            OP("dve", "tensor_copy", [sq_b], [hl_b], out=hiE, in_=sqE)
            OP("dve", "tensor_tensor", [sq_b, hl_b], [hl_b], out=loE, in0=sqE, in1=hiE, op=ALU.subtract)
            OP("dve", "tensor_copy", [sq_b], [hl_b], out=hiO, in_=sqO)
            OP("dve", "tensor_tensor", [sq_b, hl_b], [hl_b], out=loO, in0=sqO, in1=hiO, op=ALU.subtract)
            MM(X[:, :], web[:, 0:128], hiE, True, False, [web_b, hl_b], [X_b])
            MM(X[:, :], web[:, 0:128], loE, False, False, [web_b, hl_b], [X_b])
            MM(X[:, :], web[:, 128:256], hiO, False, False, [web_b, hl_b], [X_b])
            MM(X[:, :], web[:, 128:256], loO, False, True, [web_b, hl_b], [X_b])
            ACT(lnr, X[:, :], AF.Ln, [X_b], [rstd_b])
            ACT(rstd, lnr, AF.Exp, [rstd_b], [rstd_b], scale=-0.5)
            OP("dve", "tensor_tensor", [Oc_b[0], rstd_b], [tmp_b], out=tmp[0:64, :], in0=OcE[0:64, :],
               in1=rstd[0:64, :], op=ALU.mult)
            OP("dve", "tensor_tensor", [Oc_b[1], rstd_b], [tmp_b], out=tmp[64:128, :], in0=OcO[64:128, :],
               in1=rstd[64:128, :], op=ALU.mult)
            OP("dve", "scalar_tensor_tensor", [tmp_b, sz_b[qs], small_b], [mixA_b[c][G]],
               out=mixA[:, c, 512 * G:512 * G + 512], in0=tmp, scalar=ag_t[:, c:c + 1], in1=sz[qs],
               op0=ALU.mult, op1=ALU.mult)

        seq = [(c, G) for c in range(4) for G in range(NG)]
        chunk_order = [4 * blk + i for i in range(4) for blk in (1, 2, 0, 3)]
        pre_chunks = chunk_order[0:8] if NG > 2 else []
        wcv_b = [Buf() for _ in range(16)]
        wcvc = {}
        for k_, n__ in enumerate(pre_chunks):
            flat = wbf[k_ // 4].rearrange("p k e -> p (k e)")
            wcvc[n__] = flat[:, 1024 * (k_ % 4):1024 * (k_ % 4) + 1024].rearrange("p (k e) -> p k e", e=128)
        pending_post = None
        qs_next = run_all(prologue(0, 0))
        for n_, (c, G) in enumerate(seq):
            qs = qs_next
            if G == 0:
                if c > 0:
                    kv_inproj(c)
                if c + 1 < 4:
                    load_pair_w(c + 1, defer=True)
                if c == 1:
                    for w_ in range(8):
                        load_w(wob[:, :, 128 * w_:128 * w_ + 128], wob_b[w_], wout_v[:, :, 128 * w_:128 * w_ + 128],
                               defer=True)
            if c == 3 and G == 0:
                for n__ in pre_chunks[0:4]:
                    load_w(wcvc[n__], wcv_b[n__], win_v[:, :, 2056 + 128 * n__:2056 + 128 * n__ + 128],
                           extra_w=wbf_b[0], defer=True)
            if c == 3 and G == NG - 1:
                for n__ in pre_chunks[4:8]:
                    load_w(wcvc[n__], wcv_b[n__], win_v[:, :, 2056 + 128 * n__:2056 + 128 * n__ + 128],
                           extra_w=wbf_b[1], defer=True)
            wtask_step(2)
            steps = make_steps(c, G, qs)
            hook_post = min(10, 4 * G + 3)
            if G == NG - 1 and n_ + 1 < len(seq):
                wtask_flush()
            gen = prologue(*seq[n_ + 1]) if n_ + 1 < len(seq) else None
            for k_, st in enumerate(steps):
                st()
                if k_ % 16 == 12:
                    wtask_step(1)
                if k_ == hook_post and pending_post is not None:
                    post(*pending_post)
                    pending_post = None
                if gen is not None and k_ > hook_post:
                    try:
                        qs_next = next(gen)
                    except StopIteration:
                        gen = None
            if gen is not None:
                r_ = run_all(gen)
                if r_ is not None:
                    qs_next = r_
            pending_post = (c, G, qs)
        post(*pending_post)
        wtask_flush()
        wtask_flush()
        S.barrier()

        ar.reset()
        wob2 = ar.alloc(KC * D, BF16)
        wst = [ar.alloc(KC * 128, F32).rearrange("p (k e) -> p k e", e=128) for _ in range(3)]
        wst_b = [Buf() for _ in range(3)]
        ar.top = UW - 2 * (KC * 512 // 2)
        for n__ in chunk_order:
            if n__ not in wcvc:
                wcvc[n__] = ar.alloc(KC * 128, BF16).rearrange("p (k e) -> p k e", e=128)
        xt = [ar.alloc(D, F32) for _ in range(3)]; xt_b = [Buf() for _ in range(3)]
        junk = ar.alloc(D, BF16); junk_b = Buf()
        fgrep = ar.alloc(D, F32); fgrep_b = Buf()
        ymix = [ar.alloc(4 * 512, BF16).rearrange("p (i t) -> p i t", t=512) for _ in range(2)]
        ymix_b = [Buf(), Buf()]
        mc = ar.alloc(4 * 16, F32).rearrange("p (i t) -> p i t", t=16); mc_b = Buf()
        NSL = 2
        b1 = [ar.alloc(512, F32) for _ in range(NSL)]; b1_b = [Buf() for _ in range(NSL)]
        b2 = [ar.alloc(514, F32) for _ in range(NSL)]; b2_b = [Buf() for _ in range(NSL)]
        b3 = [ar.alloc(512, F32) for _ in range(NSL)]; b3_b = [Buf() for _ in range(NSL)]
        b4 = [ar.alloc(512, F32) for _ in range(NSL)]; b4_b = [Buf() for _ in range(NSL)]

        DMA("fgrep", fgrep, fg_d.partition_broadcast(128), [], [fgrep_b])
        cast_cnt = 0
        for n_ in chunk_order:
            if n_ in pre_chunks:
                continue
            sl = cast_cnt % 3
            DMA("wst%d" % sl, wst[sl][:, :, :], win_v[:, :, 2056 + 128 * n_:2056 + 128 * n_ + 128], [], [wst_b[sl]])
            if cast_cnt % 2 == 0:
                OP("dve", "tensor_copy", [wst_b[sl]], [wcv_b[n_]], out=wcvc[n_], in_=wst[sl][:, :, :])
            else:
                ACT(wcvc[n_], wst[sl][:, :, :], AF.Copy, [wst_b[sl]], [wcv_b[n_]])
            cast_cnt += 1

        A5 = [A[0], A[1], SB[0], SB[1], SB[2]]; A5_b = [A_b[0], A_b[1], SB_b[0], SB_b[1], SB_b[2]]

        def next_A5():
            i = a_cnt[0] % 5
            a_cnt[0] += 1
            return A5[i], A5_b[i]

        def conv_w(k, i):
            return cw_t[:, 4 * k + i:4 * k + i + 1]

        def meta_halo(i):
            a, a_b = next_A5()
            for kc in range(KC):
                MM(a[:, 0:16], wcvc[4 + i][:, kc, :], uT[:, kc, 0:16], kc == 0, kc == KC - 1,
                   [wcv_b[4 + i], uT_t[0]], [a_b])
            ACT(mc[:, i, :], a[:, 0:16], AF.Copy, [a_b], [mc_b])
            a2, a2_b = next_A5()
            for kc in range(KC):
                MM(a2[:, 0:16], wcvc[8 + i][:, kc, :], uT[:, kc, 0:16], kc == 0, kc == KC - 1,
                   [wcv_b[8 + i], uT_t[0]], [a2_b])
            OP("dve", "tensor_tensor", [a2_b, mc_b], [halo_b[i]], out=halo[:, i, :], in0=a2[:, 14:16],
               in1=mc[:, i, 14:16], op=ALU.mult)

        chain_cnt = [0]

        def phase1(G, i):
            q0 = 16 + 512 * G
            sl = chain_cnt[0] % NSL
            chain_cnt[0] += 1

            def inproj(blk):
                a, a_b = next_A5()
                ncol = 128 * (4 * blk + i)
                for kc in range(KC):
                    MM(a[:, :], wcvc[4 * blk + i][:, kc, :], uT[:, kc, q0:q0 + 512], kc == 0, kc == KC - 1,
                       [wcv_b[4 * blk + i]] + ub(q0, 512), [a_b])
                return a, a_b
            if G == 0:
                meta_halo(i)
            aC, aC_b = inproj(1)
            ACT(b1[sl], aC[:, :], AF.Copy, [aC_b], [b1_b[sl]])
            aX, aX_b = inproj(2)
            OP("pool", "tensor_copy", [halo_b[i]], [b2_b[sl]], out=b2[sl][:, 0:2], in_=halo[:, i, :])
            OP("dve", "tensor_tensor", [aX_b, b1_b[sl]], [b2_b[sl]], out=b2[sl][:, 2:514], in0=aX[:, :], in1=b1[sl],
               op=ALU.mult)
            OP("pool", "tensor_copy", [b2_b[sl]], [halo_b[i]], out=halo[:, i, :], in_=b2[sl][:, 512:514])
            ACT(b3[sl], b2[sl][:, 2:514], AF.Copy, [b2_b[sl], small_b], [b3_b[sl]], scale=conv_w(2, i))
            OP("dve", "scalar_tensor_tensor", [b2_b[sl], b3_b[sl], small_b], [b3_b[sl]], out=b3[sl], in0=b2[sl][:, 1:513],
               scalar=conv_w(1, i), in1=b3[sl], op0=ALU.mult, op1=ALU.add)
            OP("dve", "scalar_tensor_tensor", [b2_b[sl], b3_b[sl], small_b], [b3_b[sl]], out=b3[sl], in0=b2[sl][:, 0:512],
               scalar=conv_w(0, i), in1=b3[sl], op0=ALU.mult, op1=ALU.add)
            aB, aB_b = inproj(0)
            OP("dve", "tensor_tensor", [aB_b, b3_b[sl]], [b3_b[sl]], out=b3[sl], in0=aB[:, :], in1=b3[sl], op=ALU.mult)
            ACT(b1[sl], b3[sl], AF.Square, [b3_b[sl]], [b1_b[sl]])
            aZ, aZ_b = inproj(3)
            ACT(b4[sl], aZ[:, :], AF.Exp, [aZ_b], [b4_b[sl]], scale=-1.0)
            ACT(b4[sl], b4[sl], AF.Ln, [b4_b[sl]], [b4_b[sl]], bias=1.0)
            ACT(b4[sl], b4[sl], AF.Exp, [b4_b[sl]], [b4_b[sl]], scale=-1.0)
            OP("dve", "tensor_tensor", [aZ_b, b4_b[sl]], [b4_b[sl]], out=b4[sl], in0=aZ[:, :], in1=b4[sl], op=ALU.mult)
            return (G, i, sl)

        def phase2(state):
            G, i, sl = state
            ys = G % 2
            MM(X[:, :], cst[:, C_WG:C_WG + 128], b1[sl], True, True, [cst_b, b1_b[sl]], [X_b])
            ACT(b2[sl][:, 0:512], X[:, :], AF.Ln, [X_b], [b2_b[sl]], bias=EPS)
            ACT(b2[sl][:, 0:512], b2[sl][:, 0:512], AF.Exp, [b2_b[sl]], [b2_b[sl]], scale=-0.5)
            OP("dve", "tensor_tensor", [b3_b[sl], b2_b[sl]], [b3_b[sl]], out=b3[sl], in0=b3[sl], in1=b2[sl][:, 0:512],
               op=ALU.mult)
            OP("dve", "scalar_tensor_tensor", [b3_b[sl], b4_b[sl], small_b], [ymix_b[ys]], out=ymix[ys][:, i, :],
               in0=b3[sl], scalar=cg_t[:, i:i + 1], in1=b4[sl], op0=ALU.mult, op1=ALU.mult)

        tt_cnt = [0]

        def outproj(G, tt):
            r0 = 512 * G + 128 * tt
            ys = G % 2
            s3 = tt_cnt[0] % 3
            st_slot = tt_cnt[0] % 2
            tt_cnt[0] += 1
            DMA("xt%d" % s3, xt[s3], x_d[r0:r0 + 128, :], [], [xt_b[s3]])
            for half in range(2):
                pb, pb_b = OB[half], OB_b[half]
                for e_ in range(8):
                    if e_ < 4:
                        lh, lh_b = mixA[:, e_, r0:r0 + 128], mixA_b[e_][G]
                    else:
                        lh, lh_b = ymix[ys][:, e_ - 4, 128 * tt:128 * tt + 128], ymix_b[ys]
                    MM(pb[:, :], lh, wob[:, e_, 512 * half:512 * half + 512], e_ == 0, e_ == 7,
                       [lh_b] + wob_b, [pb_b])
                OP("dve", "tensor_tensor", [pb_b, xt_b[s3]], [xt_b[s3]], out=xt[s3][:, 512 * half:512 * half + 512],
                   in0=pb[:, :], in1=xt[s3][:, 512 * half:512 * half + 512], op=ALU.add)
            st, st_b = stat[:, 4 * st_slot:4 * st_slot + 4], stat_b[st_slot]
            ACT(junk, xt[s3], AF.Square, [xt_b[s3]], [junk_b, st_b], accum_out=st[:, 0:1])
            ACT(st[:, 1:2], st[:, 0:1], AF.Ln, [st_b], [st_b], scale=1.0 / D, bias=EPS)
            ACT(st[:, 2:3], st[:, 1:2], AF.Exp, [st_b], [st_b], scale=-0.5)
            OP("dve", "scalar_tensor_tensor", [xt_b[s3], st_b, fgrep_b], [xt_b[s3]], out=xt[s3], in0=xt[s3],
               scalar=st[:, 2:3], in1=fgrep, op0=ALU.mult, op1=ALU.mult)
            o = DMA("y%d" % s3, y_d[r0:r0 + 128, :], xt[s3], [xt_b[s3]], [])
            o.final = True

        prev = None
        for G in range(NG):
            for i in range(4):
                st_ = phase1(G, i)
                if prev is not None:
                    phase2(prev)
                prev = st_
                if G > 0:
                    outproj(G - 1, i)
        phase2(prev)
        for tt in range(4):
            outproj(NG - 1, tt)
        S.emit(nc, es)
    return nc


_CACHE = {}


def _host_inputs(NB, meta, norm_g, w_in, b_f, conv_w, attn_norm_g, conv_norm_g, w_out, final_norm_g):
    f32 = np.float32
    w = np.array(w_in[0], dtype=f32, copy=True)
    swap = np.array([1, 0, 3, 2, 5, 4, 7, 6])
    w[:, 1536:1544] = w[:, 1536:1544][:, swap]
    bf = np.asarray(b_f[0], f32)[swap]
    shared = {
        "meta": np.ascontiguousarray(meta, f32),
        "norm_g": np.ascontiguousarray(norm_g[0:1], f32),
        "w_in": np.ascontiguousarray(w),
        "bf_rep": np.ascontiguousarray(np.tile(bf, NB + 1)[None, :], f32),
        "cwT": np.ascontiguousarray(np.asarray(conv_w[0], f32).reshape(3, 4, 128).transpose(2, 0, 1).reshape(128, 12)),
        "ag": np.ascontiguousarray(np.asarray(attn_norm_g[0], f32).reshape(4, 128).T),
        "cg": np.ascontiguousarray(np.asarray(conv_norm_g[0], f32).reshape(4, 128).T),
        "w_out": np.ascontiguousarray(w_out[0], f32),
        "fg": np.ascontiguousarray(np.asarray(final_norm_g, f32)[None, :]),
        "cst": make_cst(),
        "ones_bf": np.full((1, (16 + 128 * NB) // 2), 0x3F803F80, dtype=np.uint32).view(np.float32),
    }
    return shared


def kernel(x, meta, norm_g, w_in, b_f, conv_w, attn_norm_g, conv_norm_g, w_out, final_norm_g):
    x = np.asarray(x, np.float32)
    B, SEQ, _ = x.shape
    NB = SEQ // 128
    if NB not in _CACHE:
        _CACHE[NB] = build_program(NB)
    nc = _CACHE[NB]
    shared = _host_inputs(NB, meta, norm_g, w_in, b_f, conv_w, attn_norm_g, conv_norm_g, w_out, final_norm_g)
    in_maps = [dict(shared, x=np.ascontiguousarray(x[b])) for b in range(B)]
    res = run_bass_kernel_spmd(nc, in_maps, core_ids=list(range(B)))
    return np.stack([np.asarray(res.results[b]["y"], np.float32) for b in range(B)], axis=0)
```

```python
import numpy as np
from contextlib import ExitStack
import concourse.bass as bass
import concourse.mybir as mybir
from concourse.bass_utils import run_bass_kernel_spmd

F32 = mybir.dt.float32
BF16 = mybir.dt.bfloat16
AF = mybir.ActivationFunctionType
ALU = mybir.AluOpType

D = 1024
KC = 8
DIN = 4104
EPS = 1e-6
NEG = -30000.0
C_ID, C_TRI, C_SEL, C_MASK, C_WE, C_WO, C_WG, C_COLS = 0, 128, 256, 384, 512, 640, 768, 896
NCST = 904


class Buf:
    __slots__ = ("name", "last_w", "readers")

    def __init__(self, name=""):
        self.name = name
        self.last_w = None
        self.readers = []


class Op:
    __slots__ = ("eng", "fn", "deps", "signal", "count", "dma_key", "sem", "final", "idx")

    def __init__(self, eng, fn, dma_key=None):
        self.eng = eng
        self.fn = fn
        self.deps = []
        self.signal = False
        self.count = None
        self.dma_key = dma_key
        self.sem = None
        self.final = False


class Sched:
    ENGS = ("pe", "act", "dve", "pool", "sp")

    def __init__(self):
        self.ops = []
        self.last = {}
        self.pending_barrier = {}

    def op(self, eng, fn, reads=(), writes=(), dma_key=None):
        o = Op(eng, fn, dma_key)
        deps = {}
        for b in reads:
            if b.last_w is not None:
                deps[id(b.last_w)] = b.last_w
        for b in writes:
            if b.last_w is not None:
                deps[id(b.last_w)] = b.last_w
            for r in b.readers:
                deps[id(r)] = r
        if eng in self.pending_barrier:
            for d in self.pending_barrier.pop(eng):
                deps[id(d)] = d
        best = {}
        for d in deps.values():
            k = ("dma", d.dma_key) if d.dma_key is not None else d.eng
            if k not in best or best[k].idx < d.idx:
                best[k] = d
        o.deps = list(best.values())
        o.idx = len(self.ops)
        for b in reads:
            b.readers.append(o)
        for b in writes:
            b.last_w = o
            b.readers = []
        self.ops.append(o)
        self.last[("dma", dma_key) if dma_key is not None else eng] = o
        return o

    def barrier(self):
        lasts = list(self.last.values())
        for e in self.ENGS:
            self.pending_barrier[e] = list(self.pending_barrier.get(e, [])) + lasts

    def emit(self, nc, es):
        ops = self.ops
        for o in ops:
            for d in o.deps:
                if d.dma_key is not None:
                    continue
                if d.eng == "pe" and o.eng == "pe" and o.dma_key is None:
                    continue
                d.signal = True
        eng_sem = {e: es.enter_context(nc.semaphore("s_" + e)) for e in ("pe", "act", "dve", "pool")}
        cnt = {e: 0 for e in eng_sem}
        dma_sems, dma_cnt = {}, {}
        for o in ops:
            if o.dma_key is not None:
                if o.dma_key not in dma_sems:
                    dma_sems[o.dma_key] = es.enter_context(nc.semaphore("d_" + str(o.dma_key)))
                    dma_cnt[o.dma_key] = 0
                dma_cnt[o.dma_key] += 16
                o.sem, o.count = dma_sems[o.dma_key], dma_cnt[o.dma_key]
            elif o.signal:
                cnt[o.eng] += 1
                o.sem, o.count = eng_sem[o.eng], cnt[o.eng]
        streams = {e: [o for o in ops if o.eng == e] for e in self.ENGS}
        final = [o for o in ops if o.final]

        def run(ename, eng):
            known = {}
            for o in streams[ename]:
                for d in o.deps:
                    if d.sem is None:
                        continue
                    if d.eng == "pe" and ename == "pe" and d.dma_key is None and o.dma_key is None:
                        continue
                    k = id(d.sem)
                    if known.get(k, 0) >= d.count:
                        continue
                    eng.wait_ge(d.sem, d.count)
                    known[k] = d.count
                ins = o.fn(eng)
                if o.dma_key is not None:
                    ins.then_inc(o.sem, 16)
                elif o.signal:
                    ins.then_inc(o.sem, 1)
            if ename == "sp":
                for o in final:
                    eng.wait_ge(o.sem, o.count)

        with nc.Block() as block:
            @block.tensor
            def _(e):
                run("pe", e)

            @block.scalar
            def _(e):
                run("act", e)

            @block.vector
            def _(e):
                run("dve", e)

            @block.gpsimd
            def _(e):
                run("pool", e)

            @block.sync
            def _(e):
                run("sp", e)


class Arena:
    def __init__(self, ap, width):
        self.ap, self.width, self.off, self.top = ap, width, 0, width

    def reset(self):
        self.off, self.top = 0, self.width

    def alloc_top(self, n, dt):
        w = n if dt == F32 else (n + 1) // 2
        assert self.top - w >= self.off, ("arena overflow (top)", self.off, w, self.top)
        self.top -= w
        a = self.ap[:, self.top:self.top + w]
        if dt != F32:
            a = a.bitcast(dt)[:, 0:n]
        return a

    def alloc(self, n, dt):
        w = n if dt == F32 else (n + 1) // 2
        assert self.off + w <= self.top, ("arena overflow", self.off, w, self.top)
        a = self.ap[:, self.off:self.off + w]
        self.off += w
        if dt != F32:
            a = a.bitcast(dt)[:, 0:n]
        return a


def make_cst():
    c = np.zeros((128, NCST), np.float32)
    p = np.arange(128)
    c[:, C_ID:C_ID + 128] = np.eye(128)
    c[:, C_TRI:C_TRI + 128] = (p[:, None] <= p[None, :])
    c[127, C_SEL:C_SEL + 128] = 1.0
    c[:, C_MASK:C_MASK + 128] = np.where(p[:, None] > p[None, :], NEG, 0.0)
    c[0:64, C_WE:C_WE + 64] = 1.0 / 64
    c[64, C_WE:C_WE + 64] = EPS
    c[64:128, C_WO + 64:C_WO + 128] = 1.0 / 64
    c[63, C_WO + 64:C_WO + 128] = EPS
    c[0:64, C_WG:C_WG + 64] = 1.0 / 64
    c[64:128, C_WG + 64:C_WG + 128] = 1.0 / 64
    c[:, C_COLS + 0] = (p < 16)
    c[:, C_COLS + 1] = np.where(p < 16, 0.0, NEG)
    c[:, C_COLS + 2] = -1.0 * (p < 16)
    c[:, C_COLS + 3] = (p == 63)
    c[:, C_COLS + 4] = 1.0
    c[:, C_COLS + 5] = 0.0
    return c


def build_program(NB):
    NG = NB // 4
    T = 16 + 128 * NB
    SEQ = 128 * NB
    NT = NB + 1
    NF = NT * 8

    nc = bass.Bass("TRN2", target_bir_lowering=False)

    def dram(name, shape, kind="ExternalInput"):
        return nc.dram_tensor(name, shape, F32, kind=kind).ap()

    x_d = dram("x", [SEQ, D])
    meta_d = dram("meta", [16, D])
    ng_d = dram("norm_g", [1, D])
    win_d = dram("w_in", [D, DIN])
    bfr_d = dram("bf_rep", [1, NF])
    cw_d = dram("cwT", [128, 12])
    ag_d = dram("ag", [128, 4])
    cg_d = dram("cg", [128, 4])
    wout_d = dram("w_out", [D, D])
    fg_d = dram("fg", [1, D])
    cst_d = dram("cst", [128, NCST])
    ones_d = dram("ones_bf", [1, T // 2])
    y_d = dram("y", [SEQ, D], kind="ExternalOutput")

    win_v = win_d.rearrange("(kc p) e -> p kc e", p=128)
    wout_v = wout_d.rearrange("(kc p) e -> p kc e", p=128)

    S = Sched()

    def OP(eng, name, reads, writes, *args, **kw):
        return S.op(eng, lambda e: getattr(e, name)(*args, **kw), reads=reads, writes=writes)

    def DMA(key, out, in_, reads, writes):
        return S.op("sp", lambda e: e.dma_start(out=out, in_=in_), reads=reads, writes=writes, dma_key=key)

    def MM(out, lhsT, rhs, start, stop, reads, writes):
        return S.op("pe", lambda e: e.matmul(out, lhsT=lhsT, rhs=rhs, start=start, stop=stop),
                    reads=reads, writes=writes)

    def ACT(out, in_, func, reads, writes, **kw):
        return S.op("act", lambda e: e.activation(out=out, in_=in_, func=func, **kw), reads=reads, writes=writes)

    with ExitStack() as es:
        def sb(name, shape, dt):
            return es.enter_context(nc.sbuf_tensor("sb_" + name, shape, dt))

        cst = sb("cst", [128, NCST], F32); cst_b = Buf()
        identb = sb("identb", [128, 128], BF16); identb_b = Buf()
        maskb = sb("maskb", [128, 128], BF16); maskb_b = Buf()
        uT = sb("uT", [128, KC, T], BF16)
        uT_t = [Buf() for _ in range(NB + 1)]

        def ub(p0, n):
            out = []
            if p0 < 16:
                out.append(uT_t[0])
            lo = max(p0, 16)
            hi = p0 + n
            if hi > lo:
                out += uT_t[1 + (lo - 16) // 128:1 + (hi - 16 + 127) // 128]
            return out
        mixA = sb("mixA", [128, 4, SEQ], BF16); mixA_b = [[Buf() for _ in range(NG)] for _ in range(4)]
        c_all = sb("c_all", [128, NF], F32); c_all_b = Buf()
        negc = sb("negc", [128, NF], F32); negc_b = Buf()
        cw_t = sb("cw_t", [128, 12], F32); ag_t = sb("ag_t", [128, 4], F32); cg_t = sb("cg_t", [128, 4], F32)
        small_b = Buf()
        halo = sb("halo", [128, 4, 2], F32); halo_b = [Buf() for _ in range(4)]
        stat = sb("stat", [128, 8], F32)
        stat_b = [Buf(), Buf()]
        UW = 26900
        U = sb("U", [128, UW], F32)
        ar = Arena(U[:, :], UW)

        def ps(name):
            return es.enter_context(nc.psum_tensor("ps_" + name, [128, 512], F32))
        A = [ps("A0"), ps("A1")]; A_b = [Buf(), Buf()]
        SB = [ps("S0"), ps("S1"), ps("S2")]; SB_b = [Buf(), Buf(), Buf()]
        OB = [ps("O0"), ps("O1")]; OB_b = [Buf(), Buf()]
        X = ps("X"); X_b = Buf()

        ident_f = cst[:, C_ID:C_ID + 128]
        col = lambda i: cst[:, C_COLS + i:C_COLS + i + 1]

        DMA("cst", cst[:], cst_d[:, :], [], [cst_b])
        DMA("small", cw_t[:], cw_d[:, :], [], [small_b])
        DMA("small", ag_t[:], ag_d[:, :], [], [small_b])
        DMA("small", cg_t[:], cg_d[:, :], [], [small_b])
        OP("dve", "tensor_copy", [cst_b], [identb_b], out=identb[:], in_=ident_f)
        OP("dve", "tensor_copy", [cst_b], [maskb_b], out=maskb[:], in_=cst[:, C_MASK:C_MASK + 128])

        ar.reset()
        wbf = [ar.alloc_top(KC * 512, BF16).rearrange("p (k e) -> p k e", e=512) for _ in range(2)]
        wbf_b = [[Buf() for _ in range(4)] for _ in range(2)]
        KEKO = ar.alloc_top(2 * T, BF16)
        KE, KO = KEKO[:, 0:T], KEKO[:, T:2 * T]; K_b = Buf()
        KEO = [KE, KO]
        vaug = ar.alloc_top(NT * 256, BF16).rearrange("p (j c) -> p j c", c=256); vaug_b = Buf()
        CP = [ar.alloc_top(4 * 128, BF16).rearrange("p (t m) -> p t m", m=128) for _ in range(2)]
        CP_b = [Buf(), Buf()]
        wst = [ar.alloc_top(KC * 128, F32).rearrange("p (k e) -> p k e", e=128) for _ in range(2)]
        wst_b = [Buf(), Buf()]
        wob = ar.alloc(KC * D, BF16).rearrange("p (k e) -> p k e", e=D)
        wob_b = [Buf() for _ in range(8)]
        wfb = ar.alloc(KC * 8, BF16).rearrange("p (k e) -> p k e", e=8); wfb_b = Buf()
        bfr = ar.alloc(NF, F32); bfr_b = Buf()
        fblk = ar.alloc(max(4 * NF, D), F32)
        fx, fa, fm, car = [fblk[:, k_ * NF:(k_ + 1) * NF] for k_ in range(4)]
        f_b = Buf()
        shared_off = ar.off

        zc, oc_ = col(5), col(4)
        vflat = vaug.rearrange("p j c -> p (j c)")
        init_ops = [
            lambda: ACT(KEKO.bitcast(F32), zc.to_broadcast([128, T]), AF.Copy, [cst_b], [K_b]),
            lambda: DMA("ones", KE[64:65, :].bitcast(F32), ones_d[:, :], [K_b], [K_b]),
            lambda: DMA("ones", KO[63:64, :].bitcast(F32), ones_d[:, :], [K_b], [K_b]),
            lambda: ACT(vflat.bitcast(F32), zc.to_broadcast([128, NT * 128]), AF.Copy, [cst_b], [vaug_b]),
            lambda: ACT(vaug[:, :, 64:65], oc_.to_broadcast([128, NT]).rearrange("p (j o) -> p j o", o=1), AF.Copy,
                        [cst_b], [vaug_b]),
            lambda: ACT(vaug[:, :, 191:192], oc_.to_broadcast([128, NT]).rearrange("p (j o) -> p j o", o=1), AF.Copy,
                        [cst_b], [vaug_b]),
            lambda: ACT(CP[0].rearrange("p t m -> p (t m)").bitcast(F32), zc.to_broadcast([128, 256]), AF.Copy,
                        [cst_b], [CP_b[0]]),
            lambda: ACT(CP[1].rearrange("p t m -> p (t m)").bitcast(F32), zc.to_broadcast([128, 256]), AF.Copy,
                        [cst_b], [CP_b[1]]),
        ]
        vaug4 = vaug.rearrange("p j (b d) -> p j b d", d=64)

        wst_cnt = [0]

        wtasks = []
        wpending = []

        def load_w(dst, dst_b, src, extra_w=(), defer=False):
            if defer:
                wtasks.append((dst, dst_b, src, extra_w))
                return
            s = wst_cnt[0] % 2
            wst_cnt[0] += 1
            DMA("wst%d" % s, wst[s][:, :, :], src, [], [wst_b[s]])
            OP("dve", "tensor_copy", [wst_b[s]], [dst_b] + list(extra_w), out=dst, in_=wst[s][:, :, :])

        def wtask_step(n=1):
            while wpending:
                s_, dst, dst_b, extra_w = wpending.pop(0)
                OP("dve", "tensor_copy", [wst_b[s_]], [dst_b] + list(extra_w), out=dst, in_=wst[s_][:, :, :])
            for _ in range(n):
                if not wtasks:
                    break
                dst, dst_b, src, extra_w = wtasks.pop(0)
                s_ = wst_cnt[0] % 2
                wst_cnt[0] += 1
                DMA("wst%d" % s_, wst[s_][:, :, :], src, [], [wst_b[s_]])
                wpending.append((s_, dst, dst_b, extra_w))

        def wtask_flush():
            while wtasks or wpending:
                wtask_step(2)

        def load_pair_w(c, defer=False):
            s = c % 2
            for i, c0 in enumerate((128 * c, 512 + 128 * c, 1024 + 128 * c, 1544 + 128 * c)):
                load_w(wbf[s][:, :, 128 * i:128 * i + 128], wbf_b[s][i], win_v[:, :, c0:c0 + 128], defer=defer)

        NXS = 4
        xt = [ar.alloc(D, F32) for _ in range(3)] + [fblk[:, 0:D]]; xt_b = [Buf() for _ in range(NXS)]
        xb = [ar.alloc(D, BF16) for _ in range(2)]; xb_b = [Buf(), Buf()]
        junk = ar.alloc(D, BF16); junk_b = Buf()
        grep = ar.alloc(D, F32); grep_b = Buf()
        DMA("grep", grep, ng_d.partition_broadcast(128), [], [grep_b])
        Xb16 = X[:].bitcast(BF16)

        def tile_stats(src, src_b, slot):
            st, st_b = stat[:, 4 * slot:4 * slot + 4], stat_b[slot]
            ACT(junk, src, AF.Square, [src_b], [junk_b, st_b], accum_out=st[:, 0:1])
            ACT(st[:, 1:2], st[:, 0:1], AF.Ln, [st_b], [st_b], scale=1.0 / D, bias=EPS)
            ACT(st[:, 2:3], st[:, 1:2], AF.Exp, [st_b], [st_b], scale=-0.5)
            return st[:, 2:3], st_b

        def kcols(j):
            return (0, 128) if j == 0 else (16 + 128 * (j - 1), 128)

        a_cnt = [0]

        def next_A():
            i = a_cnt[0] % 2
            a_cnt[0] += 1
            return A[i], A_b[i]

        def pair_w(c):
            ws = c % 2
            return [wbf[ws][:, :, 128 * i:128 * i + 128] for i in range(4)], wbf_b[ws]

        def kv_inproj_part(c, g):
            (wq, wk, wv, wz), (wq_b, wk_b, wv_b, wz_b) = pair_w(c)
            ranges = ([(0, 16)] if g == 0 else []) + [(16 + 512 * g, 512)]
            for (p0, n) in ranges:
                a, a_b = next_A()
                for kc in range(KC):
                    MM(a[:, 0:n], wk[:, kc, :], uT[:, kc, p0:p0 + n], kc == 0, kc == KC - 1, [wk_b] + ub(p0, n), [a_b])
                OP("dve", "tensor_copy", [a_b], [K_b], out=KE[0:64, p0:p0 + n], in_=a[0:64, 0:n])
                OP("dve", "tensor_copy", [a_b], [K_b], out=KO[64:128, p0:p0 + n], in_=a[64:128, 0:n])
            tiles = ([0] if g == 0 else []) + list(range(4 * g + 1, 4 * g + 5))
            for jb in range(0, len(tiles), 4):
                tl = tiles[jb:jb + 4]
                nt = len(tl)
                a, a_b = next_A()
                for t, j in enumerate(tl):
                    p0, n = kcols(j)
                    for kc in range(KC):
                        MM(a[:, 128 * t:128 * t + 128], uT[:, kc, p0:p0 + n], wv[:, kc, :], kc == 0, kc == KC - 1,
                           [wv_b] + ub(p0, n), [a_b])
                OP("dve", "tensor_copy", [a_b], [vaug_b], out=vaug4[:, tl[0]:tl[0] + nt, 0:4:3, :],
                   in_=a[:, 0:128 * nt].rearrange("p (t b d) -> p t b d", b=2, d=64))

        def kv_inproj(c):
            for g in range(NG):
                kv_inproj_part(c, g)

        TB = [(X, X_b), (SB[0], SB_b[0])]
        TB16 = [X[:].bitcast(BF16), SB[0][:].bitcast(BF16)]

        def a_load(ti):
            s3 = ti % NXS
            if ti == 0:
                OP("dve", "memset", [], [xt_b[s3]], xt[s3], 0.0)
                DMA("xt%d" % s3, xt[s3][0:16, :], meta_d[:, :], [], [xt_b[s3]])
            else:
                DMA("xt%d" % s3, xt[s3], x_d[(ti - 1) * 128:ti * 128, :], [], [xt_b[s3]])

        def a_norm(ti):
            s3, s2 = ti % NXS, ti % 2
            rs, rs_b = tile_stats(xt[s3], xt_b[s3], s2)
            OP("dve", "scalar_tensor_tensor", [xt_b[s3], rs_b, grep_b], [xb_b[s2]],
               out=xb[s2], in0=xt[s3], scalar=rs, in1=grep, op0=ALU.mult, op1=ALU.mult)

        for ti in range(min(NXS, NT)):
            a_load(ti)
        DMA("wf", wst[0][:, :, 0:8], win_v[:, :, 1536:1544], [], [wst_b[0]])
        DMA("bfr", bfr, bfr_d.partition_broadcast(128), [], [bfr_b])
        OP("dve", "tensor_copy", [wst_b[0]], [wfb_b], out=wfb, in_=wst[0][:, :, 0:8])
        a_norm(0)
        for ti in range(NT):
            s2 = ti % 2
            tb, tb_b = TB[s2]
            tb16 = TB16[s2]
            for kc in range(KC):
                S.op("pe", lambda e, kc=kc, s2=s2, tb16=tb16: e.transpose(out=tb16[:, kc * 128:(kc + 1) * 128],
                                                                           in_=xb[s2][:, kc * 128:(kc + 1) * 128],
                                                                           identity=identb[:]),
                     reads=[xb_b[s2], identb_b], writes=[tb_b])
            if ti + 1 < NT:
                a_norm(ti + 1)
            if ti >= 1 and init_ops:
                init_ops.pop(0)()
            if ti == 0:
                OP("dve", "tensor_copy", [tb_b], [uT_t[0]], out=uT[:, :, 0:16],
                   in_=tb16.rearrange("p (k t) -> p k t", t=128)[:, :, 0:16])
            else:
                p0 = 16 + 128 * (ti - 1)
                OP("dve", "tensor_copy", [tb_b], [uT_t[ti]], out=uT[:, :, p0:p0 + 128],
                   in_=tb16.rearrange("p (k t) -> p k t", t=128))
            if ti + NXS < NT:
                a_load(ti + NXS)
            if ti == 1:
                load_pair_w(0)
            if ti >= 4 and ti % 4 == 0:
                kv_inproj_part(0, ti // 4 - 1)

        while init_ops:
            init_ops.pop(0)()

        for j in range(NT):
            p0, n = kcols(j)
            for kc in range(KC):
                MM(X[:, 8 * j:8 * j + 8], uT[:, kc, p0:p0 + n], wfb[:, kc, :], kc == 0, kc == KC - 1,
                   ub(p0, n) + [wfb_b], [X_b])
        OP("dve", "tensor_tensor", [X_b, bfr_b], [f_b], out=fx, in0=X[:, 0:NF], in1=bfr, op=ALU.add)
        ACT(fa, fx, AF.Abs, [f_b], [f_b])
        ACT(fa, fa, AF.Exp, [f_b], [f_b], scale=-1.0)
        ACT(fa, fa, AF.Ln, [f_b], [f_b], bias=1.0)
        OP("dve", "tensor_scalar_min", [f_b], [f_b], out=fm, in0=fx, scalar1=0.0)
        OP("dve", "tensor_tensor", [f_b], [f_b], out=fm, in0=fm, in1=fa, op=ALU.subtract)
        OP("dve", "tensor_scalar", [f_b, cst_b], [f_b], out=fm[:, 0:8], in0=fm[:, 0:8], scalar1=col(0),
           scalar2=None, op0=ALU.mult)
        cw_sb, tot_sb = fx, fa
        MM(X[:, 0:NF], cst[:, C_TRI:C_TRI + 128], fm, True, True, [f_b, cst_b], [X_b])
        OP("dve", "tensor_copy", [X_b], [f_b], out=cw_sb, in_=X[:, 0:NF])
        MM(X[:, 0:NF], cst[:, C_SEL:C_SEL + 128], cw_sb, True, True, [f_b, cst_b], [X_b])
        OP("dve", "tensor_copy", [X_b], [f_b], out=tot_sb, in_=X[:, 0:NF])
        OP("dve", "memset", [], [f_b], car[:, 0:8], 0.0)
        for j in range(1, NT):
            OP("dve", "tensor_tensor", [f_b], [f_b], out=car[:, 8 * j:8 * j + 8], in0=car[:, 8 * j - 8:8 * j],
               in1=tot_sb[:, 8 * j - 8:8 * j], op=ALU.add)
        OP("dve", "tensor_tensor", [f_b], [c_all_b], out=c_all[:], in0=cw_sb, in1=car, op=ALU.add)
        OP("dve", "tensor_scalar", [c_all_b], [negc_b], out=negc[:, 8:NF], in0=c_all[:, 8:NF], scalar1=-1.0,
           scalar2=None, op0=ALU.mult)
        OP("dve", "tensor_scalar", [c_all_b, cst_b], [negc_b], out=negc[:, 0:8], in0=c_all[:, 0:8],
           scalar1=col(2), scalar2=col(1), op0=ALU.mult, op1=ALU.add)

        ar.off = shared_off
        QE = [ar.alloc(512, BF16) for _ in range(2)]; QO = [ar.alloc(512, BF16) for _ in range(2)]
        Q_b = [Buf(), Buf()]
        PT = [ar.alloc(512, BF16) for _ in range(4)]; PT_b = [Buf() for _ in range(4)]
        ez = ar.alloc(512, F32); ez_b = Buf()
        sz = [ar.alloc(512, F32) for _ in range(2)]; sz_b = [Buf(), Buf()]
        sqE = ar.alloc(512, F32); sqO = ar.alloc(512, F32); sq_b = Buf()
        rstd = ar.alloc(512, F32); rstd_b = Buf()
        lnr = rstd
        tmp, tmp_b = ez, ez_b

        kr = [(0, 16)] + [(16 + 512 * g, 512) for g in range(NG)]
        s_cnt = [0]
        q_cnt = [0]
        OcE = ar.alloc(512, F32); OcO = ar.alloc(512, F32); Oc_b = [Buf(), Buf()]
        Oc = [OcE, OcO]

        def prologue(c, G):
            (wq, wk, wv, wz), (wq_b, wk_b, wv_b, wz_b) = pair_w(c)
            q0 = 16 + 512 * G
            qs = q_cnt[0] % 2
            q_cnt[0] += 1
            a, a_b = next_A()
            for kc in range(KC):
                MM(a[:, :], wq[:, kc, :], uT[:, kc, q0:q0 + 512], kc == 0, kc == KC - 1, [wq_b] + ub(q0, 512), [a_b])
                yield qs
            OP("dve", "tensor_copy", [a_b], [Q_b[qs]], out=QE[qs][0:64, :], in_=a[0:64, :])
            OP("dve", "tensor_copy", [a_b], [Q_b[qs]], out=QO[qs][64:128, :], in_=a[64:128, :])
            OP("dve", "tensor_scalar", [c_all_b], [CP_b[qs]], out=CP[qs][:, :, 63:65],
               in0=c_all[:, 8 * (4 * G + 1):8 * (4 * G + 5)].rearrange("p (t h) -> p t h", h=8)[:, :, 2 * c:2 * c + 2],
               scalar1=8.0, scalar2=None, op0=ALU.mult)
            for t in range(4):
                MM(X[:, 128 * t:128 * t + 128], CP[qs][:, t, :], identb[:], True, True, [CP_b[qs], identb_b], [X_b])
            yield qs
            OP("dve", "tensor_copy", [X_b], [Q_b[qs]], out=QE[qs][64:128, :], in_=X[64:128, :])
            OP("dve", "tensor_copy", [X_b], [Q_b[qs]], out=QO[qs][0:64, :], in_=X[0:64, :])
            a, a_b = next_A()
            for kc in range(KC):
                MM(a[:, :], wz[:, kc, :], uT[:, kc, q0:q0 + 512], kc == 0, kc == KC - 1, [wz_b] + ub(q0, 512), [a_b])
                yield qs
            ACT(ez, a[:, :], AF.Exp, [a_b], [ez_b], scale=-1.0)
            OP("dve", "tensor_scalar_add", [ez_b], [ez_b], out=ez, in0=ez, scalar1=1.0)
            OP("dve", "reciprocal", [ez_b], [ez_b], out=ez, in_=ez)
            OP("dve", "tensor_tensor", [ez_b, a_b], [sz_b[qs]], out=sz[qs], in0=a[:, :], in1=ez, op=ALU.mult)
            yield qs

        def run_all(gen):
            qs = None
            for qs in gen:
                pass
            return qs

        def make_steps(c, G, qs):
            steps = []
            LAG = 2
            nblk = 4 * G + 5
            blocks = []
            for par in range(2):
                for j in range(nblk):
                    r = j - (4 * G + 1)
                    blocks.append((par, j, 128 * r if r > 0 else 0, r >= 0))
            slots = {}
            tot = len(blocks)
            for idx in range(tot + LAG):
                def step(idx=idx):
                    if idx < tot:
                        par, j, c0, diag = blocks[idx]
                        Kt = KEO[par]
                        Qt = (QE if par == 0 else QO)[qs]
                        hb = 2 * c + (1 - par)
                        si = s_cnt[0] % 3
                        pi = s_cnt[0] % 4
                        s_cnt[0] += 1
                        slots[idx] = pi
                        p0, n = kcols(j)
                        MM(SB[si][:, c0:512], Kt[:, p0:p0 + 128], Qt[:, c0:512], True, not diag,
                           [K_b, Q_b[qs]], [SB_b[si]])
                        if diag:
                            MM(SB[si][:, c0:c0 + 128], identb[:], maskb[:], False, True,
                               [identb_b, maskb_b], [SB_b[si]])
                        ACT(PT[pi][:, c0:512], SB[si][:, c0:512], AF.Exp, [SB_b[si], negc_b], [PT_b[pi]],
                            bias=negc[:, 8 * j + hb:8 * j + hb + 1], scale=0.125)
                    if idx >= LAG:
                        par, j, c0, diag = blocks[idx - LAG]
                        O, O_b = OB[par], OB_b[par]
                        pi = slots[idx - LAG]
                        MM(O[:, c0:512], vaug[:, j, 128 * par:128 * par + 128], PT[pi][:, c0:512],
                           j == 0, j == nblk - 1, [vaug_b, PT_b[pi]], [O_b])
                        if j == nblk - 1:
                            OP("dve", "tensor_copy", [O_b], [Oc_b[par]], out=Oc[par], in_=O[:, :])
                steps.append(step)
            return steps

        def post(c, G, qs):
            OP("dve", "tensor_tensor", [Oc_b[0]], [sq_b], out=sqE, in0=OcE, in1=OcE, op=ALU.mult)
            OP("dve", "tensor_tensor", [Oc_b[1]], [sq_b], out=sqO, in0=OcO, in1=OcO, op=ALU.mult)
            yield
            for h0 in (0, 256):
                MM(X[:, h0:h0 + 256], cst[:, C_WE:C_WE + 128], sqE[:, h0:h0 + 256], True, False, [cst_b, sq_b], [X_b])
                yield
                MM(X[:, h0:h0 + 256], cst[:, C_WO:C_WO + 128], sqO[:, h0:h0 + 256], False, True, [cst_b, sq_b], [X_b])
                yield
            ACT(lnr, X[:, :], AF.Ln, [X_b], [rstd_b])
            ACT(rstd, lnr, AF.Exp, [rstd_b], [rstd_b], scale=-0.5)
            OP("dve", "tensor_tensor", [Oc_b[0], rstd_b], [tmp_b], out=tmp[0:64, :], in0=OcE[0:64, :],
               in1=rstd[0:64, :], op=ALU.mult)
            OP("dve", "tensor_tensor", [Oc_b[1], rstd_b], [tmp_b], out=tmp[64:128, :], in0=OcO[64:128, :],
               in1=rstd[64:128, :], op=ALU.mult)
            OP("dve", "scalar_tensor_tensor", [tmp_b, sz_b[qs], small_b], [mixA_b[c][G]],
               out=mixA[:, c, 512 * G:512 * G + 512], in0=tmp, scalar=ag_t[:, c:c + 1], in1=sz[qs],
               op0=ALU.mult, op1=ALU.mult)

        seq = [(c, G) for c in range(4) for G in range(NG)]
        chunk_order = [4 * blk + i for i in range(4) for blk in (1, 2, 0, 3)]
        pre_chunks = chunk_order[0:8] if NG > 2 else []
        wcv_b = [Buf() for _ in range(16)]
        wcvc = {}
        for k_, n__ in enumerate(pre_chunks):
            flat = wbf[k_ // 4].rearrange("p k e -> p (k e)")
            wcvc[n__] = flat[:, 1024 * (k_ % 4):1024 * (k_ % 4) + 1024].rearrange("p (k e) -> p k e", e=128)
        pending_post = None
        qs_next = run_all(prologue(0, 0))
        for n_, (c, G) in enumerate(seq):
            qs = qs_next
            if G == 0:
                if c > 0:
                    kv_inproj(c)
                if c + 1 < 4:
                    load_pair_w(c + 1, defer=True)
                if c == 1:
                    for w_ in range(8):
                        load_w(wob[:, :, 128 * w_:128 * w_ + 128], wob_b[w_], wout_v[:, :, 128 * w_:128 * w_ + 128],
                               defer=True)
            if c == 3 and G == 0:
                for n__ in pre_chunks[0:4]:
                    load_w(wcvc[n__], wcv_b[n__], win_v[:, :, 2056 + 128 * n__:2056 + 128 * n__ + 128],
                           extra_w=wbf_b[0], defer=True)
            if c == 3 and G == NG - 1:
                for n__ in pre_chunks[4:8]:
                    load_w(wcvc[n__], wcv_b[n__], win_v[:, :, 2056 + 128 * n__:2056 + 128 * n__ + 128],
                           extra_w=wbf_b[1], defer=True)
            wtask_step(2)
            steps = make_steps(c, G, qs)
            hook_post = min(10, 4 * G + 1)
            k_guard = 4 * G + 6
            if G == NG - 1 and n_ + 1 < len(seq):
                wtask_flush()
            gen = prologue(*seq[n_ + 1]) if n_ + 1 < len(seq) else None
            post_gen = post(*pending_post) if pending_post is not None else None
            pending_post = None
            for k_, st in enumerate(steps):
                if k_ == k_guard and post_gen is not None:
                    run_all(post_gen)
                    post_gen = None
                st()
                if k_ % 16 == 12:
                    wtask_step(1)
                if post_gen is not None:
                    if k_ >= hook_post:
                        try:
                            next(post_gen)
                        except StopIteration:
                            post_gen = None
                elif gen is not None and k_ > hook_post:
                    try:
                        qs_next = next(gen)
                    except StopIteration:
                        gen = None
            if post_gen is not None:
                run_all(post_gen)
            if gen is not None:
                r_ = run_all(gen)
                if r_ is not None:
                    qs_next = r_
            pending_post = (c, G, qs)
        run_all(post(*pending_post))
        wtask_flush()
        wtask_flush()
        S.barrier()

        ar.reset()
        wob2 = ar.alloc(KC * D, BF16)
        wst = [ar.alloc(KC * 128, F32).rearrange("p (k e) -> p k e", e=128) for _ in range(3)]
        wst_b = [Buf() for _ in range(3)]
        ar.top = UW - 2 * (KC * 512 // 2)
        for n__ in chunk_order:
            if n__ not in wcvc:
                wcvc[n__] = ar.alloc(KC * 128, BF16).rearrange("p (k e) -> p k e", e=128)
        xt = [ar.alloc(D, F32) for _ in range(3)]; xt_b = [Buf() for _ in range(3)]
        junk = ar.alloc(D, BF16); junk_b = Buf()
        fgrep = ar.alloc(D, F32); fgrep_b = Buf()
        ymix = [ar.alloc(4 * 512, BF16).rearrange("p (i t) -> p i t", t=512) for _ in range(2)]
        ymix_b = [Buf(), Buf()]
        mc = ar.alloc(4 * 16, F32).rearrange("p (i t) -> p i t", t=16); mc_b = Buf()
        NSL = 2
        b1 = [ar.alloc(512, F32) for _ in range(NSL)]; b1_b = [Buf() for _ in range(NSL)]
        b2 = [ar.alloc(514, F32) for _ in range(NSL)]; b2_b = [Buf() for _ in range(NSL)]
        b3 = [ar.alloc(512, F32) for _ in range(NSL)]; b3_b = [Buf() for _ in range(NSL)]
        b4 = [ar.alloc(512, F32) for _ in range(NSL)]; b4_b = [Buf() for _ in range(NSL)]

        DMA("fgrep", fgrep, fg_d.partition_broadcast(128), [], [fgrep_b])
        cast_cnt = 0
        for n_ in chunk_order:
            if n_ in pre_chunks:
                continue
            sl = cast_cnt % 3
            DMA("wst%d" % sl, wst[sl][:, :, :], win_v[:, :, 2056 + 128 * n_:2056 + 128 * n_ + 128], [], [wst_b[sl]])
            if cast_cnt % 2 == 0:
                OP("dve", "tensor_copy", [wst_b[sl]], [wcv_b[n_]], out=wcvc[n_], in_=wst[sl][:, :, :])
            else:
                ACT(wcvc[n_], wst[sl][:, :, :], AF.Copy, [wst_b[sl]], [wcv_b[n_]])
            cast_cnt += 1

        A5 = [A[0], A[1], SB[0], SB[1], SB[2]]; A5_b = [A_b[0], A_b[1], SB_b[0], SB_b[1], SB_b[2]]

        def next_A5():
            i = a_cnt[0] % 5
            a_cnt[0] += 1
            return A5[i], A5_b[i]

        def conv_w(k, i):
            return cw_t[:, 4 * k + i:4 * k + i + 1]

        def meta_halo(i):
            a, a_b = next_A5()
            for kc in range(KC):
                MM(a[:, 0:16], wcvc[4 + i][:, kc, :], uT[:, kc, 0:16], kc == 0, kc == KC - 1,
                   [wcv_b[4 + i], uT_t[0]], [a_b])
            ACT(mc[:, i, :], a[:, 0:16], AF.Copy, [a_b], [mc_b])
            a2, a2_b = next_A5()
            for kc in range(KC):
                MM(a2[:, 0:16], wcvc[8 + i][:, kc, :], uT[:, kc, 0:16], kc == 0, kc == KC - 1,
                   [wcv_b[8 + i], uT_t[0]], [a2_b])
            OP("dve", "tensor_tensor", [a2_b, mc_b], [halo_b[i]], out=halo[:, i, :], in0=a2[:, 14:16],
               in1=mc[:, i, 14:16], op=ALU.mult)

        chain_cnt = [0]

        def phase1(G, i):
            q0 = 16 + 512 * G
            sl = chain_cnt[0] % NSL
            chain_cnt[0] += 1

            def inproj(blk):
                a, a_b = next_A5()
                ncol = 128 * (4 * blk + i)
                for kc in range(KC):
                    MM(a[:, :], wcvc[4 * blk + i][:, kc, :], uT[:, kc, q0:q0 + 512], kc == 0, kc == KC - 1,
                       [wcv_b[4 * blk + i]] + ub(q0, 512), [a_b])
                return a, a_b
            if G == 0:
                meta_halo(i)
            aC, aC_b = inproj(1)
            ACT(b1[sl], aC[:, :], AF.Copy, [aC_b], [b1_b[sl]])
            aX, aX_b = inproj(2)
            OP("pool", "tensor_copy", [halo_b[i]], [b2_b[sl]], out=b2[sl][:, 0:2], in_=halo[:, i, :])
            OP("dve", "tensor_tensor", [aX_b, b1_b[sl]], [b2_b[sl]], out=b2[sl][:, 2:514], in0=aX[:, :], in1=b1[sl],
               op=ALU.mult)
            OP("pool", "tensor_copy", [b2_b[sl]], [halo_b[i]], out=halo[:, i, :], in_=b2[sl][:, 512:514])
            ACT(b3[sl], b2[sl][:, 2:514], AF.Copy, [b2_b[sl], small_b], [b3_b[sl]], scale=conv_w(2, i))
            OP("dve", "scalar_tensor_tensor", [b2_b[sl], b3_b[sl], small_b], [b3_b[sl]], out=b3[sl], in0=b2[sl][:, 1:513],
               scalar=conv_w(1, i), in1=b3[sl], op0=ALU.mult, op1=ALU.add)
            OP("dve", "scalar_tensor_tensor", [b2_b[sl], b3_b[sl], small_b], [b3_b[sl]], out=b3[sl], in0=b2[sl][:, 0:512],
               scalar=conv_w(0, i), in1=b3[sl], op0=ALU.mult, op1=ALU.add)
            aB, aB_b = inproj(0)
            OP("dve", "tensor_tensor", [aB_b, b3_b[sl]], [b3_b[sl]], out=b3[sl], in0=aB[:, :], in1=b3[sl], op=ALU.mult)
            ACT(b1[sl], b3[sl], AF.Square, [b3_b[sl]], [b1_b[sl]])
            aZ, aZ_b = inproj(3)
            ACT(b4[sl], aZ[:, :], AF.Exp, [aZ_b], [b4_b[sl]], scale=-1.0)
            ACT(b4[sl], b4[sl], AF.Ln, [b4_b[sl]], [b4_b[sl]], bias=1.0)
            ACT(b4[sl], b4[sl], AF.Exp, [b4_b[sl]], [b4_b[sl]], scale=-1.0)
            OP("dve", "tensor_tensor", [aZ_b, b4_b[sl]], [b4_b[sl]], out=b4[sl], in0=aZ[:, :], in1=b4[sl], op=ALU.mult)
            return (G, i, sl)

        def phase2(state):
            G, i, sl = state
            ys = G % 2
            MM(X[:, :], cst[:, C_WG:C_WG + 128], b1[sl], True, True, [cst_b, b1_b[sl]], [X_b])
            ACT(b2[sl][:, 0:512], X[:, :], AF.Ln, [X_b], [b2_b[sl]], bias=EPS)
            ACT(b2[sl][:, 0:512], b2[sl][:, 0:512], AF.Exp, [b2_b[sl]], [b2_b[sl]], scale=-0.5)
            OP("dve", "tensor_tensor", [b3_b[sl], b2_b[sl]], [b3_b[sl]], out=b3[sl], in0=b3[sl], in1=b2[sl][:, 0:512],
               op=ALU.mult)
            OP("dve", "scalar_tensor_tensor", [b3_b[sl], b4_b[sl], small_b], [ymix_b[ys]], out=ymix[ys][:, i, :],
               in0=b3[sl], scalar=cg_t[:, i:i + 1], in1=b4[sl], op0=ALU.mult, op1=ALU.mult)

        tt_cnt = [0]

        def outproj(G, tt):
            r0 = 512 * G + 128 * tt
            ys = G % 2
            s3 = tt_cnt[0] % 3
            st_slot = tt_cnt[0] % 2
            tt_cnt[0] += 1
            DMA("xt%d" % s3, xt[s3], x_d[r0:r0 + 128, :], [], [xt_b[s3]])
            for half in range(2):
                pb, pb_b = OB[half], OB_b[half]
                for e_ in range(8):
                    if e_ < 4:
                        lh, lh_b = mixA[:, e_, r0:r0 + 128], mixA_b[e_][G]
                    else:
                        lh, lh_b = ymix[ys][:, e_ - 4, 128 * tt:128 * tt + 128], ymix_b[ys]
                    MM(pb[:, :], lh, wob[:, e_, 512 * half:512 * half + 512], e_ == 0, e_ == 7,
                       [lh_b] + wob_b, [pb_b])
                OP("dve", "tensor_tensor", [pb_b, xt_b[s3]], [xt_b[s3]], out=xt[s3][:, 512 * half:512 * half + 512],
                   in0=pb[:, :], in1=xt[s3][:, 512 * half:512 * half + 512], op=ALU.add)
            st, st_b = stat[:, 4 * st_slot:4 * st_slot + 4], stat_b[st_slot]
            ACT(junk, xt[s3], AF.Square, [xt_b[s3]], [junk_b, st_b], accum_out=st[:, 0:1])
            ACT(st[:, 1:2], st[:, 0:1], AF.Ln, [st_b], [st_b], scale=1.0 / D, bias=EPS)
            ACT(st[:, 2:3], st[:, 1:2], AF.Exp, [st_b], [st_b], scale=-0.5)
            OP("dve", "scalar_tensor_tensor", [xt_b[s3], st_b, fgrep_b], [xt_b[s3]], out=xt[s3], in0=xt[s3],
               scalar=st[:, 2:3], in1=fgrep, op0=ALU.mult, op1=ALU.mult)
            o = DMA("y%d" % s3, y_d[r0:r0 + 128, :], xt[s3], [xt_b[s3]], [])
            o.final = True

        prev = None
        for G in range(NG):
            for i in range(4):
                st_ = phase1(G, i)
                if prev is not None:
                    phase2(prev)
                prev = st_
                if G > 0:
                    outproj(G - 1, i)
        phase2(prev)
        for tt in range(4):
            outproj(NG - 1, tt)
        S.emit(nc, es)
    return nc


_CACHE = {}


def _host_inputs(NB, meta, norm_g, w_in, b_f, conv_w, attn_norm_g, conv_norm_g, w_out, final_norm_g):
    f32 = np.float32
    w = np.array(w_in[0], dtype=f32, copy=True)
    swap = np.array([1, 0, 3, 2, 5, 4, 7, 6])
    w[:, 1536:1544] = w[:, 1536:1544][:, swap]
    bf = np.asarray(b_f[0], f32)[swap]
    shared = {
        "meta": np.ascontiguousarray(meta, f32),
        "norm_g": np.ascontiguousarray(norm_g[0:1], f32),
        "w_in": np.ascontiguousarray(w),
        "bf_rep": np.ascontiguousarray(np.tile(bf, NB + 1)[None, :], f32),
        "cwT": np.ascontiguousarray(np.asarray(conv_w[0], f32).reshape(3, 4, 128).transpose(2, 0, 1).reshape(128, 12)),
        "ag": np.ascontiguousarray(np.asarray(attn_norm_g[0], f32).reshape(4, 128).T),
        "cg": np.ascontiguousarray(np.asarray(conv_norm_g[0], f32).reshape(4, 128).T),
        "w_out": np.ascontiguousarray(w_out[0], f32),
        "fg": np.ascontiguousarray(np.asarray(final_norm_g, f32)[None, :]),
        "cst": make_cst(),
        "ones_bf": np.full((1, (16 + 128 * NB) // 2), 0x3F803F80, dtype=np.uint32).view(np.float32),
    }
    return shared


def kernel(x, meta, norm_g, w_in, b_f, conv_w, attn_norm_g, conv_norm_g, w_out, final_norm_g):
    x = np.asarray(x, np.float32)
    B, SEQ, _ = x.shape
    NB = SEQ // 128
    if NB not in _CACHE:
        _CACHE[NB] = build_program(NB)
    nc = _CACHE[NB]
    shared = _host_inputs(NB, meta, norm_g, w_in, b_f, conv_w, attn_norm_g, conv_norm_g, w_out, final_norm_g)
    in_maps = [dict(shared, x=np.ascontiguousarray(x[b])) for b in range(B)]
    res = run_bass_kernel_spmd(nc, in_maps, core_ids=list(range(B)))
    return np.stack([np.asarray(res.results[b]["y"], np.float32) for b in range(B)], axis=0)
```

```python
import numpy as np
from contextlib import ExitStack
import concourse.bass as bass
import concourse.mybir as mybir
from concourse.bass_utils import run_bass_kernel_spmd

F32 = mybir.dt.float32
BF16 = mybir.dt.bfloat16
AF = mybir.ActivationFunctionType
ALU = mybir.AluOpType

D = 1024
KC = 8
DIN = 4104
EPS = 1e-6
NEG = -30000.0
C_ID, C_TRI, C_SEL, C_MASK, C_WE, C_WO, C_WG, C_COLS = 0, 128, 256, 384, 512, 640, 768, 896
NCST = 904


class Buf:
    __slots__ = ("name", "last_w", "readers")

    def __init__(self, name=""):
        self.name = name
        self.last_w = None
        self.readers = []


class Op:
    __slots__ = ("eng", "fn", "deps", "signal", "count", "dma_key", "sem", "final", "idx")

    def __init__(self, eng, fn, dma_key=None):
        self.eng = eng
        self.fn = fn
        self.deps = []
        self.signal = False
        self.count = None
        self.dma_key = dma_key
        self.sem = None
        self.final = False


class Sched:
    ENGS = ("pe", "act", "dve", "pool", "sp")

    def __init__(self):
        self.ops = []
        self.last = {}
        self.pending_barrier = {}

    def op(self, eng, fn, reads=(), writes=(), dma_key=None):
        o = Op(eng, fn, dma_key)
        deps = {}
        for b in reads:
            if b.last_w is not None:
                deps[id(b.last_w)] = b.last_w
        for b in writes:
            if b.last_w is not None:
                deps[id(b.last_w)] = b.last_w
            for r in b.readers:
                deps[id(r)] = r
        if eng in self.pending_barrier:
            for d in self.pending_barrier.pop(eng):
                deps[id(d)] = d
        best = {}
        for d in deps.values():
            k = ("dma", d.dma_key) if d.dma_key is not None else d.eng
            if k not in best or best[k].idx < d.idx:
                best[k] = d
        o.deps = list(best.values())
        o.idx = len(self.ops)
        for b in reads:
            b.readers.append(o)
        for b in writes:
            b.last_w = o
            b.readers = []
        self.ops.append(o)
        self.last[("dma", dma_key) if dma_key is not None else eng] = o
        return o

    def barrier(self):
        lasts = list(self.last.values())
        for e in self.ENGS:
            self.pending_barrier[e] = list(self.pending_barrier.get(e, [])) + lasts

    def emit(self, nc, es):
        ops = self.ops
        for o in ops:
            for d in o.deps:
                if d.dma_key is not None:
                    continue
                if d.eng == "pe" and o.eng == "pe" and o.dma_key is None:
                    continue
                d.signal = True
        eng_sem = {e: es.enter_context(nc.semaphore("s_" + e)) for e in ("pe", "act", "dve", "pool")}
        cnt = {e: 0 for e in eng_sem}
        dma_sems, dma_cnt = {}, {}
        for o in ops:
            if o.dma_key is not None:
                if o.dma_key not in dma_sems:
                    dma_sems[o.dma_key] = es.enter_context(nc.semaphore("d_" + str(o.dma_key)))
                    dma_cnt[o.dma_key] = 0
                dma_cnt[o.dma_key] += 16
                o.sem, o.count = dma_sems[o.dma_key], dma_cnt[o.dma_key]
            elif o.signal:
                cnt[o.eng] += 1
                o.sem, o.count = eng_sem[o.eng], cnt[o.eng]
        streams = {e: [o for o in ops if o.eng == e] for e in self.ENGS}
        final = [o for o in ops if o.final]

        def run(ename, eng):
            known = {}
            for o in streams[ename]:
                for d in o.deps:
                    if d.sem is None:
                        continue
                    if d.eng == "pe" and ename == "pe" and d.dma_key is None and o.dma_key is None:
                        continue
                    k = id(d.sem)
                    if known.get(k, 0) >= d.count:
                        continue
                    eng.wait_ge(d.sem, d.count)
                    known[k] = d.count
                ins = o.fn(eng)
                if o.dma_key is not None:
                    ins.then_inc(o.sem, 16)
                elif o.signal:
                    ins.then_inc(o.sem, 1)
            if ename == "sp":
                for o in final:
                    eng.wait_ge(o.sem, o.count)

        with nc.Block() as block:
            @block.tensor
            def _(e):
                run("pe", e)

            @block.scalar
            def _(e):
                run("act", e)

            @block.vector
            def _(e):
                run("dve", e)

            @block.gpsimd
            def _(e):
                run("pool", e)

            @block.sync
            def _(e):
                run("sp", e)


class Arena:
    def __init__(self, ap, width):
        self.ap, self.width, self.off, self.top = ap, width, 0, width

    def reset(self):
        self.off, self.top = 0, self.width

    def alloc_top(self, n, dt):
        w = n if dt == F32 else (n + 1) // 2
        assert self.top - w >= self.off, ("arena overflow (top)", self.off, w, self.top)
        self.top -= w
        a = self.ap[:, self.top:self.top + w]
        if dt != F32:
            a = a.bitcast(dt)[:, 0:n]
        return a

    def alloc(self, n, dt):
        w = n if dt == F32 else (n + 1) // 2
        assert self.off + w <= self.top, ("arena overflow", self.off, w, self.top)
        a = self.ap[:, self.off:self.off + w]
        self.off += w
        if dt != F32:
            a = a.bitcast(dt)[:, 0:n]
        return a


def make_cst():
    c = np.zeros((128, NCST), np.float32)
    p = np.arange(128)
    c[:, C_ID:C_ID + 128] = np.eye(128)
    c[:, C_TRI:C_TRI + 128] = (p[:, None] <= p[None, :])
    c[127, C_SEL:C_SEL + 128] = 1.0
    c[:, C_MASK:C_MASK + 128] = np.where(p[:, None] > p[None, :], NEG, 0.0)
    c[0:64, C_WE:C_WE + 64] = 1.0 / 64
    c[64, C_WE:C_WE + 64] = EPS
    c[64:128, C_WO + 64:C_WO + 128] = 1.0 / 64
    c[63, C_WO + 64:C_WO + 128] = EPS
    c[0:64, C_WG:C_WG + 64] = 1.0 / 64
    c[64:128, C_WG + 64:C_WG + 128] = 1.0 / 64
    c[:, C_COLS + 0] = (p < 16)
    c[:, C_COLS + 1] = np.where(p < 16, 0.0, NEG)
    c[:, C_COLS + 2] = -1.0 * (p < 16)
    c[:, C_COLS + 3] = (p == 63)
    c[:, C_COLS + 4] = 1.0
    c[:, C_COLS + 5] = 0.0
    return c


def build_program(NB):
    NG = NB // 4
    T = 16 + 128 * NB
    SEQ = 128 * NB
    NT = NB + 1
    NF = NT * 8

    nc = bass.Bass("TRN2", target_bir_lowering=False)

    def dram(name, shape, kind="ExternalInput"):
        return nc.dram_tensor(name, shape, F32, kind=kind).ap()

    x_d = dram("x", [SEQ, D])
    meta_d = dram("meta", [16, D])
    ng_d = dram("norm_g", [1, D])
    win_d = dram("w_in", [D, DIN])
    bfr_d = dram("bf_rep", [1, NF])
    cw_d = dram("cwT", [128, 12])
    ag_d = dram("ag", [128, 4])
    cg_d = dram("cg", [128, 4])
    wout_d = dram("w_out", [D, D])
    fg_d = dram("fg", [1, D])
    cst_d = dram("cst", [128, NCST])
    ones_d = dram("ones_bf", [1, T // 2])
    y_d = dram("y", [SEQ, D], kind="ExternalOutput")

    win_v = win_d.rearrange("(kc p) e -> p kc e", p=128)
    wout_v = wout_d.rearrange("(kc p) e -> p kc e", p=128)

    S = Sched()

    def OP(eng, name, reads, writes, *args, **kw):
        return S.op(eng, lambda e: getattr(e, name)(*args, **kw), reads=reads, writes=writes)

    def DMA(key, out, in_, reads, writes):
        return S.op("sp", lambda e: e.dma_start(out=out, in_=in_), reads=reads, writes=writes, dma_key=key)

    def MM(out, lhsT, rhs, start, stop, reads, writes):
        return S.op("pe", lambda e: e.matmul(out, lhsT=lhsT, rhs=rhs, start=start, stop=stop),
                    reads=reads, writes=writes)

    def ACT(out, in_, func, reads, writes, **kw):
        return S.op("act", lambda e: e.activation(out=out, in_=in_, func=func, **kw), reads=reads, writes=writes)

    with ExitStack() as es:
        def sb(name, shape, dt):
            return es.enter_context(nc.sbuf_tensor("sb_" + name, shape, dt))

        cst = sb("cst", [128, NCST], F32); cst_b = Buf()
        identb = sb("identb", [128, 128], BF16); identb_b = Buf()
        maskb = sb("maskb", [128, 128], BF16); maskb_b = Buf()
        uT = sb("uT", [128, KC, T], BF16)
        uT_t = [Buf() for _ in range(NB + 1)]

        def ub(p0, n):
            out = []
            if p0 < 16:
                out.append(uT_t[0])
            lo = max(p0, 16)
            hi = p0 + n
            if hi > lo:
                out += uT_t[1 + (lo - 16) // 128:1 + (hi - 16 + 127) // 128]
            return out
        mixA = sb("mixA", [128, 4, SEQ], BF16); mixA_b = [[Buf() for _ in range(NG)] for _ in range(4)]
        c_all = sb("c_all", [128, NF], F32); c_all_b = Buf()
        negc = sb("negc", [128, NF], F32); negc_b = Buf()
        cw_t = sb("cw_t", [128, 12], F32); ag_t = sb("ag_t", [128, 4], F32); cg_t = sb("cg_t", [128, 4], F32)
        small_b = Buf()
        halo = sb("halo", [128, 4, 2], F32); halo_b = [Buf() for _ in range(4)]
        stat = sb("stat", [128, 8], F32)
        stat_b = [Buf(), Buf()]
        UW = 26900
        U = sb("U", [128, UW], F32)
        ar = Arena(U[:, :], UW)

        def ps(name):
            return es.enter_context(nc.psum_tensor("ps_" + name, [128, 512], F32))
        A = [ps("A0"), ps("A1")]; A_b = [Buf(), Buf()]
        SB = [ps("S0"), ps("S1"), ps("S2")]; SB_b = [Buf(), Buf(), Buf()]
        OB = [ps("O0"), ps("O1")]; OB_b = [Buf(), Buf()]
        X = ps("X"); X_b = Buf()

        ident_f = cst[:, C_ID:C_ID + 128]
        col = lambda i: cst[:, C_COLS + i:C_COLS + i + 1]

        DMA("cst", cst[:], cst_d[:, :], [], [cst_b])
        DMA("small", cw_t[:], cw_d[:, :], [], [small_b])
        DMA("small", ag_t[:], ag_d[:, :], [], [small_b])
        DMA("small", cg_t[:], cg_d[:, :], [], [small_b])
        OP("dve", "tensor_copy", [cst_b], [identb_b], out=identb[:], in_=ident_f)
        OP("dve", "tensor_copy", [cst_b], [maskb_b], out=maskb[:], in_=cst[:, C_MASK:C_MASK + 128])

        ar.reset()
        wbf = [ar.alloc_top(KC * 512, BF16).rearrange("p (k e) -> p k e", e=512) for _ in range(2)]
        wbf_b = [[Buf() for _ in range(4)] for _ in range(2)]
        KEKO = ar.alloc_top(2 * T, BF16)
        KE, KO = KEKO[:, 0:T], KEKO[:, T:2 * T]; K_b = Buf()
        KEO = [KE, KO]
        vaug = ar.alloc_top(NT * 256, BF16).rearrange("p (j c) -> p j c", c=256); vaug_b = Buf()
        CP = [ar.alloc_top(4 * 128, BF16).rearrange("p (t m) -> p t m", m=128) for _ in range(2)]
        CP_b = [Buf(), Buf()]
        wst = [ar.alloc_top(KC * 128, F32).rearrange("p (k e) -> p k e", e=128) for _ in range(2)]
        wst_b = [Buf(), Buf()]
        wob = ar.alloc(KC * D, BF16).rearrange("p (k e) -> p k e", e=D)
        wob_b = [Buf() for _ in range(8)]
        wfb = ar.alloc(KC * 8, BF16).rearrange("p (k e) -> p k e", e=8); wfb_b = Buf()
        bfr = ar.alloc(NF, F32); bfr_b = Buf()
        fblk = ar.alloc(max(4 * NF, D), F32)
        fx, fa, fm, car = [fblk[:, k_ * NF:(k_ + 1) * NF] for k_ in range(4)]
        f_b = Buf()
        shared_off = ar.off

        zc, oc_ = col(5), col(4)
        vflat = vaug.rearrange("p j c -> p (j c)")
        init_ops = [
            lambda: ACT(KEKO.bitcast(F32), zc.to_broadcast([128, T]), AF.Copy, [cst_b], [K_b]),
            lambda: DMA("ones", KE[64:65, :].bitcast(F32), ones_d[:, :], [K_b], [K_b]),
            lambda: DMA("ones", KO[63:64, :].bitcast(F32), ones_d[:, :], [K_b], [K_b]),
            lambda: ACT(vflat.bitcast(F32), zc.to_broadcast([128, NT * 128]), AF.Copy, [cst_b], [vaug_b]),
            lambda: ACT(vaug[:, :, 64:65], oc_.to_broadcast([128, NT]).rearrange("p (j o) -> p j o", o=1), AF.Copy,
                        [cst_b], [vaug_b]),
            lambda: ACT(vaug[:, :, 191:192], oc_.to_broadcast([128, NT]).rearrange("p (j o) -> p j o", o=1), AF.Copy,
                        [cst_b], [vaug_b]),
            lambda: ACT(CP[0].rearrange("p t m -> p (t m)").bitcast(F32), zc.to_broadcast([128, 256]), AF.Copy,
                        [cst_b], [CP_b[0]]),
            lambda: ACT(CP[1].rearrange("p t m -> p (t m)").bitcast(F32), zc.to_broadcast([128, 256]), AF.Copy,
                        [cst_b], [CP_b[1]]),
        ]
        vaug4 = vaug.rearrange("p j (b d) -> p j b d", d=64)

        wst_cnt = [0]

        wtasks = []
        wpending = []

        def load_w(dst, dst_b, src, extra_w=(), defer=False):
            if defer:
                wtasks.append((dst, dst_b, src, extra_w))
                return
            s = wst_cnt[0] % 2
            wst_cnt[0] += 1
            DMA("wst%d" % s, wst[s][:, :, :], src, [], [wst_b[s]])
            OP("dve", "tensor_copy", [wst_b[s]], [dst_b] + list(extra_w), out=dst, in_=wst[s][:, :, :])

        def wtask_step(n=1):
            while wpending:
                s_, dst, dst_b, extra_w = wpending.pop(0)
                OP("dve", "tensor_copy", [wst_b[s_]], [dst_b] + list(extra_w), out=dst, in_=wst[s_][:, :, :])
            for _ in range(n):
                if not wtasks:
                    break
                dst, dst_b, src, extra_w = wtasks.pop(0)
                s_ = wst_cnt[0] % 2
                wst_cnt[0] += 1
                DMA("wst%d" % s_, wst[s_][:, :, :], src, [], [wst_b[s_]])
                wpending.append((s_, dst, dst_b, extra_w))

        def wtask_flush():
            while wtasks or wpending:
                wtask_step(2)

        def load_pair_w(c, defer=False):
            s = c % 2
            for i, c0 in enumerate((128 * c, 512 + 128 * c, 1024 + 128 * c, 1544 + 128 * c)):
                load_w(wbf[s][:, :, 128 * i:128 * i + 128], wbf_b[s][i], win_v[:, :, c0:c0 + 128], defer=defer)

        NXS = 4
        xt = [ar.alloc(D, F32) for _ in range(3)] + [fblk[:, 0:D]]; xt_b = [Buf() for _ in range(NXS)]
        xb = [ar.alloc(D, BF16) for _ in range(2)]; xb_b = [Buf(), Buf()]
        junk = ar.alloc(D, BF16); junk_b = Buf()
        grep = ar.alloc(D, F32); grep_b = Buf()
        DMA("grep", grep, ng_d.partition_broadcast(128), [], [grep_b])
        Xb16 = X[:].bitcast(BF16)

        def tile_stats(src, src_b, slot):
            st, st_b = stat[:, 4 * slot:4 * slot + 4], stat_b[slot]
            ACT(junk, src, AF.Square, [src_b], [junk_b, st_b], accum_out=st[:, 0:1])
            ACT(st[:, 1:2], st[:, 0:1], AF.Ln, [st_b], [st_b], scale=1.0 / D, bias=EPS)
            ACT(st[:, 2:3], st[:, 1:2], AF.Exp, [st_b], [st_b], scale=-0.5)
            return st[:, 2:3], st_b

        def kcols(j):
            return (0, 128) if j == 0 else (16 + 128 * (j - 1), 128)

        a_cnt = [0]

        def next_A():
            i = a_cnt[0] % 2
            a_cnt[0] += 1
            return A[i], A_b[i]

        def pair_w(c):
            ws = c % 2
            return [wbf[ws][:, :, 128 * i:128 * i + 128] for i in range(4)], wbf_b[ws]

        def kv_inproj_part(c, g):
            (wq, wk, wv, wz), (wq_b, wk_b, wv_b, wz_b) = pair_w(c)
            ranges = ([(0, 16)] if g == 0 else []) + [(16 + 512 * g, 512)]
            for (p0, n) in ranges:
                a, a_b = next_A()
                for kc in range(KC):
                    MM(a[:, 0:n], wk[:, kc, :], uT[:, kc, p0:p0 + n], kc == 0, kc == KC - 1, [wk_b] + ub(p0, n), [a_b])
                OP("dve", "tensor_copy", [a_b], [K_b], out=KE[0:64, p0:p0 + n], in_=a[0:64, 0:n])
                OP("dve", "tensor_copy", [a_b], [K_b], out=KO[64:128, p0:p0 + n], in_=a[64:128, 0:n])
            tiles = ([0] if g == 0 else []) + list(range(4 * g + 1, 4 * g + 5))
            for jb in range(0, len(tiles), 4):
                tl = tiles[jb:jb + 4]
                nt = len(tl)
                a, a_b = next_A()
                for t, j in enumerate(tl):
                    p0, n = kcols(j)
                    for kc in range(KC):
                        MM(a[:, 128 * t:128 * t + 128], uT[:, kc, p0:p0 + n], wv[:, kc, :], kc == 0, kc == KC - 1,
                           [wv_b] + ub(p0, n), [a_b])
                OP("dve", "tensor_copy", [a_b], [vaug_b], out=vaug4[:, tl[0]:tl[0] + nt, 0:4:3, :],
                   in_=a[:, 0:128 * nt].rearrange("p (t b d) -> p t b d", b=2, d=64))

        def kv_inproj(c):
            for g in range(NG):
                kv_inproj_part(c, g)

        TB = [(X, X_b), (SB[0], SB_b[0])]
        TB16 = [X[:].bitcast(BF16), SB[0][:].bitcast(BF16)]

        def a_load(ti):
            s3 = ti % NXS
            if ti == 0:
                OP("dve", "memset", [], [xt_b[s3]], xt[s3], 0.0)
                DMA("xt%d" % s3, xt[s3][0:16, :], meta_d[:, :], [], [xt_b[s3]])
            else:
                DMA("xt%d" % s3, xt[s3], x_d[(ti - 1) * 128:ti * 128, :], [], [xt_b[s3]])

        def a_norm(ti):
            s3, s2 = ti % NXS, ti % 2
            rs, rs_b = tile_stats(xt[s3], xt_b[s3], s2)
            OP("dve", "scalar_tensor_tensor", [xt_b[s3], rs_b, grep_b], [xb_b[s2]],
               out=xb[s2], in0=xt[s3], scalar=rs, in1=grep, op0=ALU.mult, op1=ALU.mult)

        for ti in range(min(NXS, NT)):
            a_load(ti)
        DMA("wf", wst[0][:, :, 0:8], win_v[:, :, 1536:1544], [], [wst_b[0]])
        DMA("bfr", bfr, bfr_d.partition_broadcast(128), [], [bfr_b])
        OP("dve", "tensor_copy", [wst_b[0]], [wfb_b], out=wfb, in_=wst[0][:, :, 0:8])
        a_norm(0)
        for ti in range(NT):
            s2 = ti % 2
            tb, tb_b = TB[s2]
            tb16 = TB16[s2]
            for kc in range(KC):
                S.op("pe", lambda e, kc=kc, s2=s2, tb16=tb16: e.transpose(out=tb16[:, kc * 128:(kc + 1) * 128],
                                                                           in_=xb[s2][:, kc * 128:(kc + 1) * 128],
                                                                           identity=identb[:]),
                     reads=[xb_b[s2], identb_b], writes=[tb_b])
            if ti + 1 < NT:
                a_norm(ti + 1)
            if ti >= 1 and init_ops:
                init_ops.pop(0)()
            if ti == 0:
                OP("dve", "tensor_copy", [tb_b], [uT_t[0]], out=uT[:, :, 0:16],
                   in_=tb16.rearrange("p (k t) -> p k t", t=128)[:, :, 0:16])
            else:
                p0 = 16 + 128 * (ti - 1)
                OP("dve", "tensor_copy", [tb_b], [uT_t[ti]], out=uT[:, :, p0:p0 + 128],
                   in_=tb16.rearrange("p (k t) -> p k t", t=128))
            if ti + NXS < NT:
                a_load(ti + NXS)
            if ti == 1:
                load_pair_w(0)
            if ti >= 4 and ti % 4 == 0:
                kv_inproj_part(0, ti // 4 - 1)

        while init_ops:
            init_ops.pop(0)()

        for j in range(NT):
            p0, n = kcols(j)
            for kc in range(KC):
                MM(X[:, 8 * j:8 * j + 8], uT[:, kc, p0:p0 + n], wfb[:, kc, :], kc == 0, kc == KC - 1,
                   ub(p0, n) + [wfb_b], [X_b])
        OP("dve", "tensor_tensor", [X_b, bfr_b], [f_b], out=fx, in0=X[:, 0:NF], in1=bfr, op=ALU.add)
        ACT(fa, fx, AF.Abs, [f_b], [f_b])
        ACT(fa, fa, AF.Exp, [f_b], [f_b], scale=-1.0)
        ACT(fa, fa, AF.Ln, [f_b], [f_b], bias=1.0)
        OP("dve", "tensor_scalar_min", [f_b], [f_b], out=fm, in0=fx, scalar1=0.0)
        OP("dve", "tensor_tensor", [f_b], [f_b], out=fm, in0=fm, in1=fa, op=ALU.subtract)
        OP("dve", "tensor_scalar", [f_b, cst_b], [f_b], out=fm[:, 0:8], in0=fm[:, 0:8], scalar1=col(0),
           scalar2=None, op0=ALU.mult)
        cw_sb, tot_sb = fx, fa
        MM(X[:, 0:NF], cst[:, C_TRI:C_TRI + 128], fm, True, True, [f_b, cst_b], [X_b])
        OP("dve", "tensor_copy", [X_b], [f_b], out=cw_sb, in_=X[:, 0:NF])
        MM(X[:, 0:NF], cst[:, C_SEL:C_SEL + 128], cw_sb, True, True, [f_b, cst_b], [X_b])
        OP("dve", "tensor_copy", [X_b], [f_b], out=tot_sb, in_=X[:, 0:NF])
        OP("dve", "memset", [], [f_b], car[:, 0:8], 0.0)
        for j in range(1, NT):
            OP("dve", "tensor_tensor", [f_b], [f_b], out=car[:, 8 * j:8 * j + 8], in0=car[:, 8 * j - 8:8 * j],
               in1=tot_sb[:, 8 * j - 8:8 * j], op=ALU.add)
        OP("dve", "tensor_tensor", [f_b], [c_all_b], out=c_all[:], in0=cw_sb, in1=car, op=ALU.add)
        OP("dve", "tensor_scalar", [c_all_b], [negc_b], out=negc[:, 8:NF], in0=c_all[:, 8:NF], scalar1=-1.0,
           scalar2=None, op0=ALU.mult)
        OP("dve", "tensor_scalar", [c_all_b, cst_b], [negc_b], out=negc[:, 0:8], in0=c_all[:, 0:8],
           scalar1=col(2), scalar2=col(1), op0=ALU.mult, op1=ALU.add)

        ar.off = shared_off
        QE = [ar.alloc(512, BF16) for _ in range(2)]; QO = [ar.alloc(512, BF16) for _ in range(2)]
        Q_b = [Buf(), Buf()]
        PT = [ar.alloc(512, BF16) for _ in range(4)]; PT_b = [Buf() for _ in range(4)]
        ez = ar.alloc(512, F32); ez_b = Buf()
        sz = [ar.alloc(512, F32) for _ in range(2)]; sz_b = [Buf(), Buf()]
        sqE = ar.alloc(512, F32); sqO = ar.alloc(512, F32); sq_b = Buf()
        rstd = ar.alloc(512, F32); rstd_b = Buf()
        lnr = rstd
        tmp, tmp_b = ez, ez_b

        kr = [(0, 16)] + [(16 + 512 * g, 512) for g in range(NG)]
        s_cnt = [0]
        q_cnt = [0]
        OcE = ar.alloc(512, F32); OcO = ar.alloc(512, F32); Oc_b = [Buf(), Buf()]
        Oc = [OcE, OcO]

        def prologue(c, G):
            (wq, wk, wv, wz), (wq_b, wk_b, wv_b, wz_b) = pair_w(c)
            q0 = 16 + 512 * G
            qs = q_cnt[0] % 2
            q_cnt[0] += 1
            a, a_b = next_A()
            for kc in range(KC):
                MM(a[:, :], wq[:, kc, :], uT[:, kc, q0:q0 + 512], kc == 0, kc == KC - 1, [wq_b] + ub(q0, 512), [a_b])
                yield qs
            OP("dve", "tensor_copy", [a_b], [Q_b[qs]], out=QE[qs][0:64, :], in_=a[0:64, :])
            OP("dve", "tensor_copy", [a_b], [Q_b[qs]], out=QO[qs][64:128, :], in_=a[64:128, :])
            OP("dve", "tensor_scalar", [c_all_b], [CP_b[qs]], out=CP[qs][:, :, 63:65],
               in0=c_all[:, 8 * (4 * G + 1):8 * (4 * G + 5)].rearrange("p (t h) -> p t h", h=8)[:, :, 2 * c:2 * c + 2],
               scalar1=8.0, scalar2=None, op0=ALU.mult)
            for t in range(4):
                MM(X[:, 128 * t:128 * t + 128], CP[qs][:, t, :], identb[:], True, True, [CP_b[qs], identb_b], [X_b])
            yield qs
            OP("dve", "tensor_copy", [X_b], [Q_b[qs]], out=QE[qs][64:128, :], in_=X[64:128, :])
            OP("dve", "tensor_copy", [X_b], [Q_b[qs]], out=QO[qs][0:64, :], in_=X[0:64, :])
            a, a_b = next_A()
            for kc in range(KC):
                MM(a[:, :], wz[:, kc, :], uT[:, kc, q0:q0 + 512], kc == 0, kc == KC - 1, [wz_b] + ub(q0, 512), [a_b])
                yield qs
            ACT(ez, a[:, :], AF.Exp, [a_b], [ez_b], scale=-1.0)
            OP("dve", "tensor_scalar_add", [ez_b], [ez_b], out=ez, in0=ez, scalar1=1.0)
            OP("dve", "reciprocal", [ez_b], [ez_b], out=ez, in_=ez)
            OP("dve", "tensor_tensor", [ez_b, a_b], [sz_b[qs]], out=sz[qs], in0=a[:, :], in1=ez, op=ALU.mult)
            yield qs

        def run_all(gen):
            qs = None
            for qs in gen:
                pass
            return qs

        def make_steps(c, G, qs):
            steps = []
            LAG = 2
            nblk = 4 * G + 5
            blocks = []
            for par in range(2):
                for j in range(nblk):
                    r = j - (4 * G + 1)
                    blocks.append((par, j, 128 * r if r > 0 else 0, r >= 0))
            slots = {}
            tot = len(blocks)
            for idx in range(tot + LAG):
                def step(idx=idx):
                    if idx < tot:
                        par, j, c0, diag = blocks[idx]
                        Kt = KEO[par]
                        Qt = (QE if par == 0 else QO)[qs]
                        hb = 2 * c + (1 - par)
                        si = s_cnt[0] % 3
                        pi = s_cnt[0] % 4
                        s_cnt[0] += 1
                        slots[idx] = pi
                        p0, n = kcols(j)
                        MM(SB[si][:, c0:512], Kt[:, p0:p0 + 128], Qt[:, c0:512], True, not diag,
                           [K_b, Q_b[qs]], [SB_b[si]])
                        if diag:
                            MM(SB[si][:, c0:c0 + 128], identb[:], maskb[:], False, True,
                               [identb_b, maskb_b], [SB_b[si]])
                        ACT(PT[pi][:, c0:512], SB[si][:, c0:512], AF.Exp, [SB_b[si], negc_b], [PT_b[pi]],
                            bias=negc[:, 8 * j + hb:8 * j + hb + 1], scale=0.125)
                    if idx >= LAG:
                        par, j, c0, diag = blocks[idx - LAG]
                        O, O_b = OB[par], OB_b[par]
                        pi = slots[idx - LAG]
                        MM(O[:, c0:512], vaug[:, j, 128 * par:128 * par + 128], PT[pi][:, c0:512],
                           j == 0, j == nblk - 1, [vaug_b, PT_b[pi]], [O_b])
                        if j == nblk - 1:
                            OP("dve", "tensor_copy", [O_b], [Oc_b[par]], out=Oc[par], in_=O[:, :])
                steps.append(step)
            return steps

        def post(c, G, qs):
            OP("dve", "tensor_tensor", [Oc_b[0]], [sq_b], out=sqE, in0=OcE, in1=OcE, op=ALU.mult)
            OP("dve", "tensor_tensor", [Oc_b[1]], [sq_b], out=sqO, in0=OcO, in1=OcO, op=ALU.mult)
            yield
            for h0 in (0, 256):
                MM(X[:, h0:h0 + 256], cst[:, C_WE:C_WE + 128], sqE[:, h0:h0 + 256], True, False, [cst_b, sq_b], [X_b])
                yield
                MM(X[:, h0:h0 + 256], cst[:, C_WO:C_WO + 128], sqO[:, h0:h0 + 256], False, True, [cst_b, sq_b], [X_b])
                yield
            ACT(lnr, X[:, :], AF.Ln, [X_b], [rstd_b])
            ACT(rstd, lnr, AF.Exp, [rstd_b], [rstd_b], scale=-0.5)
            OP("dve", "tensor_tensor", [Oc_b[0], rstd_b], [tmp_b], out=tmp[0:64, :], in0=OcE[0:64, :],
               in1=rstd[0:64, :], op=ALU.mult)
            OP("dve", "tensor_tensor", [Oc_b[1], rstd_b], [tmp_b], out=tmp[64:128, :], in0=OcO[64:128, :],
               in1=rstd[64:128, :], op=ALU.mult)
            OP("dve", "scalar_tensor_tensor", [tmp_b, sz_b[qs], small_b], [mixA_b[c][G]],
               out=mixA[:, c, 512 * G:512 * G + 512], in0=tmp, scalar=ag_t[:, c:c + 1], in1=sz[qs],
               op0=ALU.mult, op1=ALU.mult)

        seq = [(c, G) for c in range(4) for G in range(NG)]
        chunk_order = [4 * blk + i for i in range(4) for blk in (1, 2, 0, 3)]
        pre_chunks = chunk_order[0:8] if NG > 2 else []
        wcv_b = [Buf() for _ in range(16)]
        wcvc = {}
        for k_, n__ in enumerate(pre_chunks):
            flat = wbf[k_ // 4].rearrange("p k e -> p (k e)")
            wcvc[n__] = flat[:, 1024 * (k_ % 4):1024 * (k_ % 4) + 1024].rearrange("p (k e) -> p k e", e=128)
        pending_post = None
        qs_next = run_all(prologue(0, 0))
        for n_, (c, G) in enumerate(seq):
            qs = qs_next
            if G == 0:
                if c > 0:
                    kv_inproj(c)
                if c + 1 < 4:
                    load_pair_w(c + 1, defer=True)
                if c == 1:
                    for w_ in range(8):
                        load_w(wob[:, :, 128 * w_:128 * w_ + 128], wob_b[w_], wout_v[:, :, 128 * w_:128 * w_ + 128],
                               defer=True)
            if c == 3 and G == 0:
                for n__ in pre_chunks[0:4]:
                    load_w(wcvc[n__], wcv_b[n__], win_v[:, :, 2056 + 128 * n__:2056 + 128 * n__ + 128],
                           extra_w=wbf_b[0], defer=True)
            if c == 3 and G == NG - 1:
                for n__ in pre_chunks[4:8]:
                    load_w(wcvc[n__], wcv_b[n__], win_v[:, :, 2056 + 128 * n__:2056 + 128 * n__ + 128],
                           extra_w=wbf_b[1], defer=True)
            wtask_step(2)
            steps = make_steps(c, G, qs)
            hook_post = min(10, 4 * G + 1)
            k_guard = 4 * G + 6
            if G == NG - 1 and n_ + 1 < len(seq):
                wtask_flush()
            gen = prologue(*seq[n_ + 1]) if n_ + 1 < len(seq) else None
            post_gen = post(*pending_post) if pending_post is not None else None
            pending_post = None
            for k_, st in enumerate(steps):
                if k_ == k_guard and post_gen is not None:
                    run_all(post_gen)
                    post_gen = None
                st()
                if k_ % 16 == 12:
                    wtask_step(1)
                if post_gen is not None:
                    if k_ >= hook_post:
                        try:
                            next(post_gen)
                        except StopIteration:
                            post_gen = None
                elif gen is not None and k_ > hook_post:
                    try:
                        qs_next = next(gen)
                    except StopIteration:
                        gen = None
            if post_gen is not None:
                run_all(post_gen)
            if gen is not None:
                r_ = run_all(gen)
                if r_ is not None:
                    qs_next = r_
            pending_post = (c, G, qs)
        run_all(post(*pending_post))
        wtask_flush()
        wtask_flush()
        S.barrier()

        ar.reset()
        wob2 = ar.alloc(KC * D, BF16)
        wst = [ar.alloc(KC * 128, F32).rearrange("p (k e) -> p k e", e=128) for _ in range(3)]
        wst_b = [Buf() for _ in range(3)]
        ar.top = UW - 2 * (KC * 512 // 2)
        for n__ in chunk_order:
            if n__ not in wcvc:
                wcvc[n__] = ar.alloc(KC * 128, BF16).rearrange("p (k e) -> p k e", e=128)
        xt = [ar.alloc(D, F32) for _ in range(3)]; xt_b = [Buf() for _ in range(3)]
        junk = ar.alloc(D, BF16); junk_b = Buf()
        fgrep = ar.alloc(D, F32); fgrep_b = Buf()
        ymix = [ar.alloc(4 * 512, BF16).rearrange("p (i t) -> p i t", t=512) for _ in range(2)]
        ymix_b = [Buf(), Buf()]
        mc = ar.alloc(4 * 16, F32).rearrange("p (i t) -> p i t", t=16); mc_b = Buf()
        NSL = 2
        b1 = [ar.alloc(512, F32) for _ in range(NSL)]; b1_b = [Buf() for _ in range(NSL)]
        b2 = [ar.alloc(514, F32) for _ in range(NSL)]; b2_b = [Buf() for _ in range(NSL)]
        b3 = [ar.alloc(512, F32) for _ in range(NSL)]; b3_b = [Buf() for _ in range(NSL)]
        b4 = [ar.alloc(512, F32) for _ in range(NSL)]; b4_b = [Buf() for _ in range(NSL)]

        DMA("fgrep", fgrep, fg_d.partition_broadcast(128), [], [fgrep_b])
        cast_cnt = 0
        for n_ in chunk_order:
            if n_ in pre_chunks:
                continue
            sl = cast_cnt % 3
            DMA("wst%d" % sl, wst[sl][:, :, :], win_v[:, :, 2056 + 128 * n_:2056 + 128 * n_ + 128], [], [wst_b[sl]])
            if cast_cnt % 2 == 0:
                OP("dve", "tensor_copy", [wst_b[sl]], [wcv_b[n_]], out=wcvc[n_], in_=wst[sl][:, :, :])
            else:
                ACT(wcvc[n_], wst[sl][:, :, :], AF.Copy, [wst_b[sl]], [wcv_b[n_]])
            cast_cnt += 1

        A5 = [A[0], A[1], SB[0], SB[1], SB[2]]; A5_b = [A_b[0], A_b[1], SB_b[0], SB_b[1], SB_b[2]]

        def next_A5():
            i = a_cnt[0] % 5
            a_cnt[0] += 1
            return A5[i], A5_b[i]

        def conv_w(k, i):
            return cw_t[:, 4 * k + i:4 * k + i + 1]

        def meta_halo(i):
            a, a_b = next_A5()
            for kc in range(KC):
                MM(a[:, 0:16], wcvc[4 + i][:, kc, :], uT[:, kc, 0:16], kc == 0, kc == KC - 1,
                   [wcv_b[4 + i], uT_t[0]], [a_b])
            ACT(mc[:, i, :], a[:, 0:16], AF.Copy, [a_b], [mc_b])
            a2, a2_b = next_A5()
            for kc in range(KC):
                MM(a2[:, 0:16], wcvc[8 + i][:, kc, :], uT[:, kc, 0:16], kc == 0, kc == KC - 1,
                   [wcv_b[8 + i], uT_t[0]], [a2_b])
            OP("dve", "tensor_tensor", [a2_b, mc_b], [halo_b[i]], out=halo[:, i, :], in0=a2[:, 14:16],
               in1=mc[:, i, 14:16], op=ALU.mult)

        chain_cnt = [0]

        def phase1(G, i):
            q0 = 16 + 512 * G
            sl = chain_cnt[0] % NSL
            chain_cnt[0] += 1

            def inproj(blk):
                a, a_b = next_A5()
                ncol = 128 * (4 * blk + i)
                for kc in range(KC):
                    MM(a[:, :], wcvc[4 * blk + i][:, kc, :], uT[:, kc, q0:q0 + 512], kc == 0, kc == KC - 1,
                       [wcv_b[4 * blk + i]] + ub(q0, 512), [a_b])
                return a, a_b
            if G == 0:
                meta_halo(i)
            aC, aC_b = inproj(1)
            ACT(b1[sl], aC[:, :], AF.Copy, [aC_b], [b1_b[sl]])
            aX, aX_b = inproj(2)
            OP("pool", "tensor_copy", [halo_b[i]], [b2_b[sl]], out=b2[sl][:, 0:2], in_=halo[:, i, :])
            OP("dve", "tensor_tensor", [aX_b, b1_b[sl]], [b2_b[sl]], out=b2[sl][:, 2:514], in0=aX[:, :], in1=b1[sl],
               op=ALU.mult)
            OP("pool", "tensor_copy", [b2_b[sl]], [halo_b[i]], out=halo[:, i, :], in_=b2[sl][:, 512:514])
            ACT(b3[sl], b2[sl][:, 2:514], AF.Copy, [b2_b[sl], small_b], [b3_b[sl]], scale=conv_w(2, i))
            OP("dve", "scalar_tensor_tensor", [b2_b[sl], b3_b[sl], small_b], [b3_b[sl]], out=b3[sl], in0=b2[sl][:, 1:513],
               scalar=conv_w(1, i), in1=b3[sl], op0=ALU.mult, op1=ALU.add)
            OP("dve", "scalar_tensor_tensor", [b2_b[sl], b3_b[sl], small_b], [b3_b[sl]], out=b3[sl], in0=b2[sl][:, 0:512],
               scalar=conv_w(0, i), in1=b3[sl], op0=ALU.mult, op1=ALU.add)
            aB, aB_b = inproj(0)
            OP("dve", "tensor_tensor", [aB_b, b3_b[sl]], [b3_b[sl]], out=b3[sl], in0=aB[:, :], in1=b3[sl], op=ALU.mult)
            ACT(b1[sl], b3[sl], AF.Square, [b3_b[sl]], [b1_b[sl]])
            aZ, aZ_b = inproj(3)
            ACT(b4[sl], aZ[:, :], AF.Exp, [aZ_b], [b4_b[sl]], scale=-1.0)
            ACT(b4[sl], b4[sl], AF.Ln, [b4_b[sl]], [b4_b[sl]], bias=1.0)
            ACT(b4[sl], b4[sl], AF.Exp, [b4_b[sl]], [b4_b[sl]], scale=-1.0)
            OP("dve", "tensor_tensor", [aZ_b, b4_b[sl]], [b4_b[sl]], out=b4[sl], in0=aZ[:, :], in1=b4[sl], op=ALU.mult)
            return (G, i, sl)

        def phase2(state):
            G, i, sl = state
            ys = G % 2
            MM(X[:, :], cst[:, C_WG:C_WG + 128], b1[sl], True, True, [cst_b, b1_b[sl]], [X_b])
            ACT(b2[sl][:, 0:512], X[:, :], AF.Ln, [X_b], [b2_b[sl]], bias=EPS)
            ACT(b2[sl][:, 0:512], b2[sl][:, 0:512], AF.Exp, [b2_b[sl]], [b2_b[sl]], scale=-0.5)
            OP("dve", "tensor_tensor", [b3_b[sl], b2_b[sl]], [b3_b[sl]], out=b3[sl], in0=b3[sl], in1=b2[sl][:, 0:512],
               op=ALU.mult)
            OP("dve", "scalar_tensor_tensor", [b3_b[sl], b4_b[sl], small_b], [ymix_b[ys]], out=ymix[ys][:, i, :],
               in0=b3[sl], scalar=cg_t[:, i:i + 1], in1=b4[sl], op0=ALU.mult, op1=ALU.mult)

        tt_cnt = [0]

        def outproj(G, tt):
            r0 = 512 * G + 128 * tt
            ys = G % 2
            s3 = tt_cnt[0] % 3
            st_slot = tt_cnt[0] % 2
            tt_cnt[0] += 1
            DMA("xt%d" % s3, xt[s3], x_d[r0:r0 + 128, :], [], [xt_b[s3]])
            for half in range(2):
                pb, pb_b = OB[half], OB_b[half]
                for e_ in range(8):
                    if e_ < 4:
                        lh, lh_b = mixA[:, e_, r0:r0 + 128], mixA_b[e_][G]
                    else:
                        lh, lh_b = ymix[ys][:, e_ - 4, 128 * tt:128 * tt + 128], ymix_b[ys]
                    MM(pb[:, :], lh, wob[:, e_, 512 * half:512 * half + 512], e_ == 0, e_ == 7,
                       [lh_b] + wob_b, [pb_b])
                OP("dve", "tensor_tensor", [pb_b, xt_b[s3]], [xt_b[s3]], out=xt[s3][:, 512 * half:512 * half + 512],
                   in0=pb[:, :], in1=xt[s3][:, 512 * half:512 * half + 512], op=ALU.add)
            st, st_b = stat[:, 4 * st_slot:4 * st_slot + 4], stat_b[st_slot]
            ACT(junk, xt[s3], AF.Square, [xt_b[s3]], [junk_b, st_b], accum_out=st[:, 0:1])
            ACT(st[:, 1:2], st[:, 0:1], AF.Ln, [st_b], [st_b], scale=1.0 / D, bias=EPS)
            ACT(st[:, 2:3], st[:, 1:2], AF.Exp, [st_b], [st_b], scale=-0.5)
            OP("dve", "scalar_tensor_tensor", [xt_b[s3], st_b, fgrep_b], [xt_b[s3]], out=xt[s3], in0=xt[s3],
               scalar=st[:, 2:3], in1=fgrep, op0=ALU.mult, op1=ALU.mult)
            o = DMA("y%d" % s3, y_d[r0:r0 + 128, :], xt[s3], [xt_b[s3]], [])
            o.final = True

        prev = None
        pend = []
        for G in range(NG):
            for i in range(4):
                st_ = phase1(G, i)
                if prev is not None:
                    phase2(prev)
                prev = st_
                if i == 0 and G > 0:
                    pend.extend((G - 1, tt) for tt in range(4))
                    if len(pend) > 4:
                        outproj(*pend.pop(0))
                elif pend:
                    outproj(*pend.pop(0))
        phase2(prev)
        pend.extend((NG - 1, tt) for tt in range(4))
        while pend:
            outproj(*pend.pop(0))
        S.emit(nc, es)
    return nc


_CACHE = {}


def _host_inputs(NB, meta, norm_g, w_in, b_f, conv_w, attn_norm_g, conv_norm_g, w_out, final_norm_g):
    f32 = np.float32
    w = np.array(w_in[0], dtype=f32, copy=True)
    swap = np.array([1, 0, 3, 2, 5, 4, 7, 6])
    w[:, 1536:1544] = w[:, 1536:1544][:, swap]
    bf = np.asarray(b_f[0], f32)[swap]
    shared = {
        "meta": np.ascontiguousarray(meta, f32),
        "norm_g": np.ascontiguousarray(norm_g[0:1], f32),
        "w_in": np.ascontiguousarray(w),
        "bf_rep": np.ascontiguousarray(np.tile(bf, NB + 1)[None, :], f32),
        "cwT": np.ascontiguousarray(np.asarray(conv_w[0], f32).reshape(3, 4, 128).transpose(2, 0, 1).reshape(128, 12)),
        "ag": np.ascontiguousarray(np.asarray(attn_norm_g[0], f32).reshape(4, 128).T),
        "cg": np.ascontiguousarray(np.asarray(conv_norm_g[0], f32).reshape(4, 128).T),
        "w_out": np.ascontiguousarray(w_out[0], f32),
        "fg": np.ascontiguousarray(np.asarray(final_norm_g, f32)[None, :]),
        "cst": make_cst(),
        "ones_bf": np.full((1, (16 + 128 * NB) // 2), 0x3F803F80, dtype=np.uint32).view(np.float32),
    }
    return shared


def kernel(x, meta, norm_g, w_in, b_f, conv_w, attn_norm_g, conv_norm_g, w_out, final_norm_g):
    x = np.asarray(x, np.float32)
    B, SEQ, _ = x.shape
    NB = SEQ // 128
    if NB not in _CACHE:
        _CACHE[NB] = build_program(NB)
    nc = _CACHE[NB]
    shared = _host_inputs(NB, meta, norm_g, w_in, b_f, conv_w, attn_norm_g, conv_norm_g, w_out, final_norm_g)
    in_maps = [dict(shared, x=np.ascontiguousarray(x[b])) for b in range(B)]
    res = run_bass_kernel_spmd(nc, in_maps, core_ids=list(range(B)))
    return np.stack([np.asarray(res.results[b]["y"], np.float32) for b in range(B)], axis=0)
```

```python
import numpy as np
from contextlib import ExitStack
import concourse.bass as bass
import concourse.mybir as mybir
from concourse.bass_utils import run_bass_kernel_spmd

F32 = mybir.dt.float32
BF16 = mybir.dt.bfloat16
AF = mybir.ActivationFunctionType
ALU = mybir.AluOpType

D = 1024
KC = 8
DIN = 4104
EPS = 1e-6
NEG = -30000.0
C_ID, C_TRI, C_SEL, C_MASK, C_WE, C_WO, C_WG, C_COLS = 0, 128, 256, 384, 512, 640, 768, 896
NCST = 904


class Buf:
    __slots__ = ("name", "last_w", "readers")

    def __init__(self, name=""):
        self.name = name
        self.last_w = None
        self.readers = []


class Op:
    __slots__ = ("eng", "fn", "deps", "signal", "count", "dma_key", "sem", "final", "idx")

    def __init__(self, eng, fn, dma_key=None):
        self.eng = eng
        self.fn = fn
        self.deps = []
        self.signal = False
        self.count = None
        self.dma_key = dma_key
        self.sem = None
        self.final = False


class Sched:
    ENGS = ("pe", "act", "dve", "pool", "sp")

    def __init__(self):
        self.ops = []
        self.last = {}
        self.pending_barrier = {}

    def op(self, eng, fn, reads=(), writes=(), dma_key=None):
        o = Op(eng, fn, dma_key)
        deps = {}
        for b in reads:
            if b.last_w is not None:
                deps[id(b.last_w)] = b.last_w
        for b in writes:
            if b.last_w is not None:
                deps[id(b.last_w)] = b.last_w
            for r in b.readers:
                deps[id(r)] = r
        if eng in self.pending_barrier:
            for d in self.pending_barrier.pop(eng):
                deps[id(d)] = d
        best = {}
        for d in deps.values():
            k = ("dma", d.dma_key) if d.dma_key is not None else d.eng
            if k not in best or best[k].idx < d.idx:
                best[k] = d
        o.deps = list(best.values())
        o.idx = len(self.ops)
        for b in reads:
            b.readers.append(o)
        for b in writes:
            b.last_w = o
            b.readers = []
        self.ops.append(o)
        self.last[("dma", dma_key) if dma_key is not None else eng] = o
        return o

    def barrier(self):
        lasts = list(self.last.values())
        for e in self.ENGS:
            self.pending_barrier[e] = list(self.pending_barrier.get(e, [])) + lasts

    def emit(self, nc, es):
        ops = self.ops
        for o in ops:
            for d in o.deps:
                if d.dma_key is not None:
                    continue
                if d.eng == "pe" and o.eng == "pe" and o.dma_key is None:
                    continue
                d.signal = True
        eng_sem = {e: es.enter_context(nc.semaphore("s_" + e)) for e in ("pe", "act", "dve", "pool")}
        cnt = {e: 0 for e in eng_sem}
        dma_sems, dma_cnt = {}, {}
        for o in ops:
            if o.dma_key is not None:
                if o.dma_key not in dma_sems:
                    dma_sems[o.dma_key] = es.enter_context(nc.semaphore("d_" + str(o.dma_key)))
                    dma_cnt[o.dma_key] = 0
                dma_cnt[o.dma_key] += 16
                o.sem, o.count = dma_sems[o.dma_key], dma_cnt[o.dma_key]
            elif o.signal:
                cnt[o.eng] += 1
                o.sem, o.count = eng_sem[o.eng], cnt[o.eng]
        streams = {e: [o for o in ops if o.eng == e] for e in self.ENGS}
        final = [o for o in ops if o.final]

        def run(ename, eng):
            known = {}
            for o in streams[ename]:
                for d in o.deps:
                    if d.sem is None:
                        continue
                    if d.eng == "pe" and ename == "pe" and d.dma_key is None and o.dma_key is None:
                        continue
                    k = id(d.sem)
                    if known.get(k, 0) >= d.count:
                        continue
                    eng.wait_ge(d.sem, d.count)
                    known[k] = d.count
                ins = o.fn(eng)
                if o.dma_key is not None:
                    ins.then_inc(o.sem, 16)
                elif o.signal:
                    ins.then_inc(o.sem, 1)
            if ename == "sp":
                for o in final:
                    eng.wait_ge(o.sem, o.count)

        with nc.Block() as block:
            @block.tensor
            def _(e):
                run("pe", e)

            @block.scalar
            def _(e):
                run("act", e)

            @block.vector
            def _(e):
                run("dve", e)

            @block.gpsimd
            def _(e):
                run("pool", e)

            @block.sync
            def _(e):
                run("sp", e)


class Arena:
    def __init__(self, ap, width):
        self.ap, self.width, self.off, self.top = ap, width, 0, width

    def reset(self):
        self.off, self.top = 0, self.width

    def alloc_top(self, n, dt):
        w = n if dt == F32 else (n + 1) // 2
        assert self.top - w >= self.off, ("arena overflow (top)", self.off, w, self.top)
        self.top -= w
        a = self.ap[:, self.top:self.top + w]
        if dt != F32:
            a = a.bitcast(dt)[:, 0:n]
        return a

    def alloc(self, n, dt):
        w = n if dt == F32 else (n + 1) // 2
        assert self.off + w <= self.top, ("arena overflow", self.off, w, self.top)
        a = self.ap[:, self.off:self.off + w]
        self.off += w
        if dt != F32:
            a = a.bitcast(dt)[:, 0:n]
        return a


def make_cst():
    c = np.zeros((128, NCST), np.float32)
    p = np.arange(128)
    c[:, C_ID:C_ID + 128] = np.eye(128)
    c[:, C_TRI:C_TRI + 128] = (p[:, None] <= p[None, :])
    c[127, C_SEL:C_SEL + 128] = 1.0
    c[:, C_MASK:C_MASK + 128] = np.where(p[:, None] > p[None, :], NEG, 0.0)
    c[0:64, C_WE:C_WE + 64] = 1.0 / 64
    c[64, C_WE:C_WE + 64] = EPS
    c[64:128, C_WO + 64:C_WO + 128] = 1.0 / 64
    c[63, C_WO + 64:C_WO + 128] = EPS
    c[0:64, C_WG:C_WG + 64] = 1.0 / 64
    c[64:128, C_WG + 64:C_WG + 128] = 1.0 / 64
    c[:, C_COLS + 0] = (p < 16)
    c[:, C_COLS + 1] = np.where(p < 16, 0.0, NEG)
    c[:, C_COLS + 2] = -1.0 * (p < 16)
    c[:, C_COLS + 3] = (p == 63)
    c[:, C_COLS + 4] = 1.0
    c[:, C_COLS + 5] = 0.0
    return c


def build_program(NB):
    NG = NB // 4
    T = 16 + 128 * NB
    SEQ = 128 * NB
    NT = NB + 1
    NF = NT * 8

    nc = bass.Bass("TRN2", target_bir_lowering=False)

    def dram(name, shape, kind="ExternalInput"):
        return nc.dram_tensor(name, shape, F32, kind=kind).ap()

    x_d = dram("x", [SEQ, D])
    meta_d = dram("meta", [16, D])
    ng_d = dram("norm_g", [1, D])
    win_d = dram("w_in", [D, DIN])
    bfr_d = dram("bf_rep", [1, NF])
    cw_d = dram("cwT", [128, 12])
    ag_d = dram("ag", [128, 4])
    cg_d = dram("cg", [128, 4])
    wout_d = dram("w_out", [D, D])
    fg_d = dram("fg", [1, D])
    cst_d = dram("cst", [128, NCST])
    ones_d = dram("ones_bf", [1, T // 2])
    y_d = dram("y", [SEQ, D], kind="ExternalOutput")

    win_v = win_d.rearrange("(kc p) e -> p kc e", p=128)
    wout_v = wout_d.rearrange("(kc p) e -> p kc e", p=128)

    S = Sched()

    def OP(eng, name, reads, writes, *args, **kw):
        return S.op(eng, lambda e: getattr(e, name)(*args, **kw), reads=reads, writes=writes)

    def DMA(key, out, in_, reads, writes):
        return S.op("sp", lambda e: e.dma_start(out=out, in_=in_), reads=reads, writes=writes, dma_key=key)

    def MM(out, lhsT, rhs, start, stop, reads, writes):
        return S.op("pe", lambda e: e.matmul(out, lhsT=lhsT, rhs=rhs, start=start, stop=stop),
                    reads=reads, writes=writes)

    def ACT(out, in_, func, reads, writes, **kw):
        return S.op("act", lambda e: e.activation(out=out, in_=in_, func=func, **kw), reads=reads, writes=writes)

    with ExitStack() as es:
        def sb(name, shape, dt):
            return es.enter_context(nc.sbuf_tensor("sb_" + name, shape, dt))

        cst = sb("cst", [128, NCST], F32); cst_b = Buf()
        identb = sb("identb", [128, 128], BF16); identb_b = Buf()
        maskb = sb("maskb", [128, 128], BF16); maskb_b = Buf()
        uT = sb("uT", [128, KC, T], BF16)
        uT_t = [Buf() for _ in range(NB + 1)]

        def ub(p0, n):
            out = []
            if p0 < 16:
                out.append(uT_t[0])
            lo = max(p0, 16)
            hi = p0 + n
            if hi > lo:
                out += uT_t[1 + (lo - 16) // 128:1 + (hi - 16 + 127) // 128]
            return out
        mixA = sb("mixA", [128, 4, SEQ], BF16); mixA_b = [[Buf() for _ in range(NG)] for _ in range(4)]
        c_all = sb("c_all", [128, NF], F32); c_all_b = Buf()
        negc = sb("negc", [128, NF], F32); negc_b = Buf()
        cw_t = sb("cw_t", [128, 12], F32); ag_t = sb("ag_t", [128, 4], F32); cg_t = sb("cg_t", [128, 4], F32)
        small_b = Buf()
        halo = sb("halo", [128, 4, 2], F32); halo_b = [Buf() for _ in range(4)]
        stat = sb("stat", [128, 8], F32)
        stat_b = [Buf(), Buf()]
        UW = 26900
        U = sb("U", [128, UW], F32)
        ar = Arena(U[:, :], UW)

        def ps(name):
            return es.enter_context(nc.psum_tensor("ps_" + name, [128, 512], F32))
        A = [ps("A0"), ps("A1")]; A_b = [Buf(), Buf()]
        SB = [ps("S0"), ps("S1"), ps("S2")]; SB_b = [Buf(), Buf(), Buf()]
        OB = [ps("O0"), ps("O1")]; OB_b = [Buf(), Buf()]
        X = ps("X"); X_b = Buf()

        ident_f = cst[:, C_ID:C_ID + 128]
        col = lambda i: cst[:, C_COLS + i:C_COLS + i + 1]

        DMA("cst", cst[:], cst_d[:, :], [], [cst_b])
        DMA("small", cw_t[:], cw_d[:, :], [], [small_b])
        DMA("small", ag_t[:], ag_d[:, :], [], [small_b])
        DMA("small", cg_t[:], cg_d[:, :], [], [small_b])
        OP("dve", "tensor_copy", [cst_b], [identb_b], out=identb[:], in_=ident_f)
        OP("dve", "tensor_copy", [cst_b], [maskb_b], out=maskb[:], in_=cst[:, C_MASK:C_MASK + 128])

        ar.reset()
        wbf = [ar.alloc_top(KC * 512, BF16).rearrange("p (k e) -> p k e", e=512) for _ in range(2)]
        wbf_b = [[Buf() for _ in range(4)] for _ in range(2)]
        KEKO = ar.alloc_top(2 * T, BF16)
        KE, KO = KEKO[:, 0:T], KEKO[:, T:2 * T]; K_b = Buf()
        KEO = [KE, KO]
        vaug = ar.alloc_top(NT * 256, BF16).rearrange("p (j c) -> p j c", c=256); vaug_b = Buf()
        CP = [ar.alloc_top(4 * 128, BF16).rearrange("p (t m) -> p t m", m=128) for _ in range(2)]
        CP_b = [Buf(), Buf()]
        wst = [ar.alloc_top(KC * 128, F32).rearrange("p (k e) -> p k e", e=128) for _ in range(2)]
        wst_b = [Buf(), Buf()]
        wob = ar.alloc(KC * D, BF16).rearrange("p (k e) -> p k e", e=D)
        wob_b = [Buf() for _ in range(8)]
        wfb = ar.alloc(KC * 8, BF16).rearrange("p (k e) -> p k e", e=8); wfb_b = Buf()
        bfr = ar.alloc(NF, F32); bfr_b = Buf()
        fblk = ar.alloc(max(4 * NF, D), F32)
        fx, fa, fm, car = [fblk[:, k_ * NF:(k_ + 1) * NF] for k_ in range(4)]
        f_b = Buf()
        shared_off = ar.off

        zc, oc_ = col(5), col(4)
        vflat = vaug.rearrange("p j c -> p (j c)")
        init_ops = [
            lambda: ACT(KEKO.bitcast(F32), zc.to_broadcast([128, T]), AF.Copy, [cst_b], [K_b]),
            lambda: DMA("ones", KE[64:65, :].bitcast(F32), ones_d[:, :], [K_b], [K_b]),
            lambda: DMA("ones", KO[63:64, :].bitcast(F32), ones_d[:, :], [K_b], [K_b]),
            lambda: ACT(vflat.bitcast(F32), zc.to_broadcast([128, NT * 128]), AF.Copy, [cst_b], [vaug_b]),
            lambda: ACT(vaug[:, :, 64:65], oc_.to_broadcast([128, NT]).rearrange("p (j o) -> p j o", o=1), AF.Copy,
                        [cst_b], [vaug_b]),
            lambda: ACT(vaug[:, :, 191:192], oc_.to_broadcast([128, NT]).rearrange("p (j o) -> p j o", o=1), AF.Copy,
                        [cst_b], [vaug_b]),
            lambda: ACT(CP[0].rearrange("p t m -> p (t m)").bitcast(F32), zc.to_broadcast([128, 256]), AF.Copy,
                        [cst_b], [CP_b[0]]),
            lambda: ACT(CP[1].rearrange("p t m -> p (t m)").bitcast(F32), zc.to_broadcast([128, 256]), AF.Copy,
                        [cst_b], [CP_b[1]]),
        ]
        vaug4 = vaug.rearrange("p j (b d) -> p j b d", d=64)

        wst_cnt = [0]

        wtasks = []
        wpending = []

        def load_w(dst, dst_b, src, extra_w=(), defer=False):
            if defer:
                wtasks.append((dst, dst_b, src, extra_w))
                return
            s = wst_cnt[0] % 2
            wst_cnt[0] += 1
            DMA("wst%d" % s, wst[s][:, :, :], src, [], [wst_b[s]])
            OP("dve", "tensor_copy", [wst_b[s]], [dst_b] + list(extra_w), out=dst, in_=wst[s][:, :, :])

        def wtask_step(n=1):
            while wpending:
                s_, dst, dst_b, extra_w = wpending.pop(0)
                OP("dve", "tensor_copy", [wst_b[s_]], [dst_b] + list(extra_w), out=dst, in_=wst[s_][:, :, :])
            for _ in range(n):
                if not wtasks:
                    break
                dst, dst_b, src, extra_w = wtasks.pop(0)
                s_ = wst_cnt[0] % 2
                wst_cnt[0] += 1
                DMA("wst%d" % s_, wst[s_][:, :, :], src, [], [wst_b[s_]])
                wpending.append((s_, dst, dst_b, extra_w))

        def wtask_flush():
            while wtasks or wpending:
                wtask_step(2)

        def load_pair_w(c, defer=False):
            s = c % 2
            for i, c0 in enumerate((128 * c, 512 + 128 * c, 1024 + 128 * c, 1544 + 128 * c)):
                load_w(wbf[s][:, :, 128 * i:128 * i + 128], wbf_b[s][i], win_v[:, :, c0:c0 + 128], defer=defer)

        NXS = 4
        xt = [ar.alloc(D, F32) for _ in range(3)] + [fblk[:, 0:D]]; xt_b = [Buf() for _ in range(NXS)]
        xb = [ar.alloc(D, BF16) for _ in range(2)]; xb_b = [Buf(), Buf()]
        junk = ar.alloc(D, BF16); junk_b = Buf()
        grep = ar.alloc(D, F32); grep_b = Buf()
        DMA("grep", grep, ng_d.partition_broadcast(128), [], [grep_b])
        Xb16 = X[:].bitcast(BF16)

        def tile_stats(src, src_b, slot):
            st, st_b = stat[:, 4 * slot:4 * slot + 4], stat_b[slot]
            ACT(junk, src, AF.Square, [src_b], [junk_b, st_b], accum_out=st[:, 0:1])
            ACT(st[:, 1:2], st[:, 0:1], AF.Ln, [st_b], [st_b], scale=1.0 / D, bias=EPS)
            ACT(st[:, 2:3], st[:, 1:2], AF.Exp, [st_b], [st_b], scale=-0.5)
            return st[:, 2:3], st_b

        def kcols(j):
            return (0, 128) if j == 0 else (16 + 128 * (j - 1), 128)

        a_cnt = [0]

        def next_A():
            i = a_cnt[0] % 2
            a_cnt[0] += 1
            return A[i], A_b[i]

        def pair_w(c):
            ws = c % 2
            return [wbf[ws][:, :, 128 * i:128 * i + 128] for i in range(4)], wbf_b[ws]

        def kv_inproj_part(c, g):
            (wq, wk, wv, wz), (wq_b, wk_b, wv_b, wz_b) = pair_w(c)
            ranges = ([(0, 16)] if g == 0 else []) + [(16 + 512 * g, 512)]
            for (p0, n) in ranges:
                a, a_b = next_A()
                for kc in range(KC):
                    MM(a[:, 0:n], wk[:, kc, :], uT[:, kc, p0:p0 + n], kc == 0, kc == KC - 1, [wk_b] + ub(p0, n), [a_b])
                OP("dve", "tensor_copy", [a_b], [K_b], out=KE[0:64, p0:p0 + n], in_=a[0:64, 0:n])
                OP("dve", "tensor_copy", [a_b], [K_b], out=KO[64:128, p0:p0 + n], in_=a[64:128, 0:n])
            tiles = ([0] if g == 0 else []) + list(range(4 * g + 1, 4 * g + 5))
            for jb in range(0, len(tiles), 4):
                tl = tiles[jb:jb + 4]
                nt = len(tl)
                a, a_b = next_A()
                for t, j in enumerate(tl):
                    p0, n = kcols(j)
                    for kc in range(KC):
                        MM(a[:, 128 * t:128 * t + 128], uT[:, kc, p0:p0 + n], wv[:, kc, :], kc == 0, kc == KC - 1,
                           [wv_b] + ub(p0, n), [a_b])
                OP("dve", "tensor_copy", [a_b], [vaug_b], out=vaug4[:, tl[0]:tl[0] + nt, 0:4:3, :],
                   in_=a[:, 0:128 * nt].rearrange("p (t b d) -> p t b d", b=2, d=64))

        def kv_inproj(c):
            for g in range(NG):
                kv_inproj_part(c, g)

        TB = [(X, X_b), (SB[0], SB_b[0])]
        TB16 = [X[:].bitcast(BF16), SB[0][:].bitcast(BF16)]

        def a_load(ti):
            s3 = ti % NXS
            if ti == 0:
                OP("dve", "memset", [], [xt_b[s3]], xt[s3], 0.0)
                DMA("xt%d" % s3, xt[s3][0:16, :], meta_d[:, :], [], [xt_b[s3]])
            else:
                DMA("xt%d" % s3, xt[s3], x_d[(ti - 1) * 128:ti * 128, :], [], [xt_b[s3]])

        def a_norm(ti):
            s3, s2 = ti % NXS, ti % 2
            rs, rs_b = tile_stats(xt[s3], xt_b[s3], s2)
            OP("dve", "scalar_tensor_tensor", [xt_b[s3], rs_b, grep_b], [xb_b[s2]],
               out=xb[s2], in0=xt[s3], scalar=rs, in1=grep, op0=ALU.mult, op1=ALU.mult)

        for ti in range(min(NXS, NT)):
            a_load(ti)
        DMA("wf", wst[0][:, :, 0:8], win_v[:, :, 1536:1544], [], [wst_b[0]])
        DMA("bfr", bfr, bfr_d.partition_broadcast(128), [], [bfr_b])
        OP("dve", "tensor_copy", [wst_b[0]], [wfb_b], out=wfb, in_=wst[0][:, :, 0:8])
        a_norm(0)
        for ti in range(NT):
            s2 = ti % 2
            tb, tb_b = TB[s2]
            tb16 = TB16[s2]
            for kc in range(KC):
                S.op("pe", lambda e, kc=kc, s2=s2, tb16=tb16: e.transpose(out=tb16[:, kc * 128:(kc + 1) * 128],
                                                                           in_=xb[s2][:, kc * 128:(kc + 1) * 128],
                                                                           identity=identb[:]),
                     reads=[xb_b[s2], identb_b], writes=[tb_b])
            if ti + 1 < NT:
                a_norm(ti + 1)
            if ti >= 1 and init_ops:
                init_ops.pop(0)()
            if ti == 0:
                OP("dve", "tensor_copy", [tb_b], [uT_t[0]], out=uT[:, :, 0:16],
                   in_=tb16.rearrange("p (k t) -> p k t", t=128)[:, :, 0:16])
            else:
                p0 = 16 + 128 * (ti - 1)
                OP("dve", "tensor_copy", [tb_b], [uT_t[ti]], out=uT[:, :, p0:p0 + 128],
                   in_=tb16.rearrange("p (k t) -> p k t", t=128))
            if ti + NXS < NT:
                a_load(ti + NXS)
            if ti == 1:
                load_pair_w(0)
            if ti >= 4 and ti % 4 == 0:
                kv_inproj_part(0, ti // 4 - 1)

        while init_ops:
            init_ops.pop(0)()

        for j in range(NT):
            p0, n = kcols(j)
            for kc in range(KC):
                MM(X[:, 8 * j:8 * j + 8], uT[:, kc, p0:p0 + n], wfb[:, kc, :], kc == 0, kc == KC - 1,
                   ub(p0, n) + [wfb_b], [X_b])
        OP("dve", "tensor_tensor", [X_b, bfr_b], [f_b], out=fx, in0=X[:, 0:NF], in1=bfr, op=ALU.add)
        ACT(fa, fx, AF.Abs, [f_b], [f_b])
        ACT(fa, fa, AF.Exp, [f_b], [f_b], scale=-1.0)
        ACT(fa, fa, AF.Ln, [f_b], [f_b], bias=1.0)
        OP("dve", "tensor_scalar_min", [f_b], [f_b], out=fm, in0=fx, scalar1=0.0)
        OP("dve", "tensor_tensor", [f_b], [f_b], out=fm, in0=fm, in1=fa, op=ALU.subtract)
        OP("dve", "tensor_scalar", [f_b, cst_b], [f_b], out=fm[:, 0:8], in0=fm[:, 0:8], scalar1=col(0),
           scalar2=None, op0=ALU.mult)
        cw_sb, tot_sb = fx, fa
        MM(X[:, 0:NF], cst[:, C_TRI:C_TRI + 128], fm, True, True, [f_b, cst_b], [X_b])
        OP("dve", "tensor_copy", [X_b], [f_b], out=cw_sb, in_=X[:, 0:NF])
        MM(X[:, 0:NF], cst[:, C_SEL:C_SEL + 128], cw_sb, True, True, [f_b, cst_b], [X_b])
        OP("dve", "tensor_copy", [X_b], [f_b], out=tot_sb, in_=X[:, 0:NF])
        OP("dve", "memset", [], [f_b], car[:, 0:8], 0.0)
        for j in range(1, NT):
            OP("dve", "tensor_tensor", [f_b], [f_b], out=car[:, 8 * j:8 * j + 8], in0=car[:, 8 * j - 8:8 * j],
               in1=tot_sb[:, 8 * j - 8:8 * j], op=ALU.add)
        OP("dve", "tensor_tensor", [f_b], [c_all_b], out=c_all[:], in0=cw_sb, in1=car, op=ALU.add)
        OP("dve", "tensor_scalar", [c_all_b], [negc_b], out=negc[:, 8:NF], in0=c_all[:, 8:NF], scalar1=-1.0,
           scalar2=None, op0=ALU.mult)
        OP("dve", "tensor_scalar", [c_all_b, cst_b], [negc_b], out=negc[:, 0:8], in0=c_all[:, 0:8],
           scalar1=col(2), scalar2=col(1), op0=ALU.mult, op1=ALU.add)

        ar.off = shared_off
        QE = [ar.alloc(512, BF16) for _ in range(2)]; QO = [ar.alloc(512, BF16) for _ in range(2)]
        Q_b = [Buf(), Buf()]
        PT = [ar.alloc(512, BF16) for _ in range(4)]; PT_b = [Buf() for _ in range(4)]
        ez = ar.alloc(512, F32); ez_b = Buf()
        sz = [ar.alloc(512, F32) for _ in range(2)]; sz_b = [Buf(), Buf()]
        sqE = ar.alloc(512, F32); sqO = ar.alloc(512, F32); sq_b = Buf()
        rstd = ar.alloc(512, F32); rstd_b = Buf()
        lnr = rstd
        tmp, tmp_b = ez, ez_b

        kr = [(0, 16)] + [(16 + 512 * g, 512) for g in range(NG)]
        s_cnt = [0]
        q_cnt = [0]
        OcE = ar.alloc(512, F32); OcO = ar.alloc(512, F32); Oc_b = [Buf(), Buf()]
        Oc = [OcE, OcO]

        def prologue(c, G):
            (wq, wk, wv, wz), (wq_b, wk_b, wv_b, wz_b) = pair_w(c)
            q0 = 16 + 512 * G
            qs = q_cnt[0] % 2
            q_cnt[0] += 1
            a, a_b = next_A()
            for kc in range(KC):
                MM(a[:, :], wq[:, kc, :], uT[:, kc, q0:q0 + 512], kc == 0, kc == KC - 1, [wq_b] + ub(q0, 512), [a_b])
                yield qs
            OP("dve", "tensor_copy", [a_b], [Q_b[qs]], out=QE[qs][0:64, :], in_=a[0:64, :])
            OP("dve", "tensor_copy", [a_b], [Q_b[qs]], out=QO[qs][64:128, :], in_=a[64:128, :])
            OP("dve", "tensor_scalar", [c_all_b], [CP_b[qs]], out=CP[qs][:, :, 63:65],
               in0=c_all[:, 8 * (4 * G + 1):8 * (4 * G + 5)].rearrange("p (t h) -> p t h", h=8)[:, :, 2 * c:2 * c + 2],
               scalar1=8.0, scalar2=None, op0=ALU.mult)
            for t in range(4):
                MM(X[:, 128 * t:128 * t + 128], CP[qs][:, t, :], identb[:], True, True, [CP_b[qs], identb_b], [X_b])
            yield qs
            OP("dve", "tensor_copy", [X_b], [Q_b[qs]], out=QE[qs][64:128, :], in_=X[64:128, :])
            OP("dve", "tensor_copy", [X_b], [Q_b[qs]], out=QO[qs][0:64, :], in_=X[0:64, :])
            a, a_b = next_A()
            for kc in range(KC):
                MM(a[:, :], wz[:, kc, :], uT[:, kc, q0:q0 + 512], kc == 0, kc == KC - 1, [wz_b] + ub(q0, 512), [a_b])
                yield qs
            ACT(ez, a[:, :], AF.Exp, [a_b], [ez_b], scale=-1.0)
            OP("dve", "tensor_scalar_add", [ez_b], [ez_b], out=ez, in0=ez, scalar1=1.0)
            OP("dve", "reciprocal", [ez_b], [ez_b], out=ez, in_=ez)
            OP("dve", "tensor_tensor", [ez_b, a_b], [sz_b[qs]], out=sz[qs], in0=a[:, :], in1=ez, op=ALU.mult)
            yield qs

        def run_all(gen):
            qs = None
            for qs in gen:
                pass
            return qs

        def make_steps(c, G, qs):
            steps = []
            LAG = 2
            nblk = 4 * G + 5
            blocks = []
            for par in range(2):
                for j in range(nblk):
                    r = j - (4 * G + 1)
                    blocks.append((par, j, 128 * r if r > 0 else 0, r >= 0))
            slots = {}
            tot = len(blocks)
            for idx in range(tot + LAG):
                def step(idx=idx):
                    if idx < tot:
                        par, j, c0, diag = blocks[idx]
                        Kt = KEO[par]
                        Qt = (QE if par == 0 else QO)[qs]
                        hb = 2 * c + (1 - par)
                        si = s_cnt[0] % 3
                        pi = s_cnt[0] % 4
                        s_cnt[0] += 1
                        slots[idx] = pi
                        p0, n = kcols(j)
                        MM(SB[si][:, c0:512], Kt[:, p0:p0 + 128], Qt[:, c0:512], True, not diag,
                           [K_b, Q_b[qs]], [SB_b[si]])
                        if diag:
                            MM(SB[si][:, c0:c0 + 128], identb[:], maskb[:], False, True,
                               [identb_b, maskb_b], [SB_b[si]])
                        ACT(PT[pi][:, c0:512], SB[si][:, c0:512], AF.Exp, [SB_b[si], negc_b], [PT_b[pi]],
                            bias=negc[:, 8 * j + hb:8 * j + hb + 1], scale=0.125)
                    if idx >= LAG:
                        par, j, c0, diag = blocks[idx - LAG]
                        O, O_b = OB[par], OB_b[par]
                        pi = slots[idx - LAG]
                        MM(O[:, c0:512], vaug[:, j, 128 * par:128 * par + 128], PT[pi][:, c0:512],
                           j == 0, j == nblk - 1, [vaug_b, PT_b[pi]], [O_b])
                        if j == nblk - 1:
                            OP("dve", "tensor_copy", [O_b], [Oc_b[par]], out=Oc[par], in_=O[:, :])
                steps.append(step)
            return steps

        def post(c, G, qs):
            OP("dve", "tensor_tensor", [Oc_b[0]], [sq_b], out=sqE, in0=OcE, in1=OcE, op=ALU.mult)
            OP("dve", "tensor_tensor", [Oc_b[1]], [sq_b], out=sqO, in0=OcO, in1=OcO, op=ALU.mult)
            yield
            for h0 in (0, 256):
                MM(X[:, h0:h0 + 256], cst[:, C_WE:C_WE + 128], sqE[:, h0:h0 + 256], True, False, [cst_b, sq_b], [X_b])
                yield
                MM(X[:, h0:h0 + 256], cst[:, C_WO:C_WO + 128], sqO[:, h0:h0 + 256], False, True, [cst_b, sq_b], [X_b])
                yield
            ACT(lnr, X[:, :], AF.Ln, [X_b], [rstd_b])
            ACT(rstd, lnr, AF.Exp, [rstd_b], [rstd_b], scale=-0.5)
            OP("dve", "tensor_tensor", [Oc_b[0], rstd_b], [tmp_b], out=tmp[0:64, :], in0=OcE[0:64, :],
               in1=rstd[0:64, :], op=ALU.mult)
            OP("dve", "tensor_tensor", [Oc_b[1], rstd_b], [tmp_b], out=tmp[64:128, :], in0=OcO[64:128, :],
               in1=rstd[64:128, :], op=ALU.mult)
            OP("dve", "scalar_tensor_tensor", [tmp_b, sz_b[qs], small_b], [mixA_b[c][G]],
               out=mixA[:, c, 512 * G:512 * G + 512], in0=tmp, scalar=ag_t[:, c:c + 1], in1=sz[qs],
               op0=ALU.mult, op1=ALU.mult)

        seq = [(c, G) for c in range(4) for G in range(NG)]
        chunk_order = [4 * blk + i for i in range(4) for blk in (1, 2, 0, 3)]
        pre_chunks = chunk_order[0:8] if NG > 2 else []
        wcv_b = [Buf() for _ in range(16)]
        wcvc = {}
        for k_, n__ in enumerate(pre_chunks):
            flat = wbf[k_ // 4].rearrange("p k e -> p (k e)")
            wcvc[n__] = flat[:, 1024 * (k_ % 4):1024 * (k_ % 4) + 1024].rearrange("p (k e) -> p k e", e=128)
        pending_post = None
        qs_next = run_all(prologue(0, 0))
        for n_, (c, G) in enumerate(seq):
            qs = qs_next
            if G == 0:
                if c > 0:
                    kv_inproj(c)
                if c + 1 < 4:
                    load_pair_w(c + 1, defer=True)
                if c == 1:
                    for w_ in range(8):
                        load_w(wob[:, :, 128 * w_:128 * w_ + 128], wob_b[w_], wout_v[:, :, 128 * w_:128 * w_ + 128],
                               defer=True)
            if c == 3 and G == 0:
                for n__ in pre_chunks[0:4]:
                    load_w(wcvc[n__], wcv_b[n__], win_v[:, :, 2056 + 128 * n__:2056 + 128 * n__ + 128],
                           extra_w=wbf_b[0], defer=True)
            if c == 3 and G == NG - 1:
                for n__ in pre_chunks[4:8]:
                    load_w(wcvc[n__], wcv_b[n__], win_v[:, :, 2056 + 128 * n__:2056 + 128 * n__ + 128],
                           extra_w=wbf_b[1], defer=True)
            wtask_step(2)
            steps = make_steps(c, G, qs)
            hook_post = min(10, 4 * G + 1)
            k_guard = 4 * G + 6
            if G == NG - 1 and n_ + 1 < len(seq):
                wtask_flush()
            gen = prologue(*seq[n_ + 1]) if n_ + 1 < len(seq) else None
            post_gen = post(*pending_post) if pending_post is not None else None
            pending_post = None
            for k_, st in enumerate(steps):
                if k_ == k_guard and post_gen is not None:
                    run_all(post_gen)
                    post_gen = None
                st()
                if k_ % 16 == 12:
                    wtask_step(1)
                if post_gen is not None:
                    if k_ >= hook_post:
                        try:
                            next(post_gen)
                        except StopIteration:
                            post_gen = None
                elif gen is not None and k_ > hook_post:
                    try:
                        qs_next = next(gen)
                    except StopIteration:
                        gen = None
            if post_gen is not None:
                run_all(post_gen)
            if gen is not None:
                r_ = run_all(gen)
                if r_ is not None:
                    qs_next = r_
            pending_post = (c, G, qs)
        run_all(post(*pending_post))
        wtask_flush()
        wtask_flush()
        S.barrier()

        ar.reset()
        wob2 = ar.alloc(KC * D, BF16)
        wst = [ar.alloc(KC * 128, F32).rearrange("p (k e) -> p k e", e=128) for _ in range(3)]
        wst_b = [Buf() for _ in range(3)]
        ar.top = UW - 2 * (KC * 512 // 2)
        for n__ in chunk_order:
            if n__ not in wcvc:
                wcvc[n__] = ar.alloc(KC * 128, BF16).rearrange("p (k e) -> p k e", e=128)
        xt = [ar.alloc(D, F32) for _ in range(3)]; xt_b = [Buf() for _ in range(3)]
        junk = ar.alloc(D, BF16); junk_b = Buf()
        fgrep = ar.alloc(D, F32); fgrep_b = Buf()
        ymix = [ar.alloc(4 * 512, BF16).rearrange("p (i t) -> p i t", t=512) for _ in range(2)]
        ymix_b = [Buf(), Buf()]
        mc = ar.alloc(4 * 16, F32).rearrange("p (i t) -> p i t", t=16); mc_b = Buf()
        NSL = 2
        b1 = [ar.alloc(512, F32) for _ in range(NSL)]; b1_b = [Buf() for _ in range(NSL)]
        b2 = [ar.alloc(514, F32) for _ in range(NSL)]; b2_b = [Buf() for _ in range(NSL)]
        b3 = [ar.alloc(512, F32) for _ in range(NSL)]; b3_b = [Buf() for _ in range(NSL)]
        b4 = [ar.alloc(512, F32) for _ in range(NSL)]; b4_b = [Buf() for _ in range(NSL)]

        DMA("fgrep", fgrep, fg_d.partition_broadcast(128), [], [fgrep_b])
        cast_cnt = 0
        for n_ in chunk_order:
            if n_ in pre_chunks:
                continue
            sl = cast_cnt % 3
            DMA("wst%d" % sl, wst[sl][:, :, :], win_v[:, :, 2056 + 128 * n_:2056 + 128 * n_ + 128], [], [wst_b[sl]])
            if cast_cnt % 2 == 0:
                OP("dve", "tensor_copy", [wst_b[sl]], [wcv_b[n_]], out=wcvc[n_], in_=wst[sl][:, :, :])
            else:
                ACT(wcvc[n_], wst[sl][:, :, :], AF.Copy, [wst_b[sl]], [wcv_b[n_]])
            cast_cnt += 1

        A5 = [A[0], A[1], SB[0], SB[1], SB[2]]; A5_b = [A_b[0], A_b[1], SB_b[0], SB_b[1], SB_b[2]]

        def next_A5():
            i = a_cnt[0] % 5
            a_cnt[0] += 1
            return A5[i], A5_b[i]

        def conv_w(k, i):
            return cw_t[:, 4 * k + i:4 * k + i + 1]

        def meta_halo(i):
            a, a_b = next_A5()
            for kc in range(KC):
                MM(a[:, 0:16], wcvc[4 + i][:, kc, :], uT[:, kc, 0:16], kc == 0, kc == KC - 1,
                   [wcv_b[4 + i], uT_t[0]], [a_b])
            ACT(mc[:, i, :], a[:, 0:16], AF.Copy, [a_b], [mc_b])
            a2, a2_b = next_A5()
            for kc in range(KC):
                MM(a2[:, 0:16], wcvc[8 + i][:, kc, :], uT[:, kc, 0:16], kc == 0, kc == KC - 1,
                   [wcv_b[8 + i], uT_t[0]], [a2_b])
            OP("dve", "tensor_tensor", [a2_b, mc_b], [halo_b[i]], out=halo[:, i, :], in0=a2[:, 14:16],
               in1=mc[:, i, 14:16], op=ALU.mult)

        chain_cnt = [0]

        def phase1(G, i):
            q0 = 16 + 512 * G
            sl = chain_cnt[0] % NSL
            chain_cnt[0] += 1

            def inproj(blk):
                a, a_b = next_A5()
                ncol = 128 * (4 * blk + i)
                for kc in range(KC):
                    MM(a[:, :], wcvc[4 * blk + i][:, kc, :], uT[:, kc, q0:q0 + 512], kc == 0, kc == KC - 1,
                       [wcv_b[4 * blk + i]] + ub(q0, 512), [a_b])
                return a, a_b
            if G == 0:
                meta_halo(i)
            aC, aC_b = inproj(1)
            ACT(b1[sl], aC[:, :], AF.Copy, [aC_b], [b1_b[sl]])
            aX, aX_b = inproj(2)
            OP("pool", "tensor_copy", [halo_b[i]], [b2_b[sl]], out=b2[sl][:, 0:2], in_=halo[:, i, :])
            OP("dve", "tensor_tensor", [aX_b, b1_b[sl]], [b2_b[sl]], out=b2[sl][:, 2:514], in0=aX[:, :], in1=b1[sl],
               op=ALU.mult)
            OP("pool", "tensor_copy", [b2_b[sl]], [halo_b[i]], out=halo[:, i, :], in_=b2[sl][:, 512:514])
            ACT(b3[sl], b2[sl][:, 2:514], AF.Copy, [b2_b[sl], small_b], [b3_b[sl]], scale=conv_w(2, i))
            OP("dve", "scalar_tensor_tensor", [b2_b[sl], b3_b[sl], small_b], [b3_b[sl]], out=b3[sl], in0=b2[sl][:, 1:513],
               scalar=conv_w(1, i), in1=b3[sl], op0=ALU.mult, op1=ALU.add)
            OP("dve", "scalar_tensor_tensor", [b2_b[sl], b3_b[sl], small_b], [b3_b[sl]], out=b3[sl], in0=b2[sl][:, 0:512],
               scalar=conv_w(0, i), in1=b3[sl], op0=ALU.mult, op1=ALU.add)
            aB, aB_b = inproj(0)
            OP("dve", "tensor_tensor", [aB_b, b3_b[sl]], [b3_b[sl]], out=b3[sl], in0=aB[:, :], in1=b3[sl], op=ALU.mult)
            ACT(b1[sl], b3[sl], AF.Square, [b3_b[sl]], [b1_b[sl]])
            aZ, aZ_b = inproj(3)
            ACT(b4[sl], aZ[:, :], AF.Exp, [aZ_b], [b4_b[sl]], scale=-1.0)
            ACT(b4[sl], b4[sl], AF.Ln, [b4_b[sl]], [b4_b[sl]], bias=1.0)
            ACT(b4[sl], b4[sl], AF.Exp, [b4_b[sl]], [b4_b[sl]], scale=-1.0)
            OP("dve", "tensor_tensor", [aZ_b, b4_b[sl]], [b4_b[sl]], out=b4[sl], in0=aZ[:, :], in1=b4[sl], op=ALU.mult)
            return (G, i, sl)

        def phase2(state):
            G, i, sl = state
            ys = G % 2
            MM(X[:, :], cst[:, C_WG:C_WG + 128], b1[sl], True, True, [cst_b, b1_b[sl]], [X_b])
            ACT(b2[sl][:, 0:512], X[:, :], AF.Ln, [X_b], [b2_b[sl]], bias=EPS)
            ACT(b2[sl][:, 0:512], b2[sl][:, 0:512], AF.Exp, [b2_b[sl]], [b2_b[sl]], scale=-0.5)
            OP("dve", "tensor_tensor", [b3_b[sl], b2_b[sl]], [b3_b[sl]], out=b3[sl], in0=b3[sl], in1=b2[sl][:, 0:512],
               op=ALU.mult)
            OP("dve", "scalar_tensor_tensor", [b3_b[sl], b4_b[sl], small_b], [ymix_b[ys]], out=ymix[ys][:, i, :],
               in0=b3[sl], scalar=cg_t[:, i:i + 1], in1=b4[sl], op0=ALU.mult, op1=ALU.mult)

        tt_cnt = [0]

        xl_cnt = [0]
        xslot = {}

        def xload(G, tt):
            s3_ = xl_cnt[0] % 3
            xl_cnt[0] += 1
            xslot[(G, tt)] = s3_
            r0_ = 512 * G + 128 * tt
            DMA("xt%d" % s3_, xt[s3_], x_d[r0_:r0_ + 128, :], [], [xt_b[s3_]])

        def outproj(G, tt, nxt=None):
            r0 = 512 * G + 128 * tt
            ys = G % 2
            if (G, tt) not in xslot:
                xload(G, tt)
            s3 = xslot[(G, tt)]
            st_slot = tt_cnt[0] % 2
            tt_cnt[0] += 1
            for half in range(2):
                pb, pb_b = OB[half], OB_b[half]
                for e_ in range(8):
                    if e_ < 4:
                        lh, lh_b = mixA[:, e_, r0:r0 + 128], mixA_b[e_][G]
                    else:
                        lh, lh_b = ymix[ys][:, e_ - 4, 128 * tt:128 * tt + 128], ymix_b[ys]
                    MM(pb[:, :], lh, wob[:, e_, 512 * half:512 * half + 512], e_ == 0, e_ == 7,
                       [lh_b] + wob_b, [pb_b])
                OP("dve", "tensor_tensor", [pb_b, xt_b[s3]], [xt_b[s3]], out=xt[s3][:, 512 * half:512 * half + 512],
                   in0=pb[:, :], in1=xt[s3][:, 512 * half:512 * half + 512], op=ALU.add)
            st, st_b = stat[:, 4 * st_slot:4 * st_slot + 4], stat_b[st_slot]
            ACT(junk, xt[s3], AF.Square, [xt_b[s3]], [junk_b, st_b], accum_out=st[:, 0:1])
            ACT(st[:, 1:2], st[:, 0:1], AF.Ln, [st_b], [st_b], scale=1.0 / D, bias=EPS)
            ACT(st[:, 2:3], st[:, 1:2], AF.Exp, [st_b], [st_b], scale=-0.5)
            OP("dve", "scalar_tensor_tensor", [xt_b[s3], st_b, fgrep_b], [xt_b[s3]], out=xt[s3], in0=xt[s3],
               scalar=st[:, 2:3], in1=fgrep, op0=ALU.mult, op1=ALU.mult)
            if nxt is not None and nxt not in xslot:
                xload(*nxt)
            o = DMA("y%d" % s3, y_d[r0:r0 + 128, :], xt[s3], [xt_b[s3]], [])
            o.final = True

        def pop_outproj():
            t_ = pend.pop(0)
            outproj(*t_, nxt=(pend[0] if pend else None))

        prev = None
        pend = []
        for G in range(NG):
            for i in range(4):
                st_ = phase1(G, i)
                if prev is not None:
                    phase2(prev)
                prev = st_
                if i == 0 and G > 0:
                    pend.extend((G - 1, tt) for tt in range(4))
                    if len(pend) > 4:
                        pop_outproj()
                elif pend:
                    pop_outproj()
        phase2(prev)
        pend.extend((NG - 1, tt) for tt in range(4))
        while pend:
            pop_outproj()
        S.emit(nc, es)
    return nc


_CACHE = {}


def _host_inputs(NB, meta, norm_g, w_in, b_f, conv_w, attn_norm_g, conv_norm_g, w_out, final_norm_g):
    f32 = np.float32
    w = np.array(w_in[0], dtype=f32, copy=True)
    swap = np.array([1, 0, 3, 2, 5, 4, 7, 6])
    w[:, 1536:1544] = w[:, 1536:1544][:, swap]
    bf = np.asarray(b_f[0], f32)[swap]
    shared = {
        "meta": np.ascontiguousarray(meta, f32),
        "norm_g": np.ascontiguousarray(norm_g[0:1], f32),
        "w_in": np.ascontiguousarray(w),
        "bf_rep": np.ascontiguousarray(np.tile(bf, NB + 1)[None, :], f32),
        "cwT": np.ascontiguousarray(np.asarray(conv_w[0], f32).reshape(3, 4, 128).transpose(2, 0, 1).reshape(128, 12)),
        "ag": np.ascontiguousarray(np.asarray(attn_norm_g[0], f32).reshape(4, 128).T),
        "cg": np.ascontiguousarray(np.asarray(conv_norm_g[0], f32).reshape(4, 128).T),
        "w_out": np.ascontiguousarray(w_out[0], f32),
        "fg": np.ascontiguousarray(np.asarray(final_norm_g, f32)[None, :]),
        "cst": make_cst(),
        "ones_bf": np.full((1, (16 + 128 * NB) // 2), 0x3F803F80, dtype=np.uint32).view(np.float32),
    }
    return shared


def kernel(x, meta, norm_g, w_in, b_f, conv_w, attn_norm_g, conv_norm_g, w_out, final_norm_g):
    x = np.asarray(x, np.float32)
    B, SEQ, _ = x.shape
    NB = SEQ // 128
    if NB not in _CACHE:
        _CACHE[NB] = build_program(NB)
    nc = _CACHE[NB]
    shared = _host_inputs(NB, meta, norm_g, w_in, b_f, conv_w, attn_norm_g, conv_norm_g, w_out, final_norm_g)
    in_maps = [dict(shared, x=np.ascontiguousarray(x[b])) for b in range(B)]
    res = run_bass_kernel_spmd(nc, in_maps, core_ids=list(range(B)))
    return np.stack([np.asarray(res.results[b]["y"], np.float32) for b in range(B)], axis=0)
```

```python
import numpy as np
from contextlib import ExitStack
import concourse.bass as bass
import concourse.mybir as mybir
from concourse.bass_utils import run_bass_kernel_spmd

F32 = mybir.dt.float32
BF16 = mybir.dt.bfloat16
AF = mybir.ActivationFunctionType
ALU = mybir.AluOpType

D = 1024
KC = 8
DIN = 4104
EPS = 1e-6
NEG = -30000.0
C_ID, C_TRI, C_SEL, C_MASK, C_WE, C_WO, C_WG, C_COLS = 0, 128, 256, 384, 512, 640, 768, 896
NCST = 904


class Buf:
    __slots__ = ("name", "last_w", "readers")

    def __init__(self, name=""):
        self.name = name
        self.last_w = None
        self.readers = []


class Op:
    __slots__ = ("eng", "fn", "deps", "signal", "count", "dma_key", "sem", "final", "idx")

    def __init__(self, eng, fn, dma_key=None):
        self.eng = eng
        self.fn = fn
        self.deps = []
        self.signal = False
        self.count = None
        self.dma_key = dma_key
        self.sem = None
        self.final = False


class Sched:
    ENGS = ("pe", "act", "dve", "pool", "sp")

    def __init__(self):
        self.ops = []
        self.last = {}
        self.pending_barrier = {}

    def op(self, eng, fn, reads=(), writes=(), dma_key=None):
        o = Op(eng, fn, dma_key)
        deps = {}
        for b in reads:
            if b.last_w is not None:
                deps[id(b.last_w)] = b.last_w
        for b in writes:
            if b.last_w is not None:
                deps[id(b.last_w)] = b.last_w
            for r in b.readers:
                deps[id(r)] = r
        if eng in self.pending_barrier:
            for d in self.pending_barrier.pop(eng):
                deps[id(d)] = d
        best = {}
        for d in deps.values():
            k = ("dma", d.dma_key) if d.dma_key is not None else d.eng
            if k not in best or best[k].idx < d.idx:
                best[k] = d
        o.deps = list(best.values())
        o.idx = len(self.ops)
        for b in reads:
            b.readers.append(o)
        for b in writes:
            b.last_w = o
            b.readers = []
        self.ops.append(o)
        self.last[("dma", dma_key) if dma_key is not None else eng] = o
        return o

    def barrier(self):
        lasts = list(self.last.values())
        for e in self.ENGS:
            self.pending_barrier[e] = list(self.pending_barrier.get(e, [])) + lasts

    def emit(self, nc, es):
        ops = self.ops
        for o in ops:
            for d in o.deps:
                if d.dma_key is not None:
                    continue
                if d.eng == "pe" and o.eng == "pe" and o.dma_key is None:
                    continue
                d.signal = True
        eng_sem = {e: es.enter_context(nc.semaphore("s_" + e)) for e in ("pe", "act", "dve", "pool")}
        cnt = {e: 0 for e in eng_sem}
        dma_sems, dma_cnt = {}, {}
        for o in ops:
            if o.dma_key is not None:
                if o.dma_key not in dma_sems:
                    dma_sems[o.dma_key] = es.enter_context(nc.semaphore("d_" + str(o.dma_key)))
                    dma_cnt[o.dma_key] = 0
                dma_cnt[o.dma_key] += 16
                o.sem, o.count = dma_sems[o.dma_key], dma_cnt[o.dma_key]
            elif o.signal:
                cnt[o.eng] += 1
                o.sem, o.count = eng_sem[o.eng], cnt[o.eng]
        streams = {e: [o for o in ops if o.eng == e] for e in self.ENGS}
        final = [o for o in ops if o.final]

        def run(ename, eng):
            known = {}
            for o in streams[ename]:
                for d in o.deps:
                    if d.sem is None:
                        continue
                    if d.eng == "pe" and ename == "pe" and d.dma_key is None and o.dma_key is None:
                        continue
                    k = id(d.sem)
                    if known.get(k, 0) >= d.count:
                        continue
                    eng.wait_ge(d.sem, d.count)
                    known[k] = d.count
                ins = o.fn(eng)
                if o.dma_key is not None:
                    ins.then_inc(o.sem, 16)
                elif o.signal:
                    ins.then_inc(o.sem, 1)
            if ename == "sp":
                for o in final:
                    eng.wait_ge(o.sem, o.count)

        with nc.Block() as block:
            @block.tensor
            def _(e):
                run("pe", e)

            @block.scalar
            def _(e):
                run("act", e)

            @block.vector
            def _(e):
                run("dve", e)

            @block.gpsimd
            def _(e):
                run("pool", e)

            @block.sync
            def _(e):
                run("sp", e)


class Arena:
    def __init__(self, ap, width):
        self.ap, self.width, self.off, self.top = ap, width, 0, width

    def reset(self):
        self.off, self.top = 0, self.width

    def alloc_top(self, n, dt):
        w = n if dt == F32 else (n + 1) // 2
        assert self.top - w >= self.off, ("arena overflow (top)", self.off, w, self.top)
        self.top -= w
        a = self.ap[:, self.top:self.top + w]
        if dt != F32:
            a = a.bitcast(dt)[:, 0:n]
        return a

    def alloc(self, n, dt):
        w = n if dt == F32 else (n + 1) // 2
        assert self.off + w <= self.top, ("arena overflow", self.off, w, self.top)
        a = self.ap[:, self.off:self.off + w]
        self.off += w
        if dt != F32:
            a = a.bitcast(dt)[:, 0:n]
        return a


def make_cst():
    c = np.zeros((128, NCST), np.float32)
    p = np.arange(128)
    c[:, C_ID:C_ID + 128] = np.eye(128)
    c[:, C_TRI:C_TRI + 128] = (p[:, None] <= p[None, :])
    c[127, C_SEL:C_SEL + 128] = 1.0
    c[:, C_MASK:C_MASK + 128] = np.where(p[:, None] > p[None, :], NEG, 0.0)
    c[0:64, C_WE:C_WE + 64] = 1.0 / 64
    c[64, C_WE:C_WE + 64] = EPS
    c[64:128, C_WO + 64:C_WO + 128] = 1.0 / 64
    c[63, C_WO + 64:C_WO + 128] = EPS
    c[0:64, C_WG:C_WG + 64] = 1.0 / 64
    c[64:128, C_WG + 64:C_WG + 128] = 1.0 / 64
    c[:, C_COLS + 0] = (p < 16)
    c[:, C_COLS + 1] = np.where(p < 16, 0.0, NEG)
    c[:, C_COLS + 2] = -1.0 * (p < 16)
    c[:, C_COLS + 3] = (p == 63)
    c[:, C_COLS + 4] = 1.0
    c[:, C_COLS + 5] = 0.0
    return c


def build_program(NB):
    NG = NB // 4
    T = 16 + 128 * NB
    SEQ = 128 * NB
    NT = NB + 1
    NF = NT * 8

    nc = bass.Bass("TRN2", target_bir_lowering=False)

    def dram(name, shape, kind="ExternalInput"):
        return nc.dram_tensor(name, shape, F32, kind=kind).ap()

    x_d = dram("x", [SEQ, D])
    meta_d = dram("meta", [16, D])
    ng_d = dram("norm_g", [1, D])
    win_d = dram("w_in", [D, DIN])
    bfr_d = dram("bf_rep", [1, NF])
    cw_d = dram("cwT", [128, 12])
    ag_d = dram("ag", [128, 4])
    cg_d = dram("cg", [128, 4])
    wout_d = dram("w_out", [D, D])
    fg_d = dram("fg", [1, D])
    cst_d = dram("cst", [128, NCST])
    ones_d = dram("ones_bf", [1, T // 2])
    y_d = dram("y", [SEQ, D], kind="ExternalOutput")

    win_v = win_d.rearrange("(kc p) e -> p kc e", p=128)
    wout_v = wout_d.rearrange("(kc p) e -> p kc e", p=128)

    S = Sched()

    def OP(eng, name, reads, writes, *args, **kw):
        return S.op(eng, lambda e: getattr(e, name)(*args, **kw), reads=reads, writes=writes)

    def DMA(key, out, in_, reads, writes):
        return S.op("sp", lambda e: e.dma_start(out=out, in_=in_), reads=reads, writes=writes, dma_key=key)

    def MM(out, lhsT, rhs, start, stop, reads, writes):
        return S.op("pe", lambda e: e.matmul(out, lhsT=lhsT, rhs=rhs, start=start, stop=stop),
                    reads=reads, writes=writes)

    def ACT(out, in_, func, reads, writes, **kw):
        return S.op("act", lambda e: e.activation(out=out, in_=in_, func=func, **kw), reads=reads, writes=writes)

    with ExitStack() as es:
        def sb(name, shape, dt):
            return es.enter_context(nc.sbuf_tensor("sb_" + name, shape, dt))

        cst = sb("cst", [128, NCST], F32); cst_b = Buf()
        identb = sb("identb", [128, 128], BF16); identb_b = Buf()
        maskb = sb("maskb", [128, 128], BF16); maskb_b = Buf()
        uT = sb("uT", [128, KC, T], BF16)
        uT_t = [Buf() for _ in range(NB + 1)]

        def ub(p0, n):
            out = []
            if p0 < 16:
                out.append(uT_t[0])
            lo = max(p0, 16)
            hi = p0 + n
            if hi > lo:
                out += uT_t[1 + (lo - 16) // 128:1 + (hi - 16 + 127) // 128]
            return out
        mixA = sb("mixA", [128, 4, SEQ], BF16); mixA_b = [[Buf() for _ in range(NG)] for _ in range(4)]
        c_all = sb("c_all", [128, NF], F32); c_all_b = Buf()
        negc = sb("negc", [128, NF], F32); negc_b = Buf()
        cw_t = sb("cw_t", [128, 12], F32); ag_t = sb("ag_t", [128, 4], F32); cg_t = sb("cg_t", [128, 4], F32)
        small_b = Buf()
        halo = sb("halo", [128, 4, 2], F32); halo_b = [Buf() for _ in range(4)]
        stat = sb("stat", [128, 8], F32)
        stat_b = [Buf(), Buf()]
        UW = 26900
        U = sb("U", [128, UW], F32)
        ar = Arena(U[:, :], UW)

        def ps(name):
            return es.enter_context(nc.psum_tensor("ps_" + name, [128, 512], F32))
        A = [ps("A0"), ps("A1")]; A_b = [Buf(), Buf()]
        SB = [ps("S0"), ps("S1"), ps("S2")]; SB_b = [Buf(), Buf(), Buf()]
        OB = [ps("O0"), ps("O1")]; OB_b = [Buf(), Buf()]
        X = ps("X"); X_b = Buf()

        ident_f = cst[:, C_ID:C_ID + 128]
        col = lambda i: cst[:, C_COLS + i:C_COLS + i + 1]

        DMA("cst", cst[:], cst_d[:, :], [], [cst_b])
        DMA("small", cw_t[:], cw_d[:, :], [], [small_b])
        DMA("small", ag_t[:], ag_d[:, :], [], [small_b])
        DMA("small", cg_t[:], cg_d[:, :], [], [small_b])
        OP("dve", "tensor_copy", [cst_b], [identb_b], out=identb[:], in_=ident_f)
        OP("dve", "tensor_copy", [cst_b], [maskb_b], out=maskb[:], in_=cst[:, C_MASK:C_MASK + 128])

        ar.reset()
        wbf = [ar.alloc_top(KC * 512, BF16).rearrange("p (k e) -> p k e", e=512) for _ in range(2)]
        wbf_b = [[Buf() for _ in range(4)] for _ in range(2)]
        KEKO = ar.alloc_top(2 * T, BF16)
        KE, KO = KEKO[:, 0:T], KEKO[:, T:2 * T]; K_b = Buf()
        KEO = [KE, KO]
        vaug = ar.alloc_top(NT * 256, BF16).rearrange("p (j c) -> p j c", c=256); vaug_b = Buf()
        CP = [ar.alloc_top(4 * 128, BF16).rearrange("p (t m) -> p t m", m=128) for _ in range(2)]
        CP_b = [Buf(), Buf()]
        wst = [ar.alloc_top(KC * 128, F32).rearrange("p (k e) -> p k e", e=128) for _ in range(2)]
        wst_b = [Buf(), Buf()]
        wob = ar.alloc(KC * D, BF16).rearrange("p (k e) -> p k e", e=D)
        wob_b = [Buf() for _ in range(8)]
        wfb = ar.alloc(KC * 8, BF16).rearrange("p (k e) -> p k e", e=8); wfb_b = Buf()
        bfr = ar.alloc(NF, F32); bfr_b = Buf()
        fblk = ar.alloc(max(4 * NF, D), F32)
        fx, fa, fm, car = [fblk[:, k_ * NF:(k_ + 1) * NF] for k_ in range(4)]
        f_b = Buf()
        shared_off = ar.off

        zc, oc_ = col(5), col(4)
        vflat = vaug.rearrange("p j c -> p (j c)")
        init_ops = [
            lambda: ACT(KEKO.bitcast(F32), zc.to_broadcast([128, T]), AF.Copy, [cst_b], [K_b]),
            lambda: DMA("ones", KE[64:65, :].bitcast(F32), ones_d[:, :], [K_b], [K_b]),
            lambda: DMA("ones", KO[63:64, :].bitcast(F32), ones_d[:, :], [K_b], [K_b]),
            lambda: ACT(vflat.bitcast(F32), zc.to_broadcast([128, NT * 128]), AF.Copy, [cst_b], [vaug_b]),
            lambda: ACT(vaug[:, :, 64:65], oc_.to_broadcast([128, NT]).rearrange("p (j o) -> p j o", o=1), AF.Copy,
                        [cst_b], [vaug_b]),
            lambda: ACT(vaug[:, :, 191:192], oc_.to_broadcast([128, NT]).rearrange("p (j o) -> p j o", o=1), AF.Copy,
                        [cst_b], [vaug_b]),
            lambda: ACT(CP[0].rearrange("p t m -> p (t m)").bitcast(F32), zc.to_broadcast([128, 256]), AF.Copy,
                        [cst_b], [CP_b[0]]),
            lambda: ACT(CP[1].rearrange("p t m -> p (t m)").bitcast(F32), zc.to_broadcast([128, 256]), AF.Copy,
                        [cst_b], [CP_b[1]]),
        ]
        vaug4 = vaug.rearrange("p j (b d) -> p j b d", d=64)

        wst_cnt = [0]

        wtasks = []
        wpending = []

        def load_w(dst, dst_b, src, extra_w=(), defer=False):
            if defer:
                wtasks.append((dst, dst_b, src, extra_w))
                return
            s = wst_cnt[0] % 2
            wst_cnt[0] += 1
            DMA("wst%d" % s, wst[s][:, :, :], src, [], [wst_b[s]])
            OP("dve", "tensor_copy", [wst_b[s]], [dst_b] + list(extra_w), out=dst, in_=wst[s][:, :, :])

        def wtask_step(n=1):
            while wpending:
                s_, dst, dst_b, extra_w = wpending.pop(0)
                OP("dve", "tensor_copy", [wst_b[s_]], [dst_b] + list(extra_w), out=dst, in_=wst[s_][:, :, :])
            for _ in range(n):
                if not wtasks:
                    break
                dst, dst_b, src, extra_w = wtasks.pop(0)
                s_ = wst_cnt[0] % 2
                wst_cnt[0] += 1
                DMA("wst%d" % s_, wst[s_][:, :, :], src, [], [wst_b[s_]])
                wpending.append((s_, dst, dst_b, extra_w))

        def wtask_flush():
            while wtasks or wpending:
                wtask_step(2)

        def load_pair_w(c, defer=False):
            s = c % 2
            for i, c0 in enumerate((128 * c, 512 + 128 * c, 1024 + 128 * c, 1544 + 128 * c)):
                load_w(wbf[s][:, :, 128 * i:128 * i + 128], wbf_b[s][i], win_v[:, :, c0:c0 + 128], defer=defer)

        NXS = 4
        xt = [ar.alloc(D, F32) for _ in range(3)] + [fblk[:, 0:D]]; xt_b = [Buf() for _ in range(NXS)]
        xb = [ar.alloc(D, BF16) for _ in range(2)]; xb_b = [Buf(), Buf()]
        junk = ar.alloc(D, BF16); junk_b = Buf()
        grep = ar.alloc(D, F32); grep_b = Buf()
        DMA("grep", grep, ng_d.partition_broadcast(128), [], [grep_b])
        Xb16 = X[:].bitcast(BF16)

        def tile_stats(src, src_b, slot):
            st, st_b = stat[:, 4 * slot:4 * slot + 4], stat_b[slot]
            ACT(junk, src, AF.Square, [src_b], [junk_b, st_b], accum_out=st[:, 0:1])
            ACT(st[:, 1:2], st[:, 0:1], AF.Ln, [st_b], [st_b], scale=1.0 / D, bias=EPS)
            ACT(st[:, 2:3], st[:, 1:2], AF.Exp, [st_b], [st_b], scale=-0.5)
            return st[:, 2:3], st_b

        def kcols(j):
            return (0, 128) if j == 0 else (16 + 128 * (j - 1), 128)

        a_cnt = [0]

        def next_A():
            i = a_cnt[0] % 2
            a_cnt[0] += 1
            return A[i], A_b[i]

        def pair_w(c):
            ws = c % 2
            return [wbf[ws][:, :, 128 * i:128 * i + 128] for i in range(4)], wbf_b[ws]

        def kv_inproj_part(c, g):
            (wq, wk, wv, wz), (wq_b, wk_b, wv_b, wz_b) = pair_w(c)
            ranges = ([(0, 16)] if g == 0 else []) + [(16 + 512 * g, 512)]
            for (p0, n) in ranges:
                a, a_b = next_A()
                for kc in range(KC):
                    MM(a[:, 0:n], wk[:, kc, :], uT[:, kc, p0:p0 + n], kc == 0, kc == KC - 1, [wk_b] + ub(p0, n), [a_b])
                OP("dve", "tensor_copy", [a_b], [K_b], out=KE[0:64, p0:p0 + n], in_=a[0:64, 0:n])
                OP("dve", "tensor_copy", [a_b], [K_b], out=KO[64:128, p0:p0 + n], in_=a[64:128, 0:n])
            tiles = ([0] if g == 0 else []) + list(range(4 * g + 1, 4 * g + 5))
            for jb in range(0, len(tiles), 4):
                tl = tiles[jb:jb + 4]
                nt = len(tl)
                a, a_b = next_A()
                for t, j in enumerate(tl):
                    p0, n = kcols(j)
                    for kc in range(KC):
                        MM(a[:, 128 * t:128 * t + 128], uT[:, kc, p0:p0 + n], wv[:, kc, :], kc == 0, kc == KC - 1,
                           [wv_b] + ub(p0, n), [a_b])
                OP("dve", "tensor_copy", [a_b], [vaug_b], out=vaug4[:, tl[0]:tl[0] + nt, 0:4:3, :],
                   in_=a[:, 0:128 * nt].rearrange("p (t b d) -> p t b d", b=2, d=64))

        def kv_inproj(c):
            for g in range(NG):
                kv_inproj_part(c, g)

        TB = [(X, X_b), (SB[0], SB_b[0])]
        TB16 = [X[:].bitcast(BF16), SB[0][:].bitcast(BF16)]

        def a_load(ti):
            s3 = ti % NXS
            if ti == 0:
                OP("dve", "memset", [], [xt_b[s3]], xt[s3], 0.0)
                DMA("xt%d" % s3, xt[s3][0:16, :], meta_d[:, :], [], [xt_b[s3]])
            else:
                DMA("xt%d" % s3, xt[s3], x_d[(ti - 1) * 128:ti * 128, :], [], [xt_b[s3]])

        def a_norm(ti):
            s3, s2 = ti % NXS, ti % 2
            rs, rs_b = tile_stats(xt[s3], xt_b[s3], s2)
            OP("dve", "scalar_tensor_tensor", [xt_b[s3], rs_b, grep_b], [xb_b[s2]],
               out=xb[s2], in0=xt[s3], scalar=rs, in1=grep, op0=ALU.mult, op1=ALU.mult)

        for ti in range(min(NXS, NT)):
            a_load(ti)
        DMA("wf", wst[0][:, :, 0:8], win_v[:, :, 1536:1544], [], [wst_b[0]])
        DMA("bfr", bfr, bfr_d.partition_broadcast(128), [], [bfr_b])
        OP("dve", "tensor_copy", [wst_b[0]], [wfb_b], out=wfb, in_=wst[0][:, :, 0:8])
        a_norm(0)
        for ti in range(NT):
            s2 = ti % 2
            tb, tb_b = TB[s2]
            tb16 = TB16[s2]
            for kc in range(KC):
                S.op("pe", lambda e, kc=kc, s2=s2, tb16=tb16: e.transpose(out=tb16[:, kc * 128:(kc + 1) * 128],
                                                                           in_=xb[s2][:, kc * 128:(kc + 1) * 128],
                                                                           identity=identb[:]),
                     reads=[xb_b[s2], identb_b], writes=[tb_b])
            if ti + 1 < NT:
                a_norm(ti + 1)
            if ti >= 1 and init_ops:
                init_ops.pop(0)()
            if ti == 0:
                OP("dve", "tensor_copy", [tb_b], [uT_t[0]], out=uT[:, :, 0:16],
                   in_=tb16.rearrange("p (k t) -> p k t", t=128)[:, :, 0:16])
            else:
                p0 = 16 + 128 * (ti - 1)
                OP("dve", "tensor_copy", [tb_b], [uT_t[ti]], out=uT[:, :, p0:p0 + 128],
                   in_=tb16.rearrange("p (k t) -> p k t", t=128))
            if ti + NXS < NT:
                a_load(ti + NXS)
            if ti == 1:
                load_pair_w(0)
            if ti >= 4 and ti % 4 == 0:
                kv_inproj_part(0, ti // 4 - 1)

        while init_ops:
            init_ops.pop(0)()

        for j in range(NT):
            p0, n = kcols(j)
            for kc in range(KC):
                MM(X[:, 8 * j:8 * j + 8], uT[:, kc, p0:p0 + n], wfb[:, kc, :], kc == 0, kc == KC - 1,
                   ub(p0, n) + [wfb_b], [X_b])
        OP("dve", "tensor_tensor", [X_b, bfr_b], [f_b], out=fx, in0=X[:, 0:NF], in1=bfr, op=ALU.add)
        ACT(fa, fx, AF.Abs, [f_b], [f_b])
        ACT(fa, fa, AF.Exp, [f_b], [f_b], scale=-1.0)
        ACT(fa, fa, AF.Ln, [f_b], [f_b], bias=1.0)
        OP("dve", "tensor_scalar_min", [f_b], [f_b], out=fm, in0=fx, scalar1=0.0)
        OP("dve", "tensor_tensor", [f_b], [f_b], out=fm, in0=fm, in1=fa, op=ALU.subtract)
        OP("dve", "tensor_scalar", [f_b, cst_b], [f_b], out=fm[:, 0:8], in0=fm[:, 0:8], scalar1=col(0),
           scalar2=None, op0=ALU.mult)
        cw_sb, tot_sb = fx, fa
        MM(X[:, 0:NF], cst[:, C_TRI:C_TRI + 128], fm, True, True, [f_b, cst_b], [X_b])
        OP("dve", "tensor_copy", [X_b], [f_b], out=cw_sb, in_=X[:, 0:NF])
        MM(X[:, 0:NF], cst[:, C_SEL:C_SEL + 128], cw_sb, True, True, [f_b, cst_b], [X_b])
        OP("dve", "tensor_copy", [X_b], [f_b], out=tot_sb, in_=X[:, 0:NF])
        OP("dve", "memset", [], [f_b], car[:, 0:8], 0.0)
        for j in range(1, NT):
            OP("dve", "tensor_tensor", [f_b], [f_b], out=car[:, 8 * j:8 * j + 8], in0=car[:, 8 * j - 8:8 * j],
               in1=tot_sb[:, 8 * j - 8:8 * j], op=ALU.add)
        OP("dve", "tensor_tensor", [f_b], [c_all_b], out=c_all[:], in0=cw_sb, in1=car, op=ALU.add)
        OP("dve", "tensor_scalar", [c_all_b], [negc_b], out=negc[:, 8:NF], in0=c_all[:, 8:NF], scalar1=-1.0,
           scalar2=None, op0=ALU.mult)
        OP("dve", "tensor_scalar", [c_all_b, cst_b], [negc_b], out=negc[:, 0:8], in0=c_all[:, 0:8],
           scalar1=col(2), scalar2=col(1), op0=ALU.mult, op1=ALU.add)

        ar.off = shared_off
        QE = [ar.alloc(512, BF16) for _ in range(2)]; QO = [ar.alloc(512, BF16) for _ in range(2)]
        Q_b = [Buf(), Buf()]
        PT = [ar.alloc(512, BF16) for _ in range(4)]; PT_b = [Buf() for _ in range(4)]
        ez = ar.alloc(512, F32); ez_b = Buf()
        sz = [ar.alloc(512, F32) for _ in range(2)]; sz_b = [Buf(), Buf()]
        sqE = ar.alloc(512, F32); sqO = ar.alloc(512, F32); sq_b = Buf()
        rstd = ar.alloc(512, F32); rstd_b = Buf()
        lnr = rstd
        tmp, tmp_b = ez, ez_b

        kr = [(0, 16)] + [(16 + 512 * g, 512) for g in range(NG)]
        s_cnt = [0]
        q_cnt = [0]
        OcE = ar.alloc(512, F32); OcO = ar.alloc(512, F32); Oc_b = [Buf(), Buf()]
        Oc = [OcE, OcO]

        def prologue(c, G):
            (wq, wk, wv, wz), (wq_b, wk_b, wv_b, wz_b) = pair_w(c)
            q0 = 16 + 512 * G
            qs = q_cnt[0] % 2
            q_cnt[0] += 1
            a, a_b = next_A()
            for kc in range(KC):
                MM(a[:, :], wq[:, kc, :], uT[:, kc, q0:q0 + 512], kc == 0, kc == KC - 1, [wq_b] + ub(q0, 512), [a_b])
                yield qs
            OP("dve", "tensor_copy", [a_b], [Q_b[qs]], out=QE[qs][0:64, :], in_=a[0:64, :])
            OP("dve", "tensor_copy", [a_b], [Q_b[qs]], out=QO[qs][64:128, :], in_=a[64:128, :])
            OP("dve", "tensor_scalar", [c_all_b], [CP_b[qs]], out=CP[qs][:, :, 63:65],
               in0=c_all[:, 8 * (4 * G + 1):8 * (4 * G + 5)].rearrange("p (t h) -> p t h", h=8)[:, :, 2 * c:2 * c + 2],
               scalar1=8.0, scalar2=None, op0=ALU.mult)
            for t in range(4):
                MM(X[:, 128 * t:128 * t + 128], CP[qs][:, t, :], identb[:], True, True, [CP_b[qs], identb_b], [X_b])
            yield qs
            OP("dve", "tensor_copy", [X_b], [Q_b[qs]], out=QE[qs][64:128, :], in_=X[64:128, :])
            OP("dve", "tensor_copy", [X_b], [Q_b[qs]], out=QO[qs][0:64, :], in_=X[0:64, :])
            a, a_b = next_A()
            for kc in range(KC):
                MM(a[:, :], wz[:, kc, :], uT[:, kc, q0:q0 + 512], kc == 0, kc == KC - 1, [wz_b] + ub(q0, 512), [a_b])
                yield qs
            ACT(ez, a[:, :], AF.Exp, [a_b], [ez_b], scale=-1.0)
            OP("dve", "tensor_scalar_add", [ez_b], [ez_b], out=ez, in0=ez, scalar1=1.0)
            OP("dve", "reciprocal", [ez_b], [ez_b], out=ez, in_=ez)
            OP("dve", "tensor_tensor", [ez_b, a_b], [sz_b[qs]], out=sz[qs], in0=a[:, :], in1=ez, op=ALU.mult)
            yield qs

        def run_all(gen):
            qs = None
            for qs in gen:
                pass
            return qs

        def make_steps(c, G, qs):
            steps = []
            LAG = 2
            nblk = 4 * G + 5
            blocks = []
            for par in range(2):
                for j in range(nblk):
                    r = j - (4 * G + 1)
                    blocks.append((par, j, 128 * r if r > 0 else 0, r >= 0))
            slots = {}
            tot = len(blocks)
            for idx in range(tot + LAG):
                def step(idx=idx):
                    if idx < tot:
                        par, j, c0, diag = blocks[idx]
                        Kt = KEO[par]
                        Qt = (QE if par == 0 else QO)[qs]
                        hb = 2 * c + (1 - par)
                        si = s_cnt[0] % 3
                        pi = s_cnt[0] % 4
                        s_cnt[0] += 1
                        slots[idx] = pi
                        p0, n = kcols(j)
                        MM(SB[si][:, c0:512], Kt[:, p0:p0 + 128], Qt[:, c0:512], True, not diag,
                           [K_b, Q_b[qs]], [SB_b[si]])
                        if diag:
                            MM(SB[si][:, c0:c0 + 128], identb[:], maskb[:], False, True,
                               [identb_b, maskb_b], [SB_b[si]])
                        ACT(PT[pi][:, c0:512], SB[si][:, c0:512], AF.Exp, [SB_b[si], negc_b], [PT_b[pi]],
                            bias=negc[:, 8 * j + hb:8 * j + hb + 1], scale=0.125)
                    if idx >= LAG:
                        par, j, c0, diag = blocks[idx - LAG]
                        O, O_b = OB[par], OB_b[par]
                        pi = slots[idx - LAG]
                        MM(O[:, c0:512], vaug[:, j, 128 * par:128 * par + 128], PT[pi][:, c0:512],
                           j == 0, j == nblk - 1, [vaug_b, PT_b[pi]], [O_b])
                        if j == nblk - 1:
                            OP("dve", "tensor_copy", [O_b], [Oc_b[par]], out=Oc[par], in_=O[:, :])
                steps.append(step)
            return steps

        def post(c, G, qs):
            OP("dve", "tensor_tensor", [Oc_b[0]], [sq_b], out=sqE, in0=OcE, in1=OcE, op=ALU.mult)
            OP("dve", "tensor_tensor", [Oc_b[1]], [sq_b], out=sqO, in0=OcO, in1=OcO, op=ALU.mult)
            yield
            for h0 in (0, 256):
                MM(X[:, h0:h0 + 256], cst[:, C_WE:C_WE + 128], sqE[:, h0:h0 + 256], True, False, [cst_b, sq_b], [X_b])
                yield
                MM(X[:, h0:h0 + 256], cst[:, C_WO:C_WO + 128], sqO[:, h0:h0 + 256], False, True, [cst_b, sq_b], [X_b])
                yield
            ACT(lnr, X[:, :], AF.Ln, [X_b], [rstd_b])
            ACT(rstd, lnr, AF.Exp, [rstd_b], [rstd_b], scale=-0.5)
            OP("dve", "tensor_tensor", [Oc_b[0], rstd_b], [tmp_b], out=tmp[0:64, :], in0=OcE[0:64, :],
               in1=rstd[0:64, :], op=ALU.mult)
            OP("dve", "tensor_tensor", [Oc_b[1], rstd_b], [tmp_b], out=tmp[64:128, :], in0=OcO[64:128, :],
               in1=rstd[64:128, :], op=ALU.mult)
            OP("dve", "scalar_tensor_tensor", [tmp_b, sz_b[qs], small_b], [mixA_b[c][G]],
               out=mixA[:, c, 512 * G:512 * G + 512], in0=tmp, scalar=ag_t[:, c:c + 1], in1=sz[qs],
               op0=ALU.mult, op1=ALU.mult)

        seq = [(c, G) for c in range(4) for G in range(NG)]
        chunk_order = [4 * blk + i for i in range(4) for blk in (1, 2, 0, 3)]
        pre_chunks = chunk_order[0:8] if NG > 2 else []
        wcv_b = [Buf() for _ in range(16)]
        wcvc = {}
        for k_, n__ in enumerate(pre_chunks):
            flat = wbf[k_ // 4].rearrange("p k e -> p (k e)")
            wcvc[n__] = flat[:, 1024 * (k_ % 4):1024 * (k_ % 4) + 1024].rearrange("p (k e) -> p k e", e=128)
        pending_post = None
        qs_next = run_all(prologue(0, 0))
        for n_, (c, G) in enumerate(seq):
            qs = qs_next
            if G == 0:
                if c > 0:
                    kv_inproj(c)
                if c + 1 < 4:
                    load_pair_w(c + 1, defer=True)
                if c == 1:
                    for w_ in range(8):
                        load_w(wob[:, :, 128 * w_:128 * w_ + 128], wob_b[w_], wout_v[:, :, 128 * w_:128 * w_ + 128],
                               defer=True)
            if c == 3 and G == 0:
                for n__ in pre_chunks[0:4]:
                    load_w(wcvc[n__], wcv_b[n__], win_v[:, :, 2056 + 128 * n__:2056 + 128 * n__ + 128],
                           extra_w=wbf_b[0], defer=True)
            if c == 3 and G == NG - 1:
                for n__ in pre_chunks[4:8]:
                    load_w(wcvc[n__], wcv_b[n__], win_v[:, :, 2056 + 128 * n__:2056 + 128 * n__ + 128],
                           extra_w=wbf_b[1], defer=True)
            wtask_step(2)
            steps = make_steps(c, G, qs)
            hook_post = min(10, 4 * G + 1)
            k_guard = 4 * G + 6
            if G == NG - 1 and n_ + 1 < len(seq):
                wtask_flush()
            gen = prologue(*seq[n_ + 1]) if n_ + 1 < len(seq) else None
            post_gen = post(*pending_post) if pending_post is not None else None
            pending_post = None
            for k_, st in enumerate(steps):
                if k_ == k_guard and post_gen is not None:
                    run_all(post_gen)
                    post_gen = None
                st()
                if k_ % 16 == 12:
                    wtask_step(1)
                if post_gen is not None:
                    if k_ >= hook_post:
                        try:
                            next(post_gen)
                        except StopIteration:
                            post_gen = None
                elif gen is not None and k_ > hook_post:
                    try:
                        qs_next = next(gen)
                    except StopIteration:
                        gen = None
            if post_gen is not None:
                run_all(post_gen)
            if gen is not None:
                r_ = run_all(gen)
                if r_ is not None:
                    qs_next = r_
            pending_post = (c, G, qs)
        run_all(post(*pending_post))
        wtask_flush()
        wtask_flush()
        S.barrier()

        ar.reset()
        wob2 = ar.alloc(KC * D, BF16)
        wst = [ar.alloc(KC * 128, F32).rearrange("p (k e) -> p k e", e=128) for _ in range(3)]
        wst_b = [Buf() for _ in range(3)]
        ar.top = UW - 2 * (KC * 512 // 2)
        for n__ in chunk_order:
            if n__ not in wcvc:
                wcvc[n__] = ar.alloc(KC * 128, BF16).rearrange("p (k e) -> p k e", e=128)
        xt = [ar.alloc(D, F32) for _ in range(3)]; xt_b = [Buf() for _ in range(3)]
        junk = ar.alloc(D, BF16); junk_b = Buf()
        fgrep = ar.alloc(D, F32); fgrep_b = Buf()
        ymix = [ar.alloc(4 * 512, BF16).rearrange("p (i t) -> p i t", t=512) for _ in range(2)]
        ymix_b = [Buf(), Buf()]
        mc = ar.alloc(4 * 16, F32).rearrange("p (i t) -> p i t", t=16); mc_b = Buf()
        NSL = 2
        b1 = [ar.alloc(512, F32) for _ in range(NSL)]; b1_b = [Buf() for _ in range(NSL)]
        b2 = [ar.alloc(514, F32) for _ in range(NSL)]; b2_b = [Buf() for _ in range(NSL)]
        b3 = [ar.alloc(512, F32) for _ in range(NSL)]; b3_b = [Buf() for _ in range(NSL)]
        b4 = [ar.alloc(512, F32) for _ in range(NSL)]; b4_b = [Buf() for _ in range(NSL)]

        DMA("fgrep", fgrep, fg_d.partition_broadcast(128), [], [fgrep_b])
        cast_cnt = 0
        for n_ in chunk_order:
            if n_ in pre_chunks:
                continue
            sl = cast_cnt % 3
            DMA("wst%d" % sl, wst[sl][:, :, :], win_v[:, :, 2056 + 128 * n_:2056 + 128 * n_ + 128], [], [wst_b[sl]])
            if cast_cnt % 2 == 0:
                OP("dve", "tensor_copy", [wst_b[sl]], [wcv_b[n_]], out=wcvc[n_], in_=wst[sl][:, :, :])
            else:
                ACT(wcvc[n_], wst[sl][:, :, :], AF.Copy, [wst_b[sl]], [wcv_b[n_]])
            cast_cnt += 1

        A5 = [A[0], A[1], SB[0], SB[1], SB[2]]; A5_b = [A_b[0], A_b[1], SB_b[0], SB_b[1], SB_b[2]]

        def next_A5():
            i = a_cnt[0] % 5
            a_cnt[0] += 1
            return A5[i], A5_b[i]

        def conv_w(k, i):
            return cw_t[:, 4 * k + i:4 * k + i + 1]

        def meta_halo(i):
            a, a_b = next_A5()
            for kc in range(KC):
                MM(a[:, 0:16], wcvc[4 + i][:, kc, :], uT[:, kc, 0:16], kc == 0, kc == KC - 1,
                   [wcv_b[4 + i], uT_t[0]], [a_b])
            ACT(mc[:, i, :], a[:, 0:16], AF.Copy, [a_b], [mc_b])
            a2, a2_b = next_A5()
            for kc in range(KC):
                MM(a2[:, 0:16], wcvc[8 + i][:, kc, :], uT[:, kc, 0:16], kc == 0, kc == KC - 1,
                   [wcv_b[8 + i], uT_t[0]], [a2_b])
            OP("dve", "tensor_tensor", [a2_b, mc_b], [halo_b[i]], out=halo[:, i, :], in0=a2[:, 14:16],
               in1=mc[:, i, 14:16], op=ALU.mult)

        chain_cnt = [0]

        def phase1(G, i):
            q0 = 16 + 512 * G
            sl = chain_cnt[0] % NSL
            chain_cnt[0] += 1

            def inproj(blk):
                a, a_b = next_A5()
                ncol = 128 * (4 * blk + i)
                for kc in range(KC):
                    MM(a[:, :], wcvc[4 * blk + i][:, kc, :], uT[:, kc, q0:q0 + 512], kc == 0, kc == KC - 1,
                       [wcv_b[4 * blk + i]] + ub(q0, 512), [a_b])
                return a, a_b
            if G == 0:
                meta_halo(i)
            aC, aC_b = inproj(1)
            ACT(b1[sl], aC[:, :], AF.Copy, [aC_b], [b1_b[sl]])
            aX, aX_b = inproj(2)
            OP("pool", "tensor_copy", [halo_b[i]], [b2_b[sl]], out=b2[sl][:, 0:2], in_=halo[:, i, :])
            OP("dve", "tensor_tensor", [aX_b, b1_b[sl]], [b2_b[sl]], out=b2[sl][:, 2:514], in0=aX[:, :], in1=b1[sl],
               op=ALU.mult)
            OP("pool", "tensor_copy", [b2_b[sl]], [halo_b[i]], out=halo[:, i, :], in_=b2[sl][:, 512:514])
            ACT(b3[sl], b2[sl][:, 2:514], AF.Copy, [b2_b[sl], small_b], [b3_b[sl]], scale=conv_w(2, i))
            OP("dve", "scalar_tensor_tensor", [b2_b[sl], b3_b[sl], small_b], [b3_b[sl]], out=b3[sl], in0=b2[sl][:, 1:513],
               scalar=conv_w(1, i), in1=b3[sl], op0=ALU.mult, op1=ALU.add)
            OP("dve", "scalar_tensor_tensor", [b2_b[sl], b3_b[sl], small_b], [b3_b[sl]], out=b3[sl], in0=b2[sl][:, 0:512],
               scalar=conv_w(0, i), in1=b3[sl], op0=ALU.mult, op1=ALU.add)
            aZ, aZ_b = inproj(3)
            ACT(b4[sl], aZ[:, :], AF.Exp, [aZ_b], [b4_b[sl]], scale=-1.0)
            ACT(b4[sl], b4[sl], AF.Ln, [b4_b[sl]], [b4_b[sl]], bias=1.0)
            ACT(b4[sl], b4[sl], AF.Exp, [b4_b[sl]], [b4_b[sl]], scale=-1.0)
            OP("dve", "tensor_tensor", [aZ_b, b4_b[sl]], [b4_b[sl]], out=b4[sl], in0=aZ[:, :], in1=b4[sl], op=ALU.mult)
            aB, aB_b = inproj(0)
            OP("dve", "tensor_tensor", [aB_b, b3_b[sl]], [b3_b[sl]], out=b3[sl], in0=aB[:, :], in1=b3[sl], op=ALU.mult)
            ACT(b1[sl], b3[sl], AF.Square, [b3_b[sl]], [b1_b[sl]])
            return (G, i, sl)

        def phase2(state):
            G, i, sl = state
            ys = G % 2
            MM(X[:, :], cst[:, C_WG:C_WG + 128], b1[sl], True, True, [cst_b, b1_b[sl]], [X_b])
            ACT(b2[sl][:, 0:512], X[:, :], AF.Ln, [X_b], [b2_b[sl]], bias=EPS)
            ACT(b2[sl][:, 0:512], b2[sl][:, 0:512], AF.Exp, [b2_b[sl]], [b2_b[sl]], scale=-0.5)
            OP("dve", "tensor_tensor", [b3_b[sl], b2_b[sl]], [b3_b[sl]], out=b3[sl], in0=b3[sl], in1=b2[sl][:, 0:512],
               op=ALU.mult)
            OP("dve", "scalar_tensor_tensor", [b3_b[sl], b4_b[sl], small_b], [ymix_b[ys]], out=ymix[ys][:, i, :],
               in0=b3[sl], scalar=cg_t[:, i:i + 1], in1=b4[sl], op0=ALU.mult, op1=ALU.mult)

        tt_cnt = [0]

        xl_cnt = [0]
        xslot = {}

        def xload(G, tt):
            s3_ = xl_cnt[0] % 3
            xl_cnt[0] += 1
            xslot[(G, tt)] = s3_
            r0_ = 512 * G + 128 * tt
            DMA("xt%d" % s3_, xt[s3_], x_d[r0_:r0_ + 128, :], [], [xt_b[s3_]])

        def outproj(G, tt, nxt=None):
            r0 = 512 * G + 128 * tt
            ys = G % 2
            if (G, tt) not in xslot:
                xload(G, tt)
            s3 = xslot[(G, tt)]
            st_slot = tt_cnt[0] % 2
            tt_cnt[0] += 1
            for half in range(2):
                pb, pb_b = OB[half], OB_b[half]
                for e_ in range(8):
                    if e_ < 4:
                        lh, lh_b = mixA[:, e_, r0:r0 + 128], mixA_b[e_][G]
                    else:
                        lh, lh_b = ymix[ys][:, e_ - 4, 128 * tt:128 * tt + 128], ymix_b[ys]
                    MM(pb[:, :], lh, wob[:, e_, 512 * half:512 * half + 512], e_ == 0, e_ == 7,
                       [lh_b] + wob_b, [pb_b])
                OP("dve", "tensor_tensor", [pb_b, xt_b[s3]], [xt_b[s3]], out=xt[s3][:, 512 * half:512 * half + 512],
                   in0=pb[:, :], in1=xt[s3][:, 512 * half:512 * half + 512], op=ALU.add)
            st, st_b = stat[:, 4 * st_slot:4 * st_slot + 4], stat_b[st_slot]
            ACT(junk, xt[s3], AF.Square, [xt_b[s3]], [junk_b, st_b], accum_out=st[:, 0:1])
            ACT(st[:, 1:2], st[:, 0:1], AF.Ln, [st_b], [st_b], scale=1.0 / D, bias=EPS)
            ACT(st[:, 2:3], st[:, 1:2], AF.Exp, [st_b], [st_b], scale=-0.5)
            OP("dve", "scalar_tensor_tensor", [xt_b[s3], st_b, fgrep_b], [xt_b[s3]], out=xt[s3], in0=xt[s3],
               scalar=st[:, 2:3], in1=fgrep, op0=ALU.mult, op1=ALU.mult)
            if nxt is not None and nxt not in xslot:
                xload(*nxt)
            o = DMA("y%d" % s3, y_d[r0:r0 + 128, :], xt[s3], [xt_b[s3]], [])
            o.final = True

        def pop_outproj():
            t_ = pend.pop(0)
            outproj(*t_, nxt=(pend[0] if pend else None))

        prev = None
        pend = []
        for G in range(NG):
            for i in range(4):
                st_ = phase1(G, i)
                if prev is not None:
                    phase2(prev)
                prev = st_
                if i == 0 and G > 0:
                    pend.extend((G - 1, tt) for tt in range(4))
                    if len(pend) > 4:
                        pop_outproj()
                elif pend:
                    pop_outproj()
        phase2(prev)
        pend.extend((NG - 1, tt) for tt in range(4))
        while pend:
            pop_outproj()
        S.emit(nc, es)
    return nc


_CACHE = {}


def _host_inputs(NB, meta, norm_g, w_in, b_f, conv_w, attn_norm_g, conv_norm_g, w_out, final_norm_g):
    f32 = np.float32
    w = np.array(w_in[0], dtype=f32, copy=True)
    swap = np.array([1, 0, 3, 2, 5, 4, 7, 6])
    w[:, 1536:1544] = w[:, 1536:1544][:, swap]
    bf = np.asarray(b_f[0], f32)[swap]
    shared = {
        "meta": np.ascontiguousarray(meta, f32),
        "norm_g": np.ascontiguousarray(norm_g[0:1], f32),
        "w_in": np.ascontiguousarray(w),
        "bf_rep": np.ascontiguousarray(np.tile(bf, NB + 1)[None, :], f32),
        "cwT": np.ascontiguousarray(np.asarray(conv_w[0], f32).reshape(3, 4, 128).transpose(2, 0, 1).reshape(128, 12)),
        "ag": np.ascontiguousarray(np.asarray(attn_norm_g[0], f32).reshape(4, 128).T),
        "cg": np.ascontiguousarray(np.asarray(conv_norm_g[0], f32).reshape(4, 128).T),
        "w_out": np.ascontiguousarray(w_out[0], f32),
        "fg": np.ascontiguousarray(np.asarray(final_norm_g, f32)[None, :]),
        "cst": make_cst(),
        "ones_bf": np.full((1, (16 + 128 * NB) // 2), 0x3F803F80, dtype=np.uint32).view(np.float32),
    }
    return shared


def kernel(x, meta, norm_g, w_in, b_f, conv_w, attn_norm_g, conv_norm_g, w_out, final_norm_g):
    x = np.asarray(x, np.float32)
    B, SEQ, _ = x.shape
    NB = SEQ // 128
    if NB not in _CACHE:
        _CACHE[NB] = build_program(NB)
    nc = _CACHE[NB]
    shared = _host_inputs(NB, meta, norm_g, w_in, b_f, conv_w, attn_norm_g, conv_norm_g, w_out, final_norm_g)
    in_maps = [dict(shared, x=np.ascontiguousarray(x[b])) for b in range(B)]
    res = run_bass_kernel_spmd(nc, in_maps, core_ids=list(range(B)))
    return np.stack([np.asarray(res.results[b]["y"], np.float32) for b in range(B)], axis=0)
```

```python
import numpy as np
from contextlib import ExitStack
import concourse.bass as bass
import concourse.mybir as mybir
from concourse.bass_utils import run_bass_kernel_spmd

F32 = mybir.dt.float32
BF16 = mybir.dt.bfloat16
AF = mybir.ActivationFunctionType
ALU = mybir.AluOpType

D = 1024
KC = 8
DIN = 4104
EPS = 1e-6
NEG = -30000.0
C_ID, C_TRI, C_SEL, C_MASK, C_WE, C_WO, C_WG, C_COLS = 0, 128, 256, 384, 512, 640, 768, 896
NCST = 904


class Buf:
    __slots__ = ("name", "last_w", "readers")

    def __init__(self, name=""):
        self.name = name
        self.last_w = None
        self.readers = []


class Op:
    __slots__ = ("eng", "fn", "deps", "signal", "count", "dma_key", "sem", "final", "idx")

    def __init__(self, eng, fn, dma_key=None):
        self.eng = eng
        self.fn = fn
        self.deps = []
        self.signal = False
        self.count = None
        self.dma_key = dma_key
        self.sem = None
        self.final = False


class Sched:
    ENGS = ("pe", "act", "dve", "pool", "sp")

    def __init__(self):
        self.ops = []
        self.last = {}
        self.pending_barrier = {}

    def op(self, eng, fn, reads=(), writes=(), dma_key=None):
        o = Op(eng, fn, dma_key)
        deps = {}
        for b in reads:
            if b.last_w is not None:
                deps[id(b.last_w)] = b.last_w
        for b in writes:
            if b.last_w is not None:
                deps[id(b.last_w)] = b.last_w
            for r in b.readers:
                deps[id(r)] = r
        if eng in self.pending_barrier:
            for d in self.pending_barrier.pop(eng):
                deps[id(d)] = d
        best = {}
        for d in deps.values():
            k = ("dma", d.dma_key) if d.dma_key is not None else d.eng
            if k not in best or best[k].idx < d.idx:
                best[k] = d
        o.deps = list(best.values())
        o.idx = len(self.ops)
        for b in reads:
            b.readers.append(o)
        for b in writes:
            b.last_w = o
            b.readers = []
        self.ops.append(o)
        self.last[("dma", dma_key) if dma_key is not None else eng] = o
        return o

    def barrier(self):
        lasts = list(self.last.values())
        for e in self.ENGS:
            self.pending_barrier[e] = list(self.pending_barrier.get(e, [])) + lasts

    def emit(self, nc, es):
        ops = self.ops
        for o in ops:
            for d in o.deps:
                if d.dma_key is not None:
                    continue
                if d.eng == "pe" and o.eng == "pe" and o.dma_key is None:
                    continue
                d.signal = True
        eng_sem = {e: es.enter_context(nc.semaphore("s_" + e)) for e in ("pe", "act", "dve", "pool")}
        cnt = {e: 0 for e in eng_sem}
        dma_sems, dma_cnt = {}, {}
        for o in ops:
            if o.dma_key is not None:
                if o.dma_key not in dma_sems:
                    dma_sems[o.dma_key] = es.enter_context(nc.semaphore("d_" + str(o.dma_key)))
                    dma_cnt[o.dma_key] = 0
                dma_cnt[o.dma_key] += 16
                o.sem, o.count = dma_sems[o.dma_key], dma_cnt[o.dma_key]
            elif o.signal:
                cnt[o.eng] += 1
                o.sem, o.count = eng_sem[o.eng], cnt[o.eng]
        streams = {e: [o for o in ops if o.eng == e] for e in self.ENGS}
        final = [o for o in ops if o.final]

        def run(ename, eng):
            known = {}
            for o in streams[ename]:
                for d in o.deps:
                    if d.sem is None:
                        continue
                    if d.eng == "pe" and ename == "pe" and d.dma_key is None and o.dma_key is None:
                        continue
                    k = id(d.sem)
                    if known.get(k, 0) >= d.count:
                        continue
                    eng.wait_ge(d.sem, d.count)
                    known[k] = d.count
                ins = o.fn(eng)
                if o.dma_key is not None:
                    ins.then_inc(o.sem, 16)
                elif o.signal:
                    ins.then_inc(o.sem, 1)
            if ename == "sp":
                for o in final:
                    eng.wait_ge(o.sem, o.count)

        with nc.Block() as block:
            @block.tensor
            def _(e):
                run("pe", e)

            @block.scalar
            def _(e):
                run("act", e)

            @block.vector
            def _(e):
                run("dve", e)

            @block.gpsimd
            def _(e):
                run("pool", e)

            @block.sync
            def _(e):
                run("sp", e)


class Arena:
    def __init__(self, ap, width):
        self.ap, self.width, self.off, self.top = ap, width, 0, width

    def reset(self):
        self.off, self.top = 0, self.width

    def alloc_top(self, n, dt):
        w = n if dt == F32 else (n + 1) // 2
        assert self.top - w >= self.off, ("arena overflow (top)", self.off, w, self.top)
        self.top -= w
        a = self.ap[:, self.top:self.top + w]
        if dt != F32:
            a = a.bitcast(dt)[:, 0:n]
        return a

    def alloc(self, n, dt):
        w = n if dt == F32 else (n + 1) // 2
        assert self.off + w <= self.top, ("arena overflow", self.off, w, self.top)
        a = self.ap[:, self.off:self.off + w]
        self.off += w
        if dt != F32:
            a = a.bitcast(dt)[:, 0:n]
        return a


def make_cst():
    c = np.zeros((128, NCST), np.float32)
    p = np.arange(128)
    c[:, C_ID:C_ID + 128] = np.eye(128)
    c[:, C_TRI:C_TRI + 128] = (p[:, None] <= p[None, :])
    c[127, C_SEL:C_SEL + 128] = 1.0
    c[:, C_MASK:C_MASK + 128] = np.where(p[:, None] > p[None, :], NEG, 0.0)
    c[0:64, C_WE:C_WE + 64] = 1.0 / 64
    c[64, C_WE:C_WE + 64] = EPS
    c[64:128, C_WO + 64:C_WO + 128] = 1.0 / 64
    c[63, C_WO + 64:C_WO + 128] = EPS
    c[0:64, C_WG:C_WG + 64] = 1.0 / 64
    c[64:128, C_WG + 64:C_WG + 128] = 1.0 / 64
    c[:, C_COLS + 0] = (p < 16)
    c[:, C_COLS + 1] = np.where(p < 16, 0.0, NEG)
    c[:, C_COLS + 2] = -1.0 * (p < 16)
    c[:, C_COLS + 3] = (p == 63)
    c[:, C_COLS + 4] = 1.0
    c[:, C_COLS + 5] = 0.0
    return c


def build_program(NB):
    NG = NB // 4
    T = 16 + 128 * NB
    SEQ = 128 * NB
    NT = NB + 1
    NF = NT * 8

    nc = bass.Bass("TRN2", target_bir_lowering=False)

    def dram(name, shape, kind="ExternalInput"):
        return nc.dram_tensor(name, shape, F32, kind=kind).ap()

    x_d = dram("x", [SEQ, D])
    meta_d = dram("meta", [16, D])
    ng_d = dram("norm_g", [1, D])
    win_d = dram("w_in", [D, DIN])
    bfr_d = dram("bf_rep", [1, NF])
    cw_d = dram("cwT", [128, 12])
    ag_d = dram("ag", [128, 4])
    cg_d = dram("cg", [128, 4])
    wout_d = dram("w_out", [D, D])
    fg_d = dram("fg", [1, D])
    cst_d = dram("cst", [128, NCST])
    ones_d = dram("ones_bf", [1, T // 2])
    y_d = dram("y", [SEQ, D], kind="ExternalOutput")

    win_v = win_d.rearrange("(kc p) e -> p kc e", p=128)
    wout_v = wout_d.rearrange("(kc p) e -> p kc e", p=128)

    S = Sched()

    def OP(eng, name, reads, writes, *args, **kw):
        return S.op(eng, lambda e: getattr(e, name)(*args, **kw), reads=reads, writes=writes)

    def DMA(key, out, in_, reads, writes):
        return S.op("sp", lambda e: e.dma_start(out=out, in_=in_), reads=reads, writes=writes, dma_key=key)

    def MM(out, lhsT, rhs, start, stop, reads, writes):
        return S.op("pe", lambda e: e.matmul(out, lhsT=lhsT, rhs=rhs, start=start, stop=stop),
                    reads=reads, writes=writes)

    def ACT(out, in_, func, reads, writes, **kw):
        return S.op("act", lambda e: e.activation(out=out, in_=in_, func=func, **kw), reads=reads, writes=writes)

    with ExitStack() as es:
        def sb(name, shape, dt):
            return es.enter_context(nc.sbuf_tensor("sb_" + name, shape, dt))

        cst = sb("cst", [128, NCST], F32); cst_b = Buf()
        identb = sb("identb", [128, 128], BF16); identb_b = Buf()
        maskb = sb("maskb", [128, 128], BF16); maskb_b = Buf()
        uT = sb("uT", [128, KC, T], BF16)
        uT_t = [Buf() for _ in range(NB + 1)]

        def ub(p0, n):
            out = []
            if p0 < 16:
                out.append(uT_t[0])
            lo = max(p0, 16)
            hi = p0 + n
            if hi > lo:
                out += uT_t[1 + (lo - 16) // 128:1 + (hi - 16 + 127) // 128]
            return out
        mixA = sb("mixA", [128, 4, SEQ], BF16); mixA_b = [[Buf() for _ in range(NG)] for _ in range(4)]
        c_all = sb("c_all", [128, NF], F32); c_all_b = Buf()
        negc = sb("negc", [128, NF], F32); negc_b = Buf()
        cw_t = sb("cw_t", [128, 12], F32); ag_t = sb("ag_t", [128, 4], F32); cg_t = sb("cg_t", [128, 4], F32)
        small_b = Buf()
        halo = sb("halo", [128, 4, 2], F32); halo_b = [Buf() for _ in range(4)]
        stat = sb("stat", [128, 8], F32)
        stat_b = [Buf(), Buf()]
        UW = 26900
        U = sb("U", [128, UW], F32)
        ar = Arena(U[:, :], UW)

        def ps(name):
            return es.enter_context(nc.psum_tensor("ps_" + name, [128, 512], F32))
        A = [ps("A0"), ps("A1")]; A_b = [Buf(), Buf()]
        SB = [ps("S0"), ps("S1"), ps("S2")]; SB_b = [Buf(), Buf(), Buf()]
        OB = [ps("O0"), ps("O1")]; OB_b = [Buf(), Buf()]
        X = ps("X"); X_b = Buf()

        ident_f = cst[:, C_ID:C_ID + 128]
        col = lambda i: cst[:, C_COLS + i:C_COLS + i + 1]

        DMA("cst", cst[:], cst_d[:, :], [], [cst_b])
        DMA("small", cw_t[:], cw_d[:, :], [], [small_b])
        DMA("small", ag_t[:], ag_d[:, :], [], [small_b])
        DMA("small", cg_t[:], cg_d[:, :], [], [small_b])
        OP("dve", "tensor_copy", [cst_b], [identb_b], out=identb[:], in_=ident_f)
        OP("dve", "tensor_copy", [cst_b], [maskb_b], out=maskb[:], in_=cst[:, C_MASK:C_MASK + 128])

        ar.reset()
        wbf = [ar.alloc_top(KC * 512, BF16).rearrange("p (k e) -> p k e", e=512) for _ in range(2)]
        wbf_b = [[Buf() for _ in range(4)] for _ in range(2)]
        KEKO = ar.alloc_top(2 * T, BF16)
        KE, KO = KEKO[:, 0:T], KEKO[:, T:2 * T]; K_b = Buf()
        KEO = [KE, KO]
        vaug = ar.alloc_top(NT * 256, BF16).rearrange("p (j c) -> p j c", c=256); vaug_b = Buf()
        CP = [ar.alloc_top(4 * 128, BF16).rearrange("p (t m) -> p t m", m=128) for _ in range(2)]
        CP_b = [Buf(), Buf()]
        wst = [ar.alloc_top(KC * 128, F32).rearrange("p (k e) -> p k e", e=128) for _ in range(2)]
        wst_b = [Buf(), Buf()]
        wob = ar.alloc(KC * D, BF16).rearrange("p (k e) -> p k e", e=D)
        wob_b = [Buf() for _ in range(8)]
        wfb = ar.alloc(KC * 8, BF16).rearrange("p (k e) -> p k e", e=8); wfb_b = Buf()
        bfr = ar.alloc(NF, F32); bfr_b = Buf()
        fblk = ar.alloc(max(4 * NF, D), F32)
        fx, fa, fm, car = [fblk[:, k_ * NF:(k_ + 1) * NF] for k_ in range(4)]
        f_b = Buf()
        shared_off = ar.off

        zc, oc_ = col(5), col(4)
        vflat = vaug.rearrange("p j c -> p (j c)")
        init_ops = [
            lambda: ACT(KEKO.bitcast(F32), zc.to_broadcast([128, T]), AF.Copy, [cst_b], [K_b]),
            lambda: DMA("ones", KE[64:65, :].bitcast(F32), ones_d[:, :], [K_b], [K_b]),
            lambda: DMA("ones", KO[63:64, :].bitcast(F32), ones_d[:, :], [K_b], [K_b]),
            lambda: ACT(vflat.bitcast(F32), zc.to_broadcast([128, NT * 128]), AF.Copy, [cst_b], [vaug_b]),
            lambda: ACT(vaug[:, :, 64:65], oc_.to_broadcast([128, NT]).rearrange("p (j o) -> p j o", o=1), AF.Copy,
                        [cst_b], [vaug_b]),
            lambda: ACT(vaug[:, :, 191:192], oc_.to_broadcast([128, NT]).rearrange("p (j o) -> p j o", o=1), AF.Copy,
                        [cst_b], [vaug_b]),
            lambda: ACT(CP[0].rearrange("p t m -> p (t m)").bitcast(F32), zc.to_broadcast([128, 256]), AF.Copy,
                        [cst_b], [CP_b[0]]),
            lambda: ACT(CP[1].rearrange("p t m -> p (t m)").bitcast(F32), zc.to_broadcast([128, 256]), AF.Copy,
                        [cst_b], [CP_b[1]]),
        ]
        vaug4 = vaug.rearrange("p j (b d) -> p j b d", d=64)

        wst_cnt = [0]

        wtasks = []
        wpending = []

        def load_w(dst, dst_b, src, extra_w=(), defer=False):
            if defer:
                wtasks.append((dst, dst_b, src, extra_w))
                return
            s = wst_cnt[0] % 2
            wst_cnt[0] += 1
            DMA("wst%d" % s, wst[s][:, :, :], src, [], [wst_b[s]])
            OP("dve", "tensor_copy", [wst_b[s]], [dst_b] + list(extra_w), out=dst, in_=wst[s][:, :, :])

        def wtask_step(n=1):
            while wpending:
                s_, dst, dst_b, extra_w = wpending.pop(0)
                OP("dve", "tensor_copy", [wst_b[s_]], [dst_b] + list(extra_w), out=dst, in_=wst[s_][:, :, :])
            for _ in range(n):
                if not wtasks:
                    break
                dst, dst_b, src, extra_w = wtasks.pop(0)
                s_ = wst_cnt[0] % 2
                wst_cnt[0] += 1
                DMA("wst%d" % s_, wst[s_][:, :, :], src, [], [wst_b[s_]])
                wpending.append((s_, dst, dst_b, extra_w))

        def wtask_flush():
            while wtasks or wpending:
                wtask_step(2)

        def load_pair_w(c, defer=False):
            s = c % 2
            for i, c0 in enumerate((128 * c, 512 + 128 * c, 1024 + 128 * c, 1544 + 128 * c)):
                load_w(wbf[s][:, :, 128 * i:128 * i + 128], wbf_b[s][i], win_v[:, :, c0:c0 + 128], defer=defer)

        NXS = 4
        xt = [ar.alloc(D, F32) for _ in range(3)] + [fblk[:, 0:D]]; xt_b = [Buf() for _ in range(NXS)]
        xb = [ar.alloc(D, BF16) for _ in range(2)]; xb_b = [Buf(), Buf()]
        junk = ar.alloc(D, BF16); junk_b = Buf()
        grep = ar.alloc(D, F32); grep_b = Buf()
        DMA("grep", grep, ng_d.partition_broadcast(128), [], [grep_b])
        Xb16 = X[:].bitcast(BF16)

        def tile_stats(src, src_b, slot):
            st, st_b = stat[:, 4 * slot:4 * slot + 4], stat_b[slot]
            ACT(junk, src, AF.Square, [src_b], [junk_b, st_b], accum_out=st[:, 0:1])
            ACT(st[:, 1:2], st[:, 0:1], AF.Ln, [st_b], [st_b], scale=1.0 / D, bias=EPS)
            ACT(st[:, 2:3], st[:, 1:2], AF.Exp, [st_b], [st_b], scale=-0.5)
            return st[:, 2:3], st_b

        def kcols(j):
            return (0, 128) if j == 0 else (16 + 128 * (j - 1), 128)

        a_cnt = [0]

        def next_A():
            i = a_cnt[0] % 2
            a_cnt[0] += 1
            return A[i], A_b[i]

        def pair_w(c):
            ws = c % 2
            return [wbf[ws][:, :, 128 * i:128 * i + 128] for i in range(4)], wbf_b[ws]

        def kv_inproj_part(c, g):
            (wq, wk, wv, wz), (wq_b, wk_b, wv_b, wz_b) = pair_w(c)
            ranges = ([(0, 16)] if g == 0 else []) + [(16 + 512 * g, 512)]
            for (p0, n) in ranges:
                a, a_b = next_A()
                for kc in range(KC):
                    MM(a[:, 0:n], wk[:, kc, :], uT[:, kc, p0:p0 + n], kc == 0, kc == KC - 1, [wk_b] + ub(p0, n), [a_b])
                OP("dve", "tensor_copy", [a_b], [K_b], out=KE[0:64, p0:p0 + n], in_=a[0:64, 0:n])
                OP("dve", "tensor_copy", [a_b], [K_b], out=KO[64:128, p0:p0 + n], in_=a[64:128, 0:n])
            tiles = ([0] if g == 0 else []) + list(range(4 * g + 1, 4 * g + 5))
            for jb in range(0, len(tiles), 4):
                tl = tiles[jb:jb + 4]
                nt = len(tl)
                a, a_b = next_A()
                for t, j in enumerate(tl):
                    p0, n = kcols(j)
                    for kc in range(KC):
                        MM(a[:, 128 * t:128 * t + 128], uT[:, kc, p0:p0 + n], wv[:, kc, :], kc == 0, kc == KC - 1,
                           [wv_b] + ub(p0, n), [a_b])
                OP("dve", "tensor_copy", [a_b], [vaug_b], out=vaug4[:, tl[0]:tl[0] + nt, 0:4:3, :],
                   in_=a[:, 0:128 * nt].rearrange("p (t b d) -> p t b d", b=2, d=64))

        def kv_inproj(c):
            for g in range(NG):
                kv_inproj_part(c, g)

        TB = [(X, X_b), (SB[0], SB_b[0])]
        TB16 = [X[:].bitcast(BF16), SB[0][:].bitcast(BF16)]

        def a_load(ti):
            s3 = ti % NXS
            if ti == 0:
                OP("dve", "memset", [], [xt_b[s3]], xt[s3], 0.0)
                DMA("xt%d" % s3, xt[s3][0:16, :], meta_d[:, :], [], [xt_b[s3]])
            else:
                DMA("xt%d" % s3, xt[s3], x_d[(ti - 1) * 128:ti * 128, :], [], [xt_b[s3]])

        def a_norm(ti):
            s3, s2 = ti % NXS, ti % 2
            rs, rs_b = tile_stats(xt[s3], xt_b[s3], s2)
            OP("dve", "scalar_tensor_tensor", [xt_b[s3], rs_b, grep_b], [xb_b[s2]],
               out=xb[s2], in0=xt[s3], scalar=rs, in1=grep, op0=ALU.mult, op1=ALU.mult)

        for ti in range(min(NXS, NT)):
            a_load(ti)
        DMA("wf", wst[0][:, :, 0:8], win_v[:, :, 1536:1544], [], [wst_b[0]])
        DMA("bfr", bfr, bfr_d.partition_broadcast(128), [], [bfr_b])
        OP("dve", "tensor_copy", [wst_b[0]], [wfb_b], out=wfb, in_=wst[0][:, :, 0:8])
        a_norm(0)
        for ti in range(NT):
            s2 = ti % 2
            tb, tb_b = TB[s2]
            tb16 = TB16[s2]
            for kc in range(KC):
                S.op("pe", lambda e, kc=kc, s2=s2, tb16=tb16: e.transpose(out=tb16[:, kc * 128:(kc + 1) * 128],
                                                                           in_=xb[s2][:, kc * 128:(kc + 1) * 128],
                                                                           identity=identb[:]),
                     reads=[xb_b[s2], identb_b], writes=[tb_b])
            if ti + 1 < NT:
                a_norm(ti + 1)
            if ti >= 1 and init_ops:
                init_ops.pop(0)()
            if ti == 0:
                OP("dve", "tensor_copy", [tb_b], [uT_t[0]], out=uT[:, :, 0:16],
                   in_=tb16.rearrange("p (k t) -> p k t", t=128)[:, :, 0:16])
            else:
                p0 = 16 + 128 * (ti - 1)
                OP("dve", "tensor_copy", [tb_b], [uT_t[ti]], out=uT[:, :, p0:p0 + 128],
                   in_=tb16.rearrange("p (k t) -> p k t", t=128))
            if ti + NXS < NT:
                a_load(ti + NXS)
            if ti == 1:
                load_pair_w(0)
            if ti >= 4 and ti % 4 == 0:
                kv_inproj_part(0, ti // 4 - 1)

        while init_ops:
            init_ops.pop(0)()

        for j in range(NT):
            p0, n = kcols(j)
            for kc in range(KC):
                MM(X[:, 8 * j:8 * j + 8], uT[:, kc, p0:p0 + n], wfb[:, kc, :], kc == 0, kc == KC - 1,
                   ub(p0, n) + [wfb_b], [X_b])
        OP("dve", "tensor_tensor", [X_b, bfr_b], [f_b], out=fx, in0=X[:, 0:NF], in1=bfr, op=ALU.add)
        ACT(fa, fx, AF.Abs, [f_b], [f_b])
        ACT(fa, fa, AF.Exp, [f_b], [f_b], scale=-1.0)
        ACT(fa, fa, AF.Ln, [f_b], [f_b], bias=1.0)
        OP("dve", "tensor_scalar_min", [f_b], [f_b], out=fm, in0=fx, scalar1=0.0)
        OP("dve", "tensor_tensor", [f_b], [f_b], out=fm, in0=fm, in1=fa, op=ALU.subtract)
        OP("dve", "tensor_scalar", [f_b, cst_b], [f_b], out=fm[:, 0:8], in0=fm[:, 0:8], scalar1=col(0),
           scalar2=None, op0=ALU.mult)
        cw_sb, tot_sb = fx, fa
        MM(X[:, 0:NF], cst[:, C_TRI:C_TRI + 128], fm, True, True, [f_b, cst_b], [X_b])
        OP("dve", "tensor_copy", [X_b], [f_b], out=cw_sb, in_=X[:, 0:NF])
        MM(X[:, 0:NF], cst[:, C_SEL:C_SEL + 128], cw_sb, True, True, [f_b, cst_b], [X_b])
        OP("dve", "tensor_copy", [X_b], [f_b], out=tot_sb, in_=X[:, 0:NF])
        OP("dve", "memset", [], [f_b], car[:, 0:8], 0.0)
        for j in range(1, NT):
            OP("dve", "tensor_tensor", [f_b], [f_b], out=car[:, 8 * j:8 * j + 8], in0=car[:, 8 * j - 8:8 * j],
               in1=tot_sb[:, 8 * j - 8:8 * j], op=ALU.add)
        OP("dve", "tensor_tensor", [f_b], [c_all_b], out=c_all[:], in0=cw_sb, in1=car, op=ALU.add)
        OP("dve", "tensor_scalar", [c_all_b], [negc_b], out=negc[:, 8:NF], in0=c_all[:, 8:NF], scalar1=-1.0,
           scalar2=None, op0=ALU.mult)
        OP("dve", "tensor_scalar", [c_all_b, cst_b], [negc_b], out=negc[:, 0:8], in0=c_all[:, 0:8],
           scalar1=col(2), scalar2=col(1), op0=ALU.mult, op1=ALU.add)

        ar.off = shared_off
        QE = [ar.alloc(512, BF16) for _ in range(2)]; QO = [ar.alloc(512, BF16) for _ in range(2)]
        Q_b = [Buf(), Buf()]
        PT = [ar.alloc(512, BF16) for _ in range(4)]; PT_b = [Buf() for _ in range(4)]
        ez = ar.alloc(512, F32); ez_b = Buf()
        sz = [ar.alloc(512, F32) for _ in range(2)]; sz_b = [Buf(), Buf()]
        sqE = ar.alloc(512, F32); sqO = ar.alloc(512, F32); sq_b = Buf()
        rstd = ar.alloc(512, F32); rstd_b = Buf()
        lnr = rstd
        tmp, tmp_b = ez, ez_b

        kr = [(0, 16)] + [(16 + 512 * g, 512) for g in range(NG)]
        s_cnt = [0]
        q_cnt = [0]
        OcE = ar.alloc(512, F32); OcO = ar.alloc(512, F32); Oc_b = [Buf(), Buf()]
        Oc = [OcE, OcO]

        def prologue(c, G):
            (wq, wk, wv, wz), (wq_b, wk_b, wv_b, wz_b) = pair_w(c)
            q0 = 16 + 512 * G
            qs = q_cnt[0] % 2
            q_cnt[0] += 1
            a, a_b = next_A()
            for kc in range(KC):
                MM(a[:, :], wq[:, kc, :], uT[:, kc, q0:q0 + 512], kc == 0, kc == KC - 1, [wq_b] + ub(q0, 512), [a_b])
                yield qs
            OP("dve", "tensor_copy", [a_b], [Q_b[qs]], out=QE[qs][0:64, :], in_=a[0:64, :])
            OP("dve", "tensor_copy", [a_b], [Q_b[qs]], out=QO[qs][64:128, :], in_=a[64:128, :])
            OP("dve", "tensor_scalar", [c_all_b], [CP_b[qs]], out=CP[qs][:, :, 63:65],
               in0=c_all[:, 8 * (4 * G + 1):8 * (4 * G + 5)].rearrange("p (t h) -> p t h", h=8)[:, :, 2 * c:2 * c + 2],
               scalar1=8.0, scalar2=None, op0=ALU.mult)
            for t in range(4):
                MM(X[:, 128 * t:128 * t + 128], CP[qs][:, t, :], identb[:], True, True, [CP_b[qs], identb_b], [X_b])
            yield qs
            OP("dve", "tensor_copy", [X_b], [Q_b[qs]], out=QE[qs][64:128, :], in_=X[64:128, :])
            OP("dve", "tensor_copy", [X_b], [Q_b[qs]], out=QO[qs][0:64, :], in_=X[0:64, :])
            a, a_b = next_A()
            for kc in range(KC):
                MM(a[:, :], wz[:, kc, :], uT[:, kc, q0:q0 + 512], kc == 0, kc == KC - 1, [wz_b] + ub(q0, 512), [a_b])
                yield qs
            ACT(ez, a[:, :], AF.Exp, [a_b], [ez_b], scale=-1.0)
            OP("dve", "tensor_scalar_add", [ez_b], [ez_b], out=ez, in0=ez, scalar1=1.0)
            OP("dve", "reciprocal", [ez_b], [ez_b], out=ez, in_=ez)
            OP("dve", "tensor_tensor", [ez_b, a_b], [sz_b[qs]], out=sz[qs], in0=a[:, :], in1=ez, op=ALU.mult)
            yield qs

        def run_all(gen):
            qs = None
            for qs in gen:
                pass
            return qs

        def make_steps(c, G, qs):
            steps = []
            LAG = 2
            nblk = 4 * G + 5
            blocks = []
            for par in range(2):
                for j in range(nblk):
                    r = j - (4 * G + 1)
                    blocks.append((par, j, 128 * r if r > 0 else 0, r >= 0))
            slots = {}
            tot = len(blocks)
            for idx in range(tot + LAG):
                def step(idx=idx):
                    if idx < tot:
                        par, j, c0, diag = blocks[idx]
                        Kt = KEO[par]
                        Qt = (QE if par == 0 else QO)[qs]
                        hb = 2 * c + (1 - par)
                        si = s_cnt[0] % 3
                        pi = s_cnt[0] % 4
                        s_cnt[0] += 1
                        slots[idx] = pi
                        p0, n = kcols(j)
                        MM(SB[si][:, c0:512], Kt[:, p0:p0 + 128], Qt[:, c0:512], True, not diag,
                           [K_b, Q_b[qs]], [SB_b[si]])
                        if diag:
                            MM(SB[si][:, c0:c0 + 128], identb[:], maskb[:], False, True,
                               [identb_b, maskb_b], [SB_b[si]])
                        ACT(PT[pi][:, c0:512], SB[si][:, c0:512], AF.Exp, [SB_b[si], negc_b], [PT_b[pi]],
                            bias=negc[:, 8 * j + hb:8 * j + hb + 1], scale=0.125)
                    if idx >= LAG:
                        par, j, c0, diag = blocks[idx - LAG]
                        O, O_b = OB[par], OB_b[par]
                        pi = slots[idx - LAG]
                        MM(O[:, c0:512], vaug[:, j, 128 * par:128 * par + 128], PT[pi][:, c0:512],
                           j == 0, j == nblk - 1, [vaug_b, PT_b[pi]], [O_b])
                        if j == nblk - 1:
                            OP("dve", "tensor_copy", [O_b], [Oc_b[par]], out=Oc[par], in_=O[:, :])
                steps.append(step)
            return steps

        def post(c, G, qs):
            OP("dve", "tensor_tensor", [Oc_b[0]], [sq_b], out=sqE, in0=OcE, in1=OcE, op=ALU.mult)
            OP("dve", "tensor_tensor", [Oc_b[1]], [sq_b], out=sqO, in0=OcO, in1=OcO, op=ALU.mult)
            yield
            for h0 in (0, 256):
                MM(X[:, h0:h0 + 256], cst[:, C_WE:C_WE + 128], sqE[:, h0:h0 + 256], True, False, [cst_b, sq_b], [X_b])
                yield
                MM(X[:, h0:h0 + 256], cst[:, C_WO:C_WO + 128], sqO[:, h0:h0 + 256], False, True, [cst_b, sq_b], [X_b])
                yield
            ACT(lnr, X[:, :], AF.Ln, [X_b], [rstd_b])
            ACT(rstd, lnr, AF.Exp, [rstd_b], [rstd_b], scale=-0.5)
            OP("dve", "tensor_tensor", [Oc_b[0], rstd_b], [tmp_b], out=tmp[0:64, :], in0=OcE[0:64, :],
               in1=rstd[0:64, :], op=ALU.mult)
            OP("dve", "tensor_tensor", [Oc_b[1], rstd_b], [tmp_b], out=tmp[64:128, :], in0=OcO[64:128, :],
               in1=rstd[64:128, :], op=ALU.mult)
            OP("dve", "scalar_tensor_tensor", [tmp_b, sz_b[qs], small_b], [mixA_b[c][G]],
               out=mixA[:, c, 512 * G:512 * G + 512], in0=tmp, scalar=ag_t[:, c:c + 1], in1=sz[qs],
               op0=ALU.mult, op1=ALU.mult)

        seq = [(c, G) for c in range(4) for G in range(NG)]
        chunk_order = [4 * blk + i for i in range(4) for blk in (1, 2, 0, 3)]
        pre_chunks = chunk_order[0:8] if NG > 2 else []
        wcv_b = [Buf() for _ in range(16)]
        wcvc = {}
        for k_, n__ in enumerate(pre_chunks):
            flat = wbf[k_ // 4].rearrange("p k e -> p (k e)")
            wcvc[n__] = flat[:, 1024 * (k_ % 4):1024 * (k_ % 4) + 1024].rearrange("p (k e) -> p k e", e=128)
        pending_post = None
        qs_next = run_all(prologue(0, 0))
        for n_, (c, G) in enumerate(seq):
            qs = qs_next
            if G == 0:
                if c > 0:
                    kv_inproj(c)
                if c + 1 < 4:
                    load_pair_w(c + 1, defer=True)
                if c == 1:
                    for w_ in range(8):
                        load_w(wob[:, :, 128 * w_:128 * w_ + 128], wob_b[w_], wout_v[:, :, 128 * w_:128 * w_ + 128],
                               defer=True)
            if c == 3 and G == 0:
                for n__ in pre_chunks[0:4]:
                    load_w(wcvc[n__], wcv_b[n__], win_v[:, :, 2056 + 128 * n__:2056 + 128 * n__ + 128],
                           extra_w=wbf_b[0], defer=True)
            if c == 3 and G == NG - 1:
                for n__ in pre_chunks[4:8]:
                    load_w(wcvc[n__], wcv_b[n__], win_v[:, :, 2056 + 128 * n__:2056 + 128 * n__ + 128],
                           extra_w=wbf_b[1], defer=True)
            wtask_step(2)
            steps = make_steps(c, G, qs)
            hook_post = min(10, 4 * G + 1)
            k_guard = 4 * G + 6
            if G == NG - 1 and n_ + 1 < len(seq):
                wtask_flush()
            gen = prologue(*seq[n_ + 1]) if n_ + 1 < len(seq) else None
            post_gen = post(*pending_post) if pending_post is not None else None
            pending_post = None
            for k_, st in enumerate(steps):
                if k_ == k_guard and post_gen is not None:
                    run_all(post_gen)
                    post_gen = None
                st()
                if k_ % 16 == 12:
                    wtask_step(1)
                if post_gen is not None:
                    if k_ >= hook_post:
                        try:
                            next(post_gen)
                        except StopIteration:
                            post_gen = None
                elif gen is not None and k_ > hook_post:
                    try:
                        qs_next = next(gen)
                    except StopIteration:
                        gen = None
            if post_gen is not None:
                run_all(post_gen)
            if gen is not None:
                r_ = run_all(gen)
                if r_ is not None:
                    qs_next = r_
            pending_post = (c, G, qs)
        run_all(post(*pending_post))
        wtask_flush()
        wtask_flush()
        S.barrier()

        ar.reset()
        wob2 = ar.alloc(KC * D, BF16)
        wst = [ar.alloc(KC * 128, F32).rearrange("p (k e) -> p k e", e=128) for _ in range(3)]
        wst_b = [Buf() for _ in range(3)]
        ar.top = UW - 2 * (KC * 512 // 2)
        for n__ in chunk_order:
            if n__ not in wcvc:
                wcvc[n__] = ar.alloc(KC * 128, BF16).rearrange("p (k e) -> p k e", e=128)
        xt = [ar.alloc(D, F32) for _ in range(3)]; xt_b = [Buf() for _ in range(3)]
        junk = ar.alloc(D, BF16); junk_b = Buf()
        fgrep = ar.alloc(D, F32); fgrep_b = Buf()
        ymix = [ar.alloc(4 * 512, BF16).rearrange("p (i t) -> p i t", t=512) for _ in range(2)]
        ymix_b = [Buf(), Buf()]
        mc = ar.alloc(4 * 16, F32).rearrange("p (i t) -> p i t", t=16); mc_b = Buf()
        NSL = 2
        b1 = [ar.alloc(512, F32) for _ in range(NSL)]; b1_b = [Buf() for _ in range(NSL)]
        b2 = [ar.alloc(514, F32) for _ in range(NSL)]; b2_b = [Buf() for _ in range(NSL)]
        b3 = [ar.alloc(512, F32) for _ in range(NSL)]; b3_b = [Buf() for _ in range(NSL)]
        b4 = [ar.alloc(512, F32) for _ in range(NSL)]; b4_b = [Buf() for _ in range(NSL)]

        DMA("fgrep", fgrep, fg_d.partition_broadcast(128), [], [fgrep_b])
        cast_cnt = 0
        for n_ in chunk_order:
            if n_ in pre_chunks:
                continue
            sl = cast_cnt % 3
            DMA("wst%d" % sl, wst[sl][:, :, :], win_v[:, :, 2056 + 128 * n_:2056 + 128 * n_ + 128], [], [wst_b[sl]])
            if cast_cnt % 2 == 0:
                OP("dve", "tensor_copy", [wst_b[sl]], [wcv_b[n_]], out=wcvc[n_], in_=wst[sl][:, :, :])
            else:
                ACT(wcvc[n_], wst[sl][:, :, :], AF.Copy, [wst_b[sl]], [wcv_b[n_]])
            cast_cnt += 1

        A5 = [A[0], A[1], SB[0], SB[1], SB[2]]; A5_b = [A_b[0], A_b[1], SB_b[0], SB_b[1], SB_b[2]]

        def next_A5():
            i = a_cnt[0] % 5
            a_cnt[0] += 1
            return A5[i], A5_b[i]

        def conv_w(k, i):
            return cw_t[:, 4 * k + i:4 * k + i + 1]

        def meta_halo(i):
            a, a_b = next_A5()
            for kc in range(KC):
                MM(a[:, 0:16], wcvc[4 + i][:, kc, :], uT[:, kc, 0:16], kc == 0, kc == KC - 1,
                   [wcv_b[4 + i], uT_t[0]], [a_b])
            ACT(mc[:, i, :], a[:, 0:16], AF.Copy, [a_b], [mc_b])
            a2, a2_b = next_A5()
            for kc in range(KC):
                MM(a2[:, 0:16], wcvc[8 + i][:, kc, :], uT[:, kc, 0:16], kc == 0, kc == KC - 1,
                   [wcv_b[8 + i], uT_t[0]], [a2_b])
            OP("dve", "tensor_tensor", [a2_b, mc_b], [halo_b[i]], out=halo[:, i, :], in0=a2[:, 14:16],
               in1=mc[:, i, 14:16], op=ALU.mult)

        chain_cnt = [0]

        def phase1(G, i):
            q0 = 16 + 512 * G
            sl = chain_cnt[0] % NSL
            chain_cnt[0] += 1

            def inproj(blk):
                a, a_b = next_A5()
                ncol = 128 * (4 * blk + i)
                for kc in range(KC):
                    MM(a[:, :], wcvc[4 * blk + i][:, kc, :], uT[:, kc, q0:q0 + 512], kc == 0, kc == KC - 1,
                       [wcv_b[4 * blk + i]] + ub(q0, 512), [a_b])
                return a, a_b
            if G == 0:
                meta_halo(i)
            aC, aC_b = inproj(1)
            ACT(b1[sl], aC[:, :], AF.Copy, [aC_b], [b1_b[sl]])
            aX, aX_b = inproj(2)
            OP("pool", "tensor_copy", [halo_b[i]], [b2_b[sl]], out=b2[sl][:, 0:2], in_=halo[:, i, :])
            OP("dve", "tensor_tensor", [aX_b, b1_b[sl]], [b2_b[sl]], out=b2[sl][:, 2:514], in0=aX[:, :], in1=b1[sl],
               op=ALU.mult)
            OP("pool", "tensor_copy", [b2_b[sl]], [halo_b[i]], out=halo[:, i, :], in_=b2[sl][:, 512:514])
            ACT(b3[sl], b2[sl][:, 2:514], AF.Copy, [b2_b[sl], small_b], [b3_b[sl]], scale=conv_w(2, i))
            OP("dve", "scalar_tensor_tensor", [b2_b[sl], b3_b[sl], small_b], [b3_b[sl]], out=b3[sl], in0=b2[sl][:, 1:513],
               scalar=conv_w(1, i), in1=b3[sl], op0=ALU.mult, op1=ALU.add)
            OP("dve", "scalar_tensor_tensor", [b2_b[sl], b3_b[sl], small_b], [b3_b[sl]], out=b3[sl], in0=b2[sl][:, 0:512],
               scalar=conv_w(0, i), in1=b3[sl], op0=ALU.mult, op1=ALU.add)
            aZ, aZ_b = inproj(3)
            ACT(b4[sl], aZ[:, :], AF.Exp, [aZ_b], [b4_b[sl]], scale=-1.0)
            ACT(b4[sl], b4[sl], AF.Ln, [b4_b[sl]], [b4_b[sl]], bias=1.0)
            ACT(b4[sl], b4[sl], AF.Exp, [b4_b[sl]], [b4_b[sl]], scale=-1.0)
            OP("dve", "tensor_tensor", [aZ_b, b4_b[sl]], [b4_b[sl]], out=b4[sl], in0=aZ[:, :], in1=b4[sl], op=ALU.mult)
            aB, aB_b = inproj(0)
            OP("dve", "tensor_tensor", [aB_b, b3_b[sl]], [b3_b[sl]], out=b3[sl], in0=aB[:, :], in1=b3[sl], op=ALU.mult)
            ACT(b1[sl], b3[sl], AF.Square, [b3_b[sl]], [b1_b[sl]])
            return (G, i, sl)

        def phase2(state):
            G, i, sl = state
            ys = G % 2
            MM(X[:, :], cst[:, C_WG:C_WG + 128], b1[sl], True, True, [cst_b, b1_b[sl]], [X_b])
            ACT(b2[sl][:, 0:512], X[:, :], AF.Ln, [X_b], [b2_b[sl]], bias=EPS)
            ACT(b2[sl][:, 0:512], b2[sl][:, 0:512], AF.Exp, [b2_b[sl]], [b2_b[sl]], scale=-0.5)
            OP("dve", "tensor_tensor", [b3_b[sl], b2_b[sl]], [b3_b[sl]], out=b3[sl], in0=b3[sl], in1=b2[sl][:, 0:512],
               op=ALU.mult)
            OP("dve", "scalar_tensor_tensor", [b3_b[sl], b4_b[sl], small_b], [ymix_b[ys]], out=ymix[ys][:, i, :],
               in0=b3[sl], scalar=cg_t[:, i:i + 1], in1=b4[sl], op0=ALU.mult, op1=ALU.mult)

        tt_cnt = [0]

        xl_cnt = [0]
        xslot = {}

        def xload(G, tt):
            s3_ = xl_cnt[0] % 3
            xl_cnt[0] += 1
            xslot[(G, tt)] = s3_
            r0_ = 512 * G + 128 * tt
            DMA("xt%d" % s3_, xt[s3_], x_d[r0_:r0_ + 128, :], [], [xt_b[s3_]])

        def outproj_a(G, tt):
            r0 = 512 * G + 128 * tt
            ys = G % 2
            if (G, tt) not in xslot:
                xload(G, tt)
            s3 = xslot[(G, tt)]
            for half in range(2):
                pb, pb_b = OB[half], OB_b[half]
                for e_ in range(8):
                    if e_ < 4:
                        lh, lh_b = mixA[:, e_, r0:r0 + 128], mixA_b[e_][G]
                    else:
                        lh, lh_b = ymix[ys][:, e_ - 4, 128 * tt:128 * tt + 128], ymix_b[ys]
                    MM(pb[:, :], lh, wob[:, e_, 512 * half:512 * half + 512], e_ == 0, e_ == 7,
                       [lh_b] + wob_b, [pb_b])
                OP("dve", "tensor_tensor", [pb_b, xt_b[s3]], [xt_b[s3]], out=xt[s3][:, 512 * half:512 * half + 512],
                   in0=pb[:, :], in1=xt[s3][:, 512 * half:512 * half + 512], op=ALU.add)

        def outproj_b(G, tt, nxt=None):
            r0 = 512 * G + 128 * tt
            s3 = xslot[(G, tt)]
            st_slot = tt_cnt[0] % 2
            tt_cnt[0] += 1
            st, st_b = stat[:, 4 * st_slot:4 * st_slot + 4], stat_b[st_slot]
            ACT(junk, xt[s3], AF.Square, [xt_b[s3]], [junk_b, st_b], accum_out=st[:, 0:1])
            ACT(st[:, 1:2], st[:, 0:1], AF.Ln, [st_b], [st_b], scale=1.0 / D, bias=EPS)
            ACT(st[:, 2:3], st[:, 1:2], AF.Exp, [st_b], [st_b], scale=-0.5)
            OP("dve", "scalar_tensor_tensor", [xt_b[s3], st_b, fgrep_b], [xt_b[s3]], out=xt[s3], in0=xt[s3],
               scalar=st[:, 2:3], in1=fgrep, op0=ALU.mult, op1=ALU.mult)
            if nxt is not None and nxt not in xslot:
                xload(*nxt)
            o = DMA("y%d" % s3, y_d[r0:r0 + 128, :], xt[s3], [xt_b[s3]], [])
            o.final = True

        prev = None
        pend = []
        for G in range(NG):
            for i in range(4):
                st_ = phase1(G, i)
                if i == 0 and G > 0:
                    pend.extend((G - 1, tt) for tt in range(4))
                    t_ = pend.pop(0) if len(pend) > 4 else None
                else:
                    t_ = pend.pop(0) if pend else None
                nxt_ = pend[0] if pend else None
                if t_ is not None:
                    outproj_a(*t_)
                if prev is not None:
                    phase2(prev)
                prev = st_
                if t_ is not None:
                    outproj_b(*t_, nxt=nxt_)
        phase2(prev)
        pend.extend((NG - 1, tt) for tt in range(4))
        while pend:
            t_ = pend.pop(0)
            outproj_a(*t_)
            outproj_b(*t_, nxt=(pend[0] if pend else None))
        S.emit(nc, es)
    return nc


_CACHE = {}


def _host_inputs(NB, meta, norm_g, w_in, b_f, conv_w, attn_norm_g, conv_norm_g, w_out, final_norm_g):
    f32 = np.float32
    w = np.array(w_in[0], dtype=f32, copy=True)
    swap = np.array([1, 0, 3, 2, 5, 4, 7, 6])
    w[:, 1536:1544] = w[:, 1536:1544][:, swap]
    bf = np.asarray(b_f[0], f32)[swap]
    shared = {
        "meta": np.ascontiguousarray(meta, f32),
        "norm_g": np.ascontiguousarray(norm_g[0:1], f32),
        "w_in": np.ascontiguousarray(w),
        "bf_rep": np.ascontiguousarray(np.tile(bf, NB + 1)[None, :], f32),
        "cwT": np.ascontiguousarray(np.asarray(conv_w[0], f32).reshape(3, 4, 128).transpose(2, 0, 1).reshape(128, 12)),
        "ag": np.ascontiguousarray(np.asarray(attn_norm_g[0], f32).reshape(4, 128).T),
        "cg": np.ascontiguousarray(np.asarray(conv_norm_g[0], f32).reshape(4, 128).T),
        "w_out": np.ascontiguousarray(w_out[0], f32),
        "fg": np.ascontiguousarray(np.asarray(final_norm_g, f32)[None, :]),
        "cst": make_cst(),
        "ones_bf": np.full((1, (16 + 128 * NB) // 2), 0x3F803F80, dtype=np.uint32).view(np.float32),
    }
    return shared


def kernel(x, meta, norm_g, w_in, b_f, conv_w, attn_norm_g, conv_norm_g, w_out, final_norm_g):
    x = np.asarray(x, np.float32)
    B, SEQ, _ = x.shape
    NB = SEQ // 128
    if NB not in _CACHE:
        _CACHE[NB] = build_program(NB)
    nc = _CACHE[NB]
    shared = _host_inputs(NB, meta, norm_g, w_in, b_f, conv_w, attn_norm_g, conv_norm_g, w_out, final_norm_g)
    in_maps = [dict(shared, x=np.ascontiguousarray(x[b])) for b in range(B)]
    res = run_bass_kernel_spmd(nc, in_maps, core_ids=list(range(B)))
    return np.stack([np.asarray(res.results[b]["y"], np.float32) for b in range(B)], axis=0)
```
